# Optimizing a Trainium2 kernel written in Bass

```python
import jax, jax.numpy as jnp
from jax import lax
import numpy as np


D_MODEL = 1024
BATCH = 4
SEQ = 8192
DEPTH = 2

NORM_EPS = 1e-5
N_EVEN = (DEPTH + 1) // 2
N_ODD = DEPTH // 2

POOL_WINDOWS = (2, 4, 8, 16)
N_POOL_GROUPS = len(POOL_WINDOWS)
POOL_WIDTH = D_MODEL
POOL_GC = POOL_WIDTH // N_POOL_GROUPS
CONV_WIDTH = D_MODEL
CONV_K = 3
EVEN_WIDTH = POOL_WIDTH + CONV_WIDTH
EVEN_IN = POOL_WIDTH + 3 * CONV_WIDTH + EVEN_WIDTH

HEAD_DIM = 64
N_HEADS = D_MODEL // HEAD_DIM
N_KV_HEADS = 2
GROUP = N_HEADS // N_KV_HEADS
ATTN_WIDTH = N_HEADS * HEAD_DIM
KV_WIDTH = N_KV_HEADS * HEAD_DIM
ODD_IN = ATTN_WIDTH + 2 * KV_WIDTH + ATTN_WIDTH
WINDOW = 128
Q_BLOCK = 128
ROPE_THETA = 500000.0
ROT_DIMS = HEAD_DIM // 4

kernel_name = "hybrid_pool_shortconv_swa_sink_trunk"


def rms_norm(x, g):
    x32 = x.astype(jnp.float32)
    r = x32 * lax.rsqrt(jnp.mean(x32 * x32, axis=-1, keepdims=True) + NORM_EPS)
    return (r * g.astype(jnp.float32)).astype(x.dtype)


def shift_right(u, k):
    s = u.shape[1]
    return jnp.pad(u, ((0, 0), (k, 0), (0, 0)))[:, :s]


def causal_multiscale_pool(u):
    s = u.shape[1]
    u32 = u.astype(jnp.float32)
    cs = jnp.cumsum(u32, axis=1)
    t = jnp.arange(s)
    outs = []
    for g, w in enumerate(POOL_WINDOWS):
        cg = cs[:, :, g]
        window_sum = cg - shift_right(cg, w)
        count = jnp.minimum(t + 1, w).astype(jnp.float32)[None, :, None]
        outs.append(window_sum / count - u32[:, :, g])
    return jnp.stack(outs, axis=2).astype(u.dtype)


def even_mixer(y, w_in, w_pool, pool_scale, conv_w, w_out):
    b, s, _ = y.shape
    proj = y @ w_in
    u_a, gate_b, gate_c, h_c, z = jnp.split(
        proj, [POOL_WIDTH, POOL_WIDTH + CONV_WIDTH, POOL_WIDTH + 2 * CONV_WIDTH,
               POOL_WIDTH + 3 * CONV_WIDTH], axis=-1)
    pooled = causal_multiscale_pool(u_a.reshape(b, s, N_POOL_GROUPS, POOL_GC))
    a = jnp.einsum("bsgc,gcd->bsgd", pooled, w_pool).reshape(b, s, POOL_WIDTH) * pool_scale
    cu = gate_c * h_c
    v = conv_w[2] * cu + conv_w[1] * shift_right(cu, 1) + conv_w[0] * shift_right(cu, 2)
    bo = gate_b * v
    out = jnp.concatenate([a, bo], axis=-1) * jax.nn.silu(z)
    return out @ w_out


def rope_tables(positions):
    inv_freq = ROPE_THETA ** (-jnp.arange(0, ROT_DIMS, 2, dtype=jnp.float32) / ROT_DIMS)
    ang = positions.astype(jnp.float32)[..., None] * inv_freq
    return jnp.cos(ang)[:, :, None, :], jnp.sin(ang)[:, :, None, :]


def partial_rotary(x, cos, sin):
    half = ROT_DIMS // 2
    c = cos.astype(x.dtype)
    sn = sin.astype(x.dtype)
    x1 = x[..., :half]
    x2 = x[..., half:ROT_DIMS]
    return jnp.concatenate([x1 * c - x2 * sn, x2 * c + x1 * sn, x[..., ROT_DIMS:]], axis=-1)


def band_mask(nb):
    i = jnp.arange(Q_BLOCK)[:, None]
    j = jnp.arange(2 * Q_BLOCK)[None, :]
    diff = i + Q_BLOCK - j
    band = (diff >= 0) & (diff < WINDOW)
    blk = jnp.arange(nb)[:, None, None]
    return band[None] & ((blk > 0) | (j[None] >= Q_BLOCK))


def sliding_window_gqa_sinks(q, k, v, sinks):
    b, s, _, _ = q.shape
    nb = s // Q_BLOCK
    qb = q.reshape(b, nb, Q_BLOCK, N_KV_HEADS, GROUP, HEAD_DIM)

    def with_prev(t):
        tb = t.reshape(b, nb, Q_BLOCK, N_KV_HEADS, HEAD_DIM)
        prev = jnp.pad(tb, ((0, 0), (1, 0), (0, 0), (0, 0), (0, 0)))[:, :nb]
        return jnp.concatenate([prev, tb], axis=2)

    kb = with_prev(k)
    vb = with_prev(v)
    scale = HEAD_DIM ** -0.5
    scores = jnp.einsum("bnqkgd,bnskd->bnkgqs", qb, kb,
                        preferred_element_type=jnp.float32) * scale
    mask = band_mask(nb)[None, :, None, None]
    scores = jnp.where(mask, scores, -jnp.inf)
    sink = sinks.astype(jnp.float32).reshape(1, 1, N_KV_HEADS, GROUP, 1, 1)
    m = jnp.maximum(jnp.max(scores, axis=-1, keepdims=True), sink)
    p = jnp.exp(scores - m)
    denom = jnp.sum(p, axis=-1, keepdims=True) + jnp.exp(sink - m)
    p = (p / denom).astype(v.dtype)
    out = jnp.einsum("bnkgqs,bnskd->bnqkgd", p, vb)
    return out.reshape(b, s, ATTN_WIDTH)


def odd_mixer(y, cos, sin, w_in, b_in, sinks, w_out, b_out):
    b, s, _ = y.shape
    proj = y @ w_in + b_in
    q, k, v, z = jnp.split(
        proj, [ATTN_WIDTH, ATTN_WIDTH + KV_WIDTH, ATTN_WIDTH + 2 * KV_WIDTH], axis=-1)
    q = partial_rotary(q.reshape(b, s, N_HEADS, HEAD_DIM), cos, sin)
    k = partial_rotary(k.reshape(b, s, N_KV_HEADS, HEAD_DIM), cos, sin)
    v = v.reshape(b, s, N_KV_HEADS, HEAD_DIM)
    attn = sliding_window_gqa_sinks(q, k, v, sinks)
    return (attn * jax.nn.silu(z)) @ w_out + b_out


def setup_inputs(seed: int = 0) -> dict:
    key = jax.random.key(seed)
    ks = jax.random.split(key, 16)
    nrm = jax.random.normal
    f32 = jnp.float32
    x = nrm(ks[0], (BATCH, SEQ, D_MODEL), f32)
    positions = jnp.broadcast_to(jnp.arange(SEQ, dtype=jnp.int32), (BATCH, SEQ))
    norm_g = 1.0 + 0.02 * nrm(ks[1], (DEPTH, D_MODEL), f32)
    w_in_even = nrm(ks[2], (N_EVEN, D_MODEL, EVEN_IN), f32) * D_MODEL ** -0.5
    w_pool = nrm(ks[3], (N_EVEN, N_POOL_GROUPS, POOL_GC, POOL_GC), f32) * POOL_GC ** -0.5
    pool_scale = 1.0 + 0.02 * nrm(ks[4], (N_EVEN, POOL_WIDTH), f32)
    conv_w = nrm(ks[5], (N_EVEN, CONV_K, CONV_WIDTH), f32) * CONV_K ** -0.5
    w_out_even = nrm(ks[6], (N_EVEN, EVEN_WIDTH, D_MODEL), f32) * EVEN_WIDTH ** -0.5
    w_in_odd = nrm(ks[7], (N_ODD, D_MODEL, ODD_IN), f32) * D_MODEL ** -0.5
    b_in_odd = 0.02 * nrm(ks[8], (N_ODD, ODD_IN), f32)
    attn_sinks = nrm(ks[9], (N_ODD, N_HEADS), f32)
    w_out_odd = nrm(ks[10], (N_ODD, ATTN_WIDTH, D_MODEL), f32) * ATTN_WIDTH ** -0.5
    b_out_odd = 0.02 * nrm(ks[11], (N_ODD, D_MODEL), f32)
    final_norm_g = 1.0 + 0.02 * nrm(ks[12], (D_MODEL,), f32)
    return {"x": x, "positions": positions, "norm_g": norm_g,
            "w_in_even": w_in_even, "w_pool": w_pool, "pool_scale": pool_scale,
            "conv_w": conv_w, "w_out_even": w_out_even,
            "w_in_odd": w_in_odd, "b_in_odd": b_in_odd, "attn_sinks": attn_sinks,
            "w_out_odd": w_out_odd, "b_out_odd": b_out_odd,
            "final_norm_g": final_norm_g}


def reference(x, positions, norm_g, w_in_even, w_pool, pool_scale, conv_w, w_out_even,
              w_in_odd, b_in_odd, attn_sinks, w_out_odd, b_out_odd, final_norm_g):
    cos, sin = rope_tables(positions)
    h = x
    for layer in range(DEPTH):
        y = rms_norm(h, norm_g[layer])
        i = layer // 2
        if layer % 2 == 0:
            h = h + even_mixer(y, w_in_even[i], w_pool[i], pool_scale[i], conv_w[i],
                               w_out_even[i])
        else:
            h = h + odd_mixer(y, cos, sin, w_in_odd[i], b_in_odd[i], attn_sinks[i],
                              w_out_odd[i], b_out_odd[i])
    return rms_norm(h, final_norm_g)
```

```python
from contextlib import ExitStack
import numpy as np
import concourse.bass as bass
import concourse.mybir as mybir
from concourse.bass_utils import run_bass_kernel_spmd

F32 = mybir.dt.float32
BF16 = mybir.dt.bfloat16
I32 = mybir.dt.int32
AF = mybir.ActivationFunctionType
ALU = mybir.AluOpType
AX = mybir.AxisListType


class _I:
    __slots__ = ("eng", "fn", "kind", "deps", "signal", "sig", "idx", "rawdeps", "line", "rw")


class Prog:
    NDMA = 28
    SAME_ENGINE_RAW = True

    def __init__(self, nc):
        self.nc = nc
        self.stack = ExitStack()
        self.instrs = []
        self.lw = {}
        self.rd = {}
        self.trace = None

    def sbuf(self, name, shape, dtype):
        return self.stack.enter_context(self.nc.sbuf_tensor(name, shape, dtype))

    def psum(self, name, shape, dtype):
        return self.stack.enter_context(self.nc.psum_tensor(name, shape, dtype))

    def _add(self, eng, fn, reads, writes, kind):
        ins = _I()
        ins.eng, ins.fn, ins.kind = eng, fn, kind
        ins.idx = len(self.instrs)
        import sys as _sys
        f = _sys._getframe(2)
        ln = []
        while f is not None and len(ln) < 4:
            ln.append(str(f.f_lineno))
            f = f.f_back
        ins.line = "<".join(ln)
        ins.rw = (reads, writes)
        ins.signal = False
        ins.sig = None
        deps = {}
        for k in reads:
            w = self.lw.get(k)
            if w is not None:
                deps[w.idx] = (w, True)
        for k in writes:
            w = self.lw.get(k)
            if w is not None and w.idx not in deps:
                deps[w.idx] = (w, False)
            for r in self.rd.get(k, ()):
                if r.idx not in deps:
                    deps[r.idx] = (r, False)
        out = []
        for d, raw in deps.values():
            if d.kind == "op" and d.eng == eng and kind == "op":
                if eng == "tensor" or not self.SAME_ENGINE_RAW:
                    continue
            out.append(d)
        ins.deps = out
        for k in reads:
            lst = self.rd.setdefault(k, [])
            if kind == "op":
                lst[:] = [r for r in lst if not (r.kind == "op" and r.eng == eng)]
            lst.append(ins)
        for k in writes:
            self.lw[k] = ins
            self.rd[k] = []
        self.instrs.append(ins)
        return ins

    def op(self, eng, fn, reads=(), writes=()):
        return self._add(eng, fn, tuple(reads), tuple(writes), "op")

    def dma(self, eng, out, in_, reads=(), writes=(), is_output=False):
        return self._add(eng, lambda e: e.dma_start(out=out, in_=in_), tuple(reads), tuple(writes), "dma")

    def emit(self):
        nc = self.nc
        st = self.stack
        engs = ["tensor", "vector", "scalar", "pool", "sp"]
        esem = {e: st.enter_context(nc.semaphore("sem_" + e)) for e in engs}
        dsem = [st.enter_context(nc.semaphore("dsem%d" % i)) for i in range(self.NDMA)]
        for ins in self.instrs:
            for d in ins.deps:
                d.signal = True
        cnt = {e: 0 for e in engs}
        dcnt = [0] * self.NDMA
        dlast = [None] * self.NDMA
        pools = {"sp": list(range(0, self.NDMA - 12)), "pool": list(range(self.NDMA - 12, self.NDMA - 8)),
                 "scalar": list(range(self.NDMA - 8, self.NDMA))}
        nd = {"sp": 0, "pool": 0, "scalar": 0}
        for ins in self.instrs:
            if ins.kind == "dma":
                pl = pools[ins.eng]
                s = pl[nd[ins.eng] % len(pl)]
                nd[ins.eng] += 1
                if dlast[s] is not None:
                    ins.deps.append(dlast[s])
                dcnt[s] += 16
                ins.sig = (dsem[s], dcnt[s])
                dlast[s] = ins
                ins.signal = True
            elif ins.signal:
                cnt[ins.eng] += 1
                ins.sig = (esem[ins.eng], cnt[ins.eng])
        per = {e: [i for i in self.instrs if i.eng == e] for e in engs}
        self.stats = {e: len(per[e]) for e in engs}
        self.stats["signals"] = dict(cnt)
        block = st.enter_context(nc.Block())

        def run(e, lst, final):
            waited = {}
            for ins in lst:
                if self.trace is not None:
                    self.trace.append((ins.eng, ins.idx, ins.kind, ins.line, [(d.eng, d.idx, d.sig[1]) for d in ins.deps], ins.sig[1] if ins.sig else None, ins.rw))
                need = {}
                for d in ins.deps:
                    sem, val = d.sig
                    key = id(sem)
                    if waited.get(key, 0) >= val:
                        continue
                    if key not in need or need[key][1] < val:
                        need[key] = (sem, val)
                for key, (sem, val) in need.items():
                    e.wait_ge(sem, val)
                    waited[key] = val
                r = ins.fn(e)
                if ins.signal:
                    sem, val = ins.sig
                    r.then_inc(sem, 16 if ins.kind == "dma" else 1)
            if final:
                for s in range(self.NDMA):
                    if dcnt[s] and waited.get(id(dsem[s]), 0) < dcnt[s]:
                        e.wait_ge(dsem[s], dcnt[s])

        @block.tensor
        def _(e):
            run(e, per["tensor"], False)

        @block.vector
        def _(e):
            run(e, per["vector"], False)

        @block.scalar
        def _(e):
            run(e, per["scalar"], True)

        @block.gpsimd
        def _(e):
            run(e, per["pool"], True)

        @block.sync
        def _(e):
            run(e, per["sp"], True)

        st.close()


NCORES = 8
DM = 1024
TOK = 4096
HALO = 256
TL = TOK + HALO
EPS = 1e-5
NEG = -30000.0
TWO_PI = 6.283185307179586
C1 = 6.28125
C2 = TWO_PI - C1
PI_SAFE = 3.1415925
TILES = [(0, 256, True)] + [(HALO + 1024 * i, 1024, False) for i in range(4)]

C_G = 0
C_PS = 16
C_CW = 24
C_B = 48
C_INVF = 65
C_SGN = 66
C_MHALF = 67
NCST = 68
R_BOUT = 0
R_FG = 1024
R_BV = 2048
R_SINK = 2176
NROW = 2192


def build(mode="fused", ntiles=5, dbg=99):
    nc = bass.Bass("TRN2", target_bir_lowering=False)
    P = Prog(nc)

    def din(name, shape, dt=F32):
        return nc.dram_tensor(name, shape, dt, kind="ExternalInput").ap()

    x_ext = din("x_ext", [TL, DM])
    pos_bc = din("pos_bc", [128, TL], I32)
    wie = din("wie", [48, 128, 1024])
    wpl = din("wpl", [2, 128, 1024])
    woe = din("woe", [16, 128, 1024])
    wio = din("wio", [18, 128, 1024])
    woo = din("woo", [8, 128, 1024])
    cst_d = din("cst", [128, NCST])
    rowc_d = din("rowc", [128, NROW])
    pcorr_d = din("pcorr", [128, 64])
    masks_d = din("masks", [2, 128, 1024])
    pmat_d = din("pmat", [128, 1024])
    out_d = nc.dram_tensor("out", [TOK, DM], F32, kind="ExternalOutput").ap()

    NSTG = 3
    h = P.sbuf("h", [128, 8, 1024], F32)
    ybuf = P.sbuf("ybuf", [128, 2, 1024], BF16)
    yT = P.sbuf("yT", [128, 8, 1024], BF16)
    big = P.sbuf("big", [128, 16, 1024], BF16)
    stg = P.sbuf("stg", [128, NSTG, 1024], F32)
    wg = P.sbuf("wg", [128, 2, 4, 1024], BF16)
    wout = P.sbuf("wout", [128, 16, 1024], BF16)
    wpool = P.sbuf("wpool", [128, 2, 1024], BF16)
    cst = P.sbuf("cstt", [128, NCST], F32)
    rowc = P.sbuf("rowct", [128, NROW], F32)
    pcorr = P.sbuf("pcorrt", [128, 64], F32)
    maskb = P.sbuf("maskb", [128, 3, 512], BF16)
    identf = P.sbuf("identf", [128, 128], F32)
    identb = P.sbuf("identb", [128, 128], BF16)
    esink = P.sbuf("esink", [128, 16], F32)
    ucar = P.sbuf("ucar", [128, 8, 16], F32)
    ccar = P.sbuf("ccar", [128, 8, 2], F32)
    NS = 8
    S = P.sbuf("S", [128, NS, 528], F32)
    pooled = P.sbuf("pooled", [128, 2, 512], BF16)
    xbq = P.sbuf("xbq", [128, 2, 512], BF16)
    kT = P.sbuf("kT", [128, 128 + 1024], BF16)
    vaug = P.sbuf("vaug", [128, 9, 2, 65], BF16)
    cosT = P.sbuf("cosT", [128, 1024], F32)
    sinT = P.sbuf("sinT", [128, 1024], F32)
    NPT = 8
    PT = P.sbuf("PT", [128, NPT, 512], BF16)
    attn = P.sbuf("attn", [128, 2, 1024], BF16)
    ss = P.sbuf("ss", [128, 32], F32)
    rstd = P.sbuf("rstd", [128, 32], F32)
    small = P.sbuf("small", [128, 16], F32)
    ps = [P.psum("ps%d" % i, [128, 512], F32) for i in range(8)]
    junk = xbq[:].rearrange("p a b -> p (a b)")
    JK = [("xbq", 0), ("xbq", 1)]

    state = {"bank": 0, "stg": 0, "S": 0, "pt": 0, "ssi": 0, "xs": 0}

    def nbank():
        b = state["bank"] % 8
        state["bank"] += 1
        return b

    def nS():
        s = state["S"] % NS
        state["S"] += 1
        return s

    def col(i):
        return cst[:, i:i + 1]

    def mm(out, lhsT, rhs, start, stop, reads, writes):
        P.op("tensor", lambda e: e.matmul(out=out, lhsT=lhsT, rhs=rhs, start=start, stop=stop),
             reads, writes)

    def tr(out, in_, reads, writes):
        P.op("tensor", lambda e: e.transpose(out=out, in_=in_, identity=identb[:]), list(reads) + ["ident"], writes)

    def act(out, in_, func, reads, writes, scale=None, bias=None, accum=None):
        kw = {}
        if scale is not None:
            kw["scale"] = scale
        if bias is not None:
            kw["bias"] = bias
        if accum is not None:
            kw["accum_out"] = accum
        P.op("scalar", lambda e: e.activation(out=out, in_=in_, func=func, **kw), reads, writes)

    def tt(eng, out, in0, in1, op, reads, writes):
        P.op(eng, lambda e: e.tensor_tensor(out=out, in0=in0, in1=in1, op=op), reads, writes)

    def stt(out, in0, scalar, in1, op0, op1, reads, writes):
        P.op("vector", lambda e: e.scalar_tensor_tensor(out=out, in0=in0, scalar=scalar, in1=in1, op0=op0, op1=op1),
             reads, writes)

    def ts(eng, out, in0, s1, op0, reads, writes, s2=None, op1=None):
        if op1 is None:
            P.op(eng, lambda e: e.tensor_scalar(out=out, in0=in0, scalar1=s1, scalar2=None, op0=op0), reads, writes)
        else:
            P.op(eng, lambda e: e.tensor_scalar(out=out, in0=in0, scalar1=s1, scalar2=s2, op0=op0, op1=op1), reads, writes)

    def cp(eng, out, in_, reads, writes):
        if eng == "scalar":
            P.op(eng, lambda e: e.copy(out=out, in_=in_), reads, writes)
        else:
            P.op(eng, lambda e: e.tensor_copy(out=out, in_=in_), reads, writes)

    cast_rr = {"i": 0}
    CAST_ENGS = ["pool", "pool", "scalar"]

    def stage_cast(dram_ap, dst_ap, dst_keys, ncols=1024, eng=None):
        s = state["stg"] % NSTG
        state["stg"] += 1
        P.dma("sp", stg[:, s, 0:ncols], dram_ap, reads=[], writes=[("stg", s)])
        if eng is None:
            eng = CAST_ENGS[cast_rr["i"] % len(CAST_ENGS)]
            cast_rr["i"] += 1
        cp(eng, dst_ap, stg[:, s, 0:ncols], [("stg", s)], dst_keys)

    P.dma("sp", cst[:], cst_d, writes=["cst"])
    P.dma("sp", rowc[:], rowc_d, writes=["rowc"])
    P.dma("sp", pcorr[:], pcorr_d, writes=["pcorr"])
    P.op("pool", lambda e: e.memset(identf[:], 0.0), writes=["identf"])
    P.op("pool", lambda e: e.affine_select(out=identf[:], in_=identf[:], pattern=[[-1, 128]],
                                           compare_op=ALU.not_equal, fill=1.0, base=0, channel_multiplier=1),
         reads=["identf"], writes=["identf"])
    cp("vector", identb[:], identf[:], ["identf"], ["ident"])
    P.op("pool", lambda e: e.memset(ucar[:], 0.0), writes=["ucar%d" % i for i in range(8)])
    P.op("pool", lambda e: e.memset(ccar[:], 0.0), writes=["ccar%d" % i for i in range(8)])
    P.op("vector", lambda e: e.memset(vaug[:, :, :, 64:65], 1.0), writes=[("vaug", i) for i in range(9)])
    for i in range(2):
        stage_cast(wpl[i], wpool[:, i, :], [("wpool", i)], eng="vector")
    stage_cast(masks_d[0], maskb[:, 0:2, :].rearrange("p a b -> p (a b)"), ["mask"], eng="vector")
    stage_cast(masks_d[1][:, 0:512], maskb[:, 2, :], ["mask2"], ncols=512, eng="vector")
    P.op("vector", lambda e: e.memset(S[:], 0.0), writes=[("S", i) for i in range(NS)])
    act(esink[:], rowc[:, R_SINK:R_SINK + 16], AF.Exp, ["rowc"], ["esink"])

    def load_x(t0, nst):
        for st in range(nst):
            P.dma("sp", h[:, st, :], x_ext[t0 + st * 128: t0 + (st + 1) * 128, :], writes=[("h", st)])

    def norm_T(layer, sts):
        base = state["ssi"]
        state["ssi"] = (state["ssi"] + 8) % 24
        for st in sts:
            act(ybuf[:, st % 2, :], h[:, st, :], AF.Square, [("h", st)], [("ybuf", st % 2), ("ss", base + st)],
                accum=ss[:, base + st: base + st + 1])
        lo, hi = base + sts[0], base + sts[-1] + 1
        keys_ss = [("ss", base + st) for st in sts]
        keys_r = [("rstd", base + st) for st in sts]
        ts("pool", rstd[:, lo:hi], ss[:, lo:hi], 1.0 / DM, ALU.mult, keys_ss, keys_r, s2=EPS, op1=ALU.add)
        tt("pool", rstd[:, lo:hi], rstd[:, lo:hi], cst[:, C_MHALF:C_MHALF + 1].to_broadcast([128, hi - lo]),
           ALU.pow, keys_r + ["cst"], keys_r)
        for st in sts:
            sl = st % 2
            act(ybuf[:, sl, :], h[:, st, :], AF.Copy, [("h", st), ("rstd", base + st)], [("ybuf", sl)],
                scale=rstd[:, base + st: base + st + 1])
            b = nbank()
            pb = ps[b][:].bitcast(BF16)
            for kc in range(8):
                tr(pb[:, kc * 128:(kc + 1) * 128], ybuf[:, sl, kc * 128:(kc + 1) * 128], [("ybuf", sl)], [("ps", b)])
            tt("vector", yT[:, :, st * 128:(st + 1) * 128], pb.rearrange("p (k t) -> p k t", k=8),
               cst[:, C_G + layer * 8: C_G + layer * 8 + 8].unsqueeze(2).to_broadcast([128, 8, 128]), ALU.mult,
               [("ps", b), "cst"], [("yT", st)])

    def load_group(src, chunk_ids, gs):
        for ci, ch in enumerate(chunk_ids):
            stage_cast(src[ch], wg[:, gs, ci, :], [("wg", gs, ci)])

    def proj(gs, ci, t0, n, sts):
        b = nbank()
        w = wg[:, gs, ci, :].rearrange("p (k n) -> p k n", k=8)
        for kc in range(8):
            mm(ps[b][:, 0:n], w[:, kc, :], yT[:, kc, t0:t0 + n], kc == 0, kc == 7,
               [("wg", gs, ci)] + [("yT", st) for st in sts], [("ps", b)])
        return b

    def l0_group_a(g, gs, slices, first_real):
        w = 2 << g
        for sl, (t0, n, sts) in enumerate(slices):
            bu = [proj(gs, 0, t0, n, sts), proj(gs, 1, t0, n, sts)]
            bz = [proj(gs, 2, t0, n, sts), proj(gs, 3, t0, n, sts)]
            zs = []
            for i in range(2):
                c = 2 * g + i
                ub, ta, tb = nS(), nS(), nS()
                kc_ = "ucar%d" % c
                cp("pool", S[:, ub, 0:16], ucar[:, c, :], [kc_], [("S", ub)])
                cp("scalar", S[:, ub, 16:16 + n], ps[bu[i]][:, 0:n], [("ps", bu[i])], [("S", ub)])
                cp("pool", ucar[:, c, :], S[:, ub, n:n + 16], [("S", ub)], [kc_])
                src = ub
                lvl = 1
                dst = ta
                while (1 << lvl) <= w:
                    sh = 1 << (lvl - 1)
                    lo = 16 - (16 - (1 << lvl)) if (1 << lvl) < 16 else 16
                    lo = (1 << lvl)
                    tt("vector", S[:, dst, lo:16 + n], S[:, src, lo:16 + n], S[:, src, lo - sh:16 + n - sh], ALU.add,
                       [("S", src)], [("S", dst)])
                    src = dst
                    dst = tb if dst == ta else ta
                    lvl += 1
                if first_real and sl == 0:
                    tt("vector", S[:, src, 16:32], S[:, src, 16:32], pcorr[:, g * 16:(g + 1) * 16], ALU.mult,
                       [("S", src), "pcorr"], [("S", src)])
                stt(pooled[:, i, 0:n], S[:, src, 16:16 + n], 1.0 / w, S[:, ub, 16:16 + n], ALU.mult, ALU.subtract,
                    [("S", src), ("S", ub)], [("pooled", i)])
                z = nS()
                act(S[:, z, 0:n], ps[bz[i]][:, 0:n], AF.Silu, [("ps", bz[i])], [("S", z)])
                zs.append(z)
            for oc in range(2):
                c = 2 * g + oc
                b = nbank()
                for kc in range(2):
                    o = (g % 2) * 512 + kc * 256 + oc * 128
                    mm(ps[b][:, 0:n], wpool[:, g // 2, o:o + 128], pooled[:, kc, 0:n], kc == 0, kc == 1,
                       [("wpool", g // 2), ("pooled", 0), ("pooled", 1)], [("ps", b)])
                stt(big[:, c, t0:t0 + n], ps[b][:, 0:n], col(C_PS + c), S[:, zs[oc], 0:n], ALU.mult, ALU.mult,
                    [("ps", b), "cst", ("S", zs[oc])], [("big", c, t0 // 512)])

    def l0_group_b(j, gs, slices):
        for sl, (t0, n, sts) in enumerate(slices):
            bgc = proj(gs, 0, t0, n, sts)
            bhc = proj(gs, 1, t0, n, sts)
            bgb = proj(gs, 2, t0, n, sts)
            bzb = proj(gs, 3, t0, n, sts)
            hc, cu, v0, v1, zz, gz = nS(), nS(), nS(), nS(), nS(), nS()
            kc_ = "ccar%d" % j
            cp("scalar", S[:, hc, 0:n], ps[bhc][:, 0:n], [("ps", bhc)], [("S", hc)])
            cp("pool", S[:, cu, 0:2], ccar[:, j, :], [kc_], [("S", cu)])
            tt("vector", S[:, cu, 2:2 + n], ps[bgc][:, 0:n], S[:, hc, 0:n], ALU.mult, [("ps", bgc), ("S", hc)], [("S", cu)])
            cp("pool", ccar[:, j, :], S[:, cu, n:n + 2], [("S", cu)], [kc_])
            act(S[:, v0, 0:n], S[:, cu, 2:2 + n], AF.Copy, [("S", cu), "cst"], [("S", v0)], scale=col(C_CW + 16 + j))
            stt(S[:, v1, 0:n], S[:, cu, 1:1 + n], col(C_CW + 8 + j), S[:, v0, 0:n], ALU.mult, ALU.add,
                [("S", cu), ("S", v0), "cst"], [("S", v1)])
            stt(S[:, v0, 0:n], S[:, cu, 0:n], col(C_CW + j), S[:, v1, 0:n], ALU.mult, ALU.add,
                [("S", cu), ("S", v1), "cst"], [("S", v0)])
            act(S[:, zz, 0:n], ps[bzb][:, 0:n], AF.Silu, [("ps", bzb)], [("S", zz)])
            tt("vector", S[:, gz, 0:n], ps[bgb][:, 0:n], S[:, zz, 0:n], ALU.mult, [("ps", bgb), ("S", zz)], [("S", gz)])
            tt("pool", big[:, 8 + j, t0:t0 + n], S[:, gz, 0:n], S[:, v0, 0:n], ALU.mult, [("S", gz), ("S", v0)],
               [("big", 8 + j, t0 // 512)])

    def out_proj(st, nkc, coff):
        for nh in range(2):
            b = nbank()
            for kc in range(nkc):
                mm(ps[b][:, 0:512], big[:, coff + kc, st * 128:(st + 1) * 128], wout[:, kc, nh * 512:(nh + 1) * 512],
                   kc == 0, kc == nkc - 1, [("big", coff + kc, st // 4), ("wout", kc)], [("ps", b)])
            tt("vector", h[:, st, nh * 512:(nh + 1) * 512], ps[b][:, 0:512], h[:, st, nh * 512:(nh + 1) * 512], ALU.add,
               [("ps", b), ("h", st)], [("h", st)])

    def rope_tables(t0, n):
        a, k_, r, m = nS(), nS(), nS(), nS()
        posi = S[:, a, :].bitcast(I32)
        for hh in range(0, n, 512):
            nn = min(512, n - hh)
            ki = S[:, k_, 0:nn].bitcast(I32)
            P.dma("sp", posi[:, 0:nn], pos_bc[:, t0 + hh:t0 + hh + nn], writes=[("S", a)])
            ts("vector", S[:, r, 0:nn], posi[:, 0:nn], col(C_INVF), ALU.mult, [("S", a), "cst"], [("S", r)])
            ts("vector", ki, S[:, r, 0:nn], 1.0 / TWO_PI, ALU.mult, [("S", r)], [("S", k_)])
            stt(S[:, m, 0:nn], ki, -C1, S[:, r, 0:nn], ALU.mult, ALU.add, [("S", k_), ("S", r)], [("S", m)])
            stt(S[:, r, 0:nn], ki, -C2, S[:, m, 0:nn], ALU.mult, ALU.add, [("S", k_), ("S", m)], [("S", r)])

            def wrap(buf, tmp):
                ts("vector", S[:, tmp, 0:nn], S[:, buf, 0:nn], PI_SAFE, ALU.is_gt, [("S", buf)], [("S", tmp)], s2=-TWO_PI, op1=ALU.mult)
                tt("vector", S[:, buf, 0:nn], S[:, buf, 0:nn], S[:, tmp, 0:nn], ALU.add, [("S", buf), ("S", tmp)], [("S", buf)])
                ts("vector", S[:, tmp, 0:nn], S[:, buf, 0:nn], -PI_SAFE, ALU.is_lt, [("S", buf)], [("S", tmp)], s2=TWO_PI, op1=ALU.mult)
                tt("vector", S[:, buf, 0:nn], S[:, buf, 0:nn], S[:, tmp, 0:nn], ALU.add, [("S", buf), ("S", tmp)], [("S", buf)])
                ts("vector", S[:, buf, 0:nn], S[:, buf, 0:nn], PI_SAFE, ALU.min, [("S", buf)], [("S", buf)], s2=-PI_SAFE, op1=ALU.max)

            wrap(r, m)
            act(sinT[:, hh:hh + nn], S[:, r, 0:nn], AF.Sin, [("S", r), "cst"], [("sinT", hh // 512)], scale=col(C_SGN))
            ts("vector", S[:, r, 0:nn], S[:, r, 0:nn], 1.5707963267948966, ALU.add, [("S", r)], [("S", r)])
            wrap(r, m)
            act(cosT[:, hh:hh + nn], S[:, r, 0:nn], AF.Sin, [("S", r)], [("cosT", hh // 512)])

    def l1_rot_chunk(gs, ci, bias_col, dst_fn, slices):
        for sl, (t0, n, sts, tb) in enumerate(slices):
            b = proj(gs, ci, t0, n, sts)
            r, xp, t1 = nS(), nS(), nS()
            act(S[:, r, 0:n], ps[b][:, 0:n], AF.Identity, [("ps", b), "cst"], [("S", r)], bias=col(bias_col))
            for (d0, s0) in ((0, 8), (8, 0), (64, 72), (72, 64)):
                P.dma("scalar", S[d0:d0 + 8, xp, 0:n], S[s0:s0 + 8, r, 0:n], reads=[("S", r)], writes=[("S", xp)])
            stt(S[:, t1, 0:n], ps[b][:, 0:n], col(bias_col), cosT[:, tb:tb + n], ALU.add, ALU.mult,
                [("ps", b), "cst", ("cosT", tb // 512)], [("S", t1)])
            tt("pool", S[:, xp, 0:n], S[:, xp, 0:n], sinT[:, tb:tb + n], ALU.mult, [("S", xp), ("sinT", tb // 512)], [("S", xp)])
            dst, dkeys = dst_fn(t0, n)
            tt("vector", dst, S[:, t1, 0:n], S[:, xp, 0:n], ALU.add, [("S", t1), ("S", xp)], dkeys)

    def l1_z_chunk(gs, ci, c, slices):
        for sl, (t0, n, sts, tb) in enumerate(slices):
            b = proj(gs, ci, t0, n, sts)
            act(big[:, 8 + c, t0:t0 + n], ps[b][:, 0:n], AF.Silu, [("ps", b), "cst"], [("big", 8 + c, t0 // 512)],
                bias=col(C_B + 9 + c))

    def l1_v(gs, ci, st, blk):
        b = nbank()
        w = wg[:, gs, ci, :].rearrange("p (k n) -> p k n", k=8)
        for kc in range(8):
            mm(ps[b][:, 0:128], yT[:, kc, st * 128:(st + 1) * 128], w[:, kc, :], kc == 0, kc == 7,
               [("wg", gs, ci), ("yT", st)], [("ps", b)])
        tt("vector", vaug[:, blk, :, 0:64], ps[b][:, 0:128].rearrange("p (a d) -> p a d", a=2),
           rowc[:, R_BV:R_BV + 128].rearrange("p (a d) -> p a d", a=2), ALU.add, [("ps", b), "rowc"], [("vaug", blk)])

    def attn_scores(st, kv, first_block):
        slots = {}
        for kb in range(2):
            kcol = (st + kb) * 128
            for hg in range(2):
                b = nbank()
                mm(ps[b][:, 0:512], kT[kv * 64:(kv + 1) * 64, kcol:kcol + 128],
                   big[kv * 64:(kv + 1) * 64, hg * 4:(hg + 1) * 4, st * 128:(st + 1) * 128], True, False,
                   [("kT", st + kb)] + [("big", c, st // 4) for c in range(hg * 4, hg * 4 + 4)], [("ps", b)])
                mi = 1 if kb == 1 else (2 if first_block else 0)
                mm(ps[b][:, 0:512], identb[:], maskb[:, mi, :], False, True, ["ident", "mask", "mask2"], [("ps", b)])
                s = state["pt"] % NPT
                state["pt"] += 1
                act(PT[:, s, :], ps[b][:, 0:512], AF.Exp, [("ps", b)], [("PT", s)], scale=0.125)
                slots[(kb, hg)] = s
        return slots

    def attn_pv(st, kv, slots, asl):
        for hg in range(2):
            b = nbank()
            for hl in range(4):
                for kb in range(2):
                    s = slots[(kb, hg)]
                    mm(ps[b][:, hl * 128: hl * 128 + 65], PT[:, s, hl * 128:(hl + 1) * 128], vaug[:, st + kb, kv, :],
                       kb == 0, kb == 1, [("PT", s), ("vaug", st + kb)], [("ps", b)])
            pv = ps[b][:].rearrange("p (a d) -> p a d", a=4)
            es = esink[:].rearrange("p (c k) -> p c k", k=2)[:, hg * 4:(hg + 1) * 4, kv:kv + 1]
            sm = small[:, (kv * 2 + hg) * 4:(kv * 2 + hg) * 4 + 4]
            kk = ("small", kv * 2 + hg)
            tt("vector", sm.unsqueeze(2), pv[:, :, 64:65], es, ALU.add, [("ps", b), "esink"], [kk])
            P.op("vector", lambda e, sm=sm: e.reciprocal(out=sm, in_=sm), [kk], [kk])
            dst = attn[:, asl, :].rearrange("p (c k d) -> p c k d", c=8, k=2)[:, hg * 4:(hg + 1) * 4, kv, :]
            tt("vector", dst, pv[:, :, 0:64], sm.unsqueeze(2).to_broadcast([128, 4, 64]), ALU.mult,
               [("ps", b), kk], [("attn", asl, kv, hg)])

    def attn_finish(st, asl, out_row0, final):
        b = nbank()
        pb = ps[b][:].bitcast(BF16)
        akeys = [("attn", asl, kv, hg) for kv in range(2) for hg in range(2)]
        for c in range(8):
            tr(pb[:, c * 128:(c + 1) * 128], attn[:, asl, c * 128:(c + 1) * 128], akeys, [("ps", b)])
        gk = [("big", 8 + c, st // 4) for c in range(8)]
        tt("vector", big[:, 8:16, st * 128:(st + 1) * 128], pb.rearrange("p (k t) -> p k t", k=8),
           big[:, 8:16, st * 128:(st + 1) * 128], ALU.mult, [("ps", b)] + gk, gk)
        out_proj(st, 8, 8)
        if not final:
            return
        i = state["ssi"]
        state["ssi"] = (state["ssi"] + 1) % 24
        act(junk, h[:, st, :], AF.Square, [("h", st)], JK + [("ss", i)], accum=ss[:, i:i + 1])
        ts("pool", rstd[:, i:i + 1], ss[:, i:i + 1], 1.0 / DM, ALU.mult, [("ss", i)], [("rstd", i)], s2=EPS, op1=ALU.add)
        tt("pool", rstd[:, i:i + 1], rstd[:, i:i + 1], cst[:, C_MHALF:C_MHALF + 1], ALU.pow, [("rstd", i), "cst"], [("rstd", i)])
        stt(h[:, st, :], h[:, st, :], rstd[:, i:i + 1], rowc[:, R_FG:R_FG + 1024], ALU.mult, ALU.mult,
            [("h", st), ("rstd", i), "rowc"], [("h", st)])
        P.dma("pool", out_d[out_row0: out_row0 + 128, :], h[:, st, :], reads=[("h", st)], writes=[("out", out_row0)])

    A_GROUPS = [[4 * g + i for i in range(4)] for g in range(4)]
    B_GROUPS = [[16 + 4 * j + i for i in range(4)] for j in range(8)]
    L0_GROUPS = A_GROUPS + B_GROUPS
    L1_GROUPS = [[0, 1, 2, 3], [4, 5, 6, 7], [8, 17], [9, 10, 11, 12], [13, 14, 15, 16]]
    gslot = {"i": 0}

    def next_gs():
        g = gslot["i"] % 2
        gslot["i"] += 1
        return g

    first_real_done = False
    for (T0, NT, is_halo) in TILES[:ntiles]:
        nst = NT // 128
        load_x(T0, nst)
        if is_halo:
            slices0 = [(0, 256, [0, 1])]
        else:
            slices0 = [(0, 512, [0, 1, 2, 3]), (512, 512, [4, 5, 6, 7])]
        norm_T(0, list(range(nst)))
        first_real = (not is_halo) and (not first_real_done)
        pending = None
        gs_list = []
        for gi, chunks in enumerate(L0_GROUPS):
            gs = next_gs()
            load_group(wie, chunks, gs)
            if gi == 1:
                for kc in range(16):
                    stage_cast(woe[kc], wout[:, kc, :], [("wout", kc)])
            if pending is not None:
                pg, pgs = pending
                if pg < 4:
                    l0_group_a(pg, pgs, slices0, first_real)
                else:
                    l0_group_b(pg - 4, pgs, slices0)
            pending = (gi, gs)
        l1_gs = []
        gs = next_gs()
        load_group(wio, L1_GROUPS[2] if is_halo else L1_GROUPS[0], gs)
        l1_gs.append(gs)
        pg, pgs = pending
        l0_group_b(pg - 4, pgs, slices0)
        if not is_halo:
            first_real_done = True
        l0_sts = [1] if is_halo else list(range(nst))
        for st in l0_sts:
            out_proj(st, 16, 0)
        if mode == "L0":
            if not is_halo:
                for st in range(nst):
                    r0 = T0 - HALO + st * 128
                    P.dma("pool", out_d[r0:r0 + 128, :], h[:, st, :], reads=[("h", st)], writes=[("out", r0)])
            continue
        if dbg < 1:
            continue
        norm_T(1, l0_sts)
        if is_halo:
            if dbg < 2:
                continue
            rope_tables(T0 + 128, 128)
            gs = l1_gs[0]
            if dbg < 3:
                continue
            l1_rot_chunk(gs, 0, C_B + 8, lambda t0, n: (kT[:, 0:128], [("kT", 0)]), [(128, 128, [1], 0)])
            if dbg < 4:
                continue
            l1_v(gs, 1, 1, 0)
            continue
        if dbg < 4.5:
            continue
        rope_tables(T0, NT)
        if dbg < 4.6:
            continue
        slices1 = [(0, 512, [0, 1, 2, 3], 0), (512, 512, [4, 5, 6, 7], 512)]
        import os
        if os.environ.get("SL1") == "a":
            slices1 = [(0, 128, [0], 0)]
        if os.environ.get("SL1") == "b":
            slices1 = [(0, 512, [0, 1, 2, 3], 0)]
        if os.environ.get("SL1") == "c":
            slices1 = [(0, 256, [0, 1], 0)]
        import os
        if not os.environ.get("SKIP_BOUT"):
            for st in range(nst):
                tt("pool", h[:, st, :], h[:, st, :], rowc[:, R_BOUT:R_BOUT + 1024], ALU.add, [("h", st), "rowc"], [("h", st)])
        for gi in range(5):
            if dbg < 5 and dbg < [4.7, 4.8, 4.9, 4.95, 4.97][gi]:
                break
            if gi + 1 < 5:
                gs = next_gs()
                if not os.environ.get("NOLOAD1"):
                    load_group(wio, L1_GROUPS[gi + 1], gs)
                l1_gs.append(gs)
            if gi == 1:
                for kc in range(8):
                    stage_cast(woo[kc], wout[:, kc, :], [("wout", kc)])
            gs = l1_gs[gi]
            if gi in (0, 1):
                for ci in range(int(os.environ.get("ONECI", "4"))):
                    c = gi * 4 + ci
                    l1_rot_chunk(gs, ci, C_B + (0 if os.environ.get("BIAS0") else c),
                                 lambda t0, n, c=c: (big[:, c, t0:t0 + n], [("big", c, t0 // 512)]), slices1)
            elif gi == 2:
                l1_rot_chunk(gs, 0, C_B + 8,
                             lambda t0, n: (kT[:, 128 + t0:128 + t0 + n], [("kT", 1 + t0 // 128 + i) for i in range(n // 128)]),
                             slices1)
                for st in range(nst):
                    l1_v(gs, 1, st, 1 + st)
            elif not os.environ.get("SKIP_Z"):
                for ci in range(4):
                    l1_z_chunk(gs, ci, (gi - 3) * 4 + ci, slices1)
        if dbg < 6:
            continue
        units = [(st, kv) for st in range(nst) for kv in range(2)]
        prev = None
        for ui, (st, kv) in enumerate(units):
            fb = (T0 == HALO and st == 0)
            slots = attn_scores(st, kv, fb)
            if prev is not None:
                pst, pkv, pslots = prev
                if dbg >= 7:
                    attn_pv(pst, pkv, pslots, pst % 2)
                if pkv == 1 and dbg >= 8:
                    attn_finish(pst, pst % 2, T0 - HALO + pst * 128, True)
            prev = (st, kv, slots)
        pst, pkv, pslots = prev
        if dbg >= 7:
            attn_pv(pst, pkv, pslots, pst % 2)
        if dbg >= 8:
            attn_finish(pst, pst % 2, T0 - HALO + pst * 128, True)
        cp("pool", kT[:, 0:128], kT[:, 1024:1152], [("kT", 8)], [("kT", 0)])
        cp("pool", vaug[:, 0, :, :], vaug[:, 8, :, :], [("vaug", 8)], [("vaug", 0)])

    P.emit()
    return nc, P


def _blk(w, cols):
    sub = w[:, cols]
    return np.ascontiguousarray(sub.reshape(8, 128, len(cols)).transpose(1, 0, 2).reshape(128, 8 * len(cols)))


def prepare(inputs):
    f = np.float32
    x = np.asarray(inputs["x"], f)
    pos = np.asarray(inputs["positions"]).astype(np.int32)
    norm_g = np.asarray(inputs["norm_g"], f)
    w_in_even = np.asarray(inputs["w_in_even"], f)[0]
    w_pool = np.asarray(inputs["w_pool"], f)[0]
    pool_scale = np.asarray(inputs["pool_scale"], f)[0]
    conv_w = np.asarray(inputs["conv_w"], f)[0]
    w_out_even = np.asarray(inputs["w_out_even"], f)[0]
    w_in_odd = np.asarray(inputs["w_in_odd"], f)[0]
    b_in_odd = np.asarray(inputs["b_in_odd"], f)[0]
    sinks = np.asarray(inputs["attn_sinks"], f)[0]
    w_out_odd = np.asarray(inputs["w_out_odd"], f)[0]
    b_out_odd = np.asarray(inputs["b_out_odd"], f)[0]
    fg = np.asarray(inputs["final_norm_g"], f)

    ar = np.arange(128)
    cols = []
    for g in range(4):
        cols += [(2 * g) * 128 + ar, (2 * g + 1) * 128 + ar, 4096 + (2 * g) * 128 + ar, 4096 + (2 * g + 1) * 128 + ar]
    for j in range(8):
        cols += [2048 + j * 128 + ar, 3072 + j * 128 + ar, 1024 + j * 128 + ar, 4096 + (8 + j) * 128 + ar]
    wie = np.stack([_blk(w_in_even, c) for c in cols])
    wpl = np.ascontiguousarray(w_pool.reshape(4, 2, 128, 256).transpose(2, 0, 1, 3).reshape(128, 2048))
    wpl = np.ascontiguousarray(wpl.reshape(128, 2, 1024).transpose(1, 0, 2))
    woe = np.ascontiguousarray(w_out_even.reshape(16, 128, 1024))
    perm = np.concatenate([np.concatenate([c * 64 + np.arange(64), (8 + c) * 64 + np.arange(64)]) for c in range(8)])
    ocols = [perm[c * 128:(c + 1) * 128] for c in range(8)]
    ocols += [1024 + ar]
    ocols += [1280 + perm[c * 128:(c + 1) * 128] for c in range(8)]
    ocols += [1152 + ar]
    wio = np.stack([_blk(w_in_odd, c) for c in ocols])
    woo = np.ascontiguousarray(w_out_odd[perm].reshape(8, 128, 1024))
    cst = np.zeros((128, NCST), f)
    cst[:, C_G:C_G + 16] = norm_g.reshape(2, 8, 128).transpose(2, 0, 1).reshape(128, 16)
    cst[:, C_PS:C_PS + 8] = pool_scale.reshape(8, 128).T
    cst[:, C_CW:C_CW + 24] = conv_w.reshape(3, 8, 128).transpose(2, 0, 1).reshape(128, 24)
    for i in range(17):
        cst[:, C_B + i] = b_in_odd[ocols[i]]
    d = ar % 64
    inv_freq = (np.float32(500000.0) ** (-np.arange(0, 16, 2, dtype=np.float32) / np.float32(16))).astype(f)
    cst[:, C_INVF] = np.where(d < 16, inv_freq[d % 8], 0.0)
    cst[:, C_SGN] = np.where(d < 8, -1.0, np.where(d < 16, 1.0, 0.0))
    cst[:, C_MHALF] = -0.5
    rowc = np.zeros((128, NROW), f)
    rowc[:, R_BOUT:R_BOUT + 1024] = b_out_odd[None]
    rowc[:, R_FG:R_FG + 1024] = fg[None]
    rowc[:, R_BV:R_BV + 128] = b_in_odd[1152:1280][None]
    sl = np.zeros(16, f)
    for c in range(8):
        for k in range(2):
            sl[2 * c + k] = sinks[c + 8 * k]
    rowc[:, R_SINK:R_SINK + 16] = sl[None]
    pm = np.zeros((128, 1024), f)
    for m in range(128):
        dd = m % 64
        if dd < 8:
            pm[m + 8, m] = 1.0
        elif dd < 16:
            pm[m - 8, m] = 1.0
    s_ = np.arange(128)[:, None]
    q_ = np.arange(128)[None, :]
    m_prev = np.where(q_ < s_, 0.0, NEG).astype(f)
    m_diag = np.where(q_ >= s_, 0.0, NEG).astype(f)
    m_all = np.full((128, 128), NEG, f)
    in_maps = []
    for c in range(NCORES):
        b, half = c // 2, c % 2
        t0 = half * TOK
        xe = np.zeros((TL, DM), f)
        pe = np.zeros((TL,), np.int32)
        if half == 0:
            xe[HALO:] = x[b, 0:TOK]
            pe[HALO:] = pos[b, 0:TOK]
        else:
            xe[:] = x[b, t0 - HALO:t0 + TOK]
            pe[:] = pos[b, t0 - HALO:t0 + TOK]
        pc = np.ones((4, 16), f)
        if half == 0:
            for g in range(4):
                w = 2 << g
                pc[g] = w / np.minimum(np.arange(16) + 1, w)
        masks = np.zeros((2, 128, 1024), f)
        masks[0, :, 0:512] = np.tile(m_prev, (1, 4))
        masks[0, :, 512:1024] = np.tile(m_diag, (1, 4))
        masks[1, :, 0:512] = np.tile(m_all if half == 0 else m_prev, (1, 4))
        in_maps.append({
            "x_ext": xe, "pos_bc": np.ascontiguousarray(np.broadcast_to(pe[None], (128, TL))),
            "wie": wie, "wpl": wpl, "woe": woe, "wio": wio, "woo": woo, "cst": cst, "rowc": rowc,
            "pcorr": np.ascontiguousarray(np.broadcast_to(pc.reshape(1, 64), (128, 64))),
            "masks": masks, "pmat": pm,
        })
    return in_maps


_CACHE = {}


def kernel(**inputs):
    mode = "fused"
    if mode not in _CACHE:
        _CACHE[mode] = build(mode)[0]
    nc = _CACHE[mode]
    in_maps = prepare(inputs)
    res = run_bass_kernel_spmd(nc, in_maps, core_ids=list(range(NCORES)))
    out = np.empty((4, 8192, DM), np.float32)
    for c in range(NCORES):
        b, half = c // 2, c % 2
        out[b, half * TOK:(half + 1) * TOK] = res.results[c]["out"]
    return out
```

```python
from contextlib import ExitStack
import numpy as np
import concourse.bass as bass
import concourse.mybir as mybir
from concourse.bass_utils import run_bass_kernel_spmd

F32 = mybir.dt.float32
BF16 = mybir.dt.bfloat16
I32 = mybir.dt.int32
AF = mybir.ActivationFunctionType
ALU = mybir.AluOpType
AX = mybir.AxisListType


class _I:
    __slots__ = ("eng", "fn", "kind", "deps", "signal", "sig", "idx", "rawdeps", "line", "rw")


class Prog:
    NDMA = 30
    SAME_ENGINE_RAW = True

    def __init__(self, nc):
        self.nc = nc
        self.stack = ExitStack()
        self.instrs = []
        self.lw = {}
        self.rd = {}
        self.trace = None

    def sbuf(self, name, shape, dtype):
        return self.stack.enter_context(self.nc.sbuf_tensor(name, shape, dtype))

    def psum(self, name, shape, dtype):
        return self.stack.enter_context(self.nc.psum_tensor(name, shape, dtype))

    def _add(self, eng, fn, reads, writes, kind):
        ins = _I()
        ins.eng, ins.fn, ins.kind = eng, fn, kind
        ins.idx = len(self.instrs)
        import sys as _sys
        f = _sys._getframe(2)
        ln = []
        while f is not None and len(ln) < 4:
            ln.append(str(f.f_lineno))
            f = f.f_back
        ins.line = "<".join(ln)
        ins.rw = (reads, writes)
        ins.signal = False
        ins.sig = None
        deps = {}
        for k in reads:
            w = self.lw.get(k)
            if w is not None:
                deps[w.idx] = (w, True)
        for k in writes:
            w = self.lw.get(k)
            if w is not None and w.idx not in deps:
                deps[w.idx] = (w, False)
            for r in self.rd.get(k, ()):
                if r.idx not in deps:
                    deps[r.idx] = (r, False)
        out = []
        for d, raw in deps.values():
            if d.kind == "op" and d.eng == eng and kind == "op":
                if eng == "tensor" or not self.SAME_ENGINE_RAW:
                    continue
            out.append(d)
        ins.deps = out
        for k in reads:
            lst = self.rd.setdefault(k, [])
            if kind == "op":
                lst[:] = [r for r in lst if not (r.kind == "op" and r.eng == eng)]
            lst.append(ins)
        for k in writes:
            self.lw[k] = ins
            self.rd[k] = []
        self.instrs.append(ins)
        return ins

    def op(self, eng, fn, reads=(), writes=()):
        return self._add(eng, fn, tuple(reads), tuple(writes), "op")

    def dma(self, eng, out, in_, reads=(), writes=(), is_output=False):
        return self._add(eng, lambda e: e.dma_start(out=out, in_=in_), tuple(reads), tuple(writes), "dma")

    def emit(self):
        nc = self.nc
        st = self.stack
        engs = ["tensor", "vector", "scalar", "pool", "sp"]
        esem = {e: st.enter_context(nc.semaphore("sem_" + e)) for e in engs}
        dsem = [st.enter_context(nc.semaphore("dsem%d" % i)) for i in range(self.NDMA)]
        for ins in self.instrs:
            for d in ins.deps:
                d.signal = True
        cnt = {e: 0 for e in engs}
        dcnt = [0] * self.NDMA
        dlast = [None] * self.NDMA
        pools = {"sp": list(range(0, self.NDMA - 14)), "pool": list(range(self.NDMA - 14, self.NDMA - 8)),
                 "scalar": list(range(self.NDMA - 8, self.NDMA))}
        nd = {"sp": 0, "pool": 0, "scalar": 0}
        for ins in self.instrs:
            if ins.kind == "dma":
                pl = pools[ins.eng]
                s = pl[nd[ins.eng] % len(pl)]
                nd[ins.eng] += 1
                if dlast[s] is not None:
                    ins.deps.append(dlast[s])
                dcnt[s] += 16
                ins.sig = (dsem[s], dcnt[s])
                dlast[s] = ins
                ins.signal = True
            elif ins.signal:
                cnt[ins.eng] += 1
                ins.sig = (esem[ins.eng], cnt[ins.eng])
        per = {e: [i for i in self.instrs if i.eng == e] for e in engs}
        self.stats = {e: len(per[e]) for e in engs}
        self.stats["signals"] = dict(cnt)
        block = st.enter_context(nc.Block())

        def run(e, lst, final):
            waited = {}
            for ins in lst:
                if self.trace is not None:
                    self.trace.append((ins.eng, ins.idx, ins.kind, ins.line, [(d.eng, d.idx, d.sig[1]) for d in ins.deps], ins.sig[1] if ins.sig else None, ins.rw))
                need = {}
                for d in ins.deps:
                    sem, val = d.sig
                    key = id(sem)
                    if waited.get(key, 0) >= val:
                        continue
                    if key not in need or need[key][1] < val:
                        need[key] = (sem, val)
                for key, (sem, val) in need.items():
                    e.wait_ge(sem, val)
                    waited[key] = val
                r = ins.fn(e)
                if ins.signal:
                    sem, val = ins.sig
                    r.then_inc(sem, 16 if ins.kind == "dma" else 1)
            if final:
                for s in range(self.NDMA):
                    if dcnt[s] and waited.get(id(dsem[s]), 0) < dcnt[s]:
                        e.wait_ge(dsem[s], dcnt[s])

        @block.tensor
        def _(e):
            run(e, per["tensor"], False)

        @block.vector
        def _(e):
            run(e, per["vector"], False)

        @block.scalar
        def _(e):
            run(e, per["scalar"], True)

        @block.gpsimd
        def _(e):
            run(e, per["pool"], True)

        @block.sync
        def _(e):
            run(e, per["sp"], True)

        st.close()


NCORES = 8
DM = 1024
TOK = 4096
HALO = 256
TL = TOK + HALO
EPS = 1e-5
NEG = -30000.0
TWO_PI = 6.283185307179586
C1 = 6.28125
C2 = TWO_PI - C1
PI_SAFE = 3.1415925
TILES = [(0, 256, True)] + [(HALO + 1024 * i, 1024, False) for i in range(4)]

C_G = 0
C_PS = 16
C_CW = 24
C_B = 48
C_INVF = 65
C_SGN = 66
C_MHALF = 67
NCST = 68
R_BOUT = 0
R_FG = 1024
R_BV = 2048
R_SINK = 2176
NROW = 2192


def build(mode="fused", ntiles=5, dbg=99):
    nc = bass.Bass("TRN2", target_bir_lowering=False)
    P = Prog(nc)

    def din(name, shape, dt=F32):
        return nc.dram_tensor(name, shape, dt, kind="ExternalInput").ap()

    x_ext = din("x_ext", [TL, DM])
    pos_bc = din("pos_bc", [128, TL], I32)
    wie = din("wie", [48, 128, 1024])
    wpl = din("wpl", [2, 128, 1024])
    woe = din("woe", [16, 128, 1024])
    wio = din("wio", [18, 128, 1024])
    woo = din("woo", [8, 128, 1024])
    cst_d = din("cst", [128, NCST])
    rowc_d = din("rowc", [128, NROW])
    pcorr_d = din("pcorr", [128, 64])
    masks_d = din("masks", [2, 128, 1024])
    pmat_d = din("pmat", [128, 1024])
    out_d = nc.dram_tensor("out", [TOK, DM], F32, kind="ExternalOutput").ap()
    wscr = nc.dram_tensor("wscr", [90, 128, 1024], BF16, kind="Internal").ap()
    PID = {"wie": 0, "woe": 48, "wio": 64, "woo": 82}

    NSTG = 3
    h = P.sbuf("h", [128, 8, 1024], F32)
    ybuf = P.sbuf("ybuf", [128, 2, 1024], BF16)
    yT = P.sbuf("yT", [128, 8, 1024], BF16)
    big = P.sbuf("big", [128, 16, 1024], BF16)
    stg = P.sbuf("stg", [128, NSTG, 1024], F32)
    wg = P.sbuf("wg", [128, 2, 4, 1024], BF16)
    wout = P.sbuf("wout", [128, 16, 1024], BF16)
    wpool = P.sbuf("wpool", [128, 2, 1024], BF16)
    cst = P.sbuf("cstt", [128, NCST], F32)
    rowc = P.sbuf("rowct", [128, NROW], F32)
    pcorr = P.sbuf("pcorrt", [128, 64], F32)
    maskb = P.sbuf("maskb", [128, 3, 512], BF16)
    identf = P.sbuf("identf", [128, 128], F32)
    identb = P.sbuf("identb", [128, 128], BF16)
    esink = P.sbuf("esink", [128, 16], F32)
    ucar = P.sbuf("ucar", [128, 8, 16], F32)
    ccar = P.sbuf("ccar", [128, 8, 2], F32)
    NS = 8
    S = P.sbuf("S", [128, NS, 528], F32)
    pooled = P.sbuf("pooled", [128, 2, 512], BF16)
    xbq = P.sbuf("xbq", [128, 2, 512], BF16)
    kT = P.sbuf("kT", [128, 128 + 1024], BF16)
    vaug = P.sbuf("vaug", [128, 9, 2, 65], BF16)
    cosT = P.sbuf("cosT", [128, 1024], F32)
    sinT = P.sbuf("sinT", [128, 1024], F32)
    NPT = 8
    PT = P.sbuf("PT", [128, NPT, 512], BF16)
    attn = P.sbuf("attn", [128, 2, 1024], BF16)
    ss = P.sbuf("ss", [128, 32], F32)
    rstd = P.sbuf("rstd", [128, 32], F32)
    small = P.sbuf("small", [128, 16], F32)
    ps = [P.psum("ps%d" % i, [128, 512], F32) for i in range(8)]
    junk = xbq[:].rearrange("p a b -> p (a b)")
    JK = [("xbq", 0), ("xbq", 1)]

    state = {"bank": 0, "stg": 0, "S": 0, "pt": 0, "ssi": 0, "xs": 0}

    def nbank():
        b = state["bank"] % 8
        state["bank"] += 1
        return b

    def nS():
        s = state["S"] % NS
        state["S"] += 1
        return s

    def col(i):
        return cst[:, i:i + 1]

    def mm(out, lhsT, rhs, start, stop, reads, writes):
        P.op("tensor", lambda e: e.matmul(out=out, lhsT=lhsT, rhs=rhs, start=start, stop=stop),
             reads, writes)

    def tr(out, in_, reads, writes):
        P.op("tensor", lambda e: e.transpose(out=out, in_=in_, identity=identb[:]), list(reads) + ["ident"], writes)

    def act(out, in_, func, reads, writes, scale=None, bias=None, accum=None):
        kw = {}
        if scale is not None:
            kw["scale"] = scale
        if bias is not None:
            kw["bias"] = bias
        if accum is not None:
            kw["accum_out"] = accum
        P.op("scalar", lambda e: e.activation(out=out, in_=in_, func=func, **kw), reads, writes)

    def tt(eng, out, in0, in1, op, reads, writes):
        P.op(eng, lambda e: e.tensor_tensor(out=out, in0=in0, in1=in1, op=op), reads, writes)

    def stt(out, in0, scalar, in1, op0, op1, reads, writes):
        P.op("vector", lambda e: e.scalar_tensor_tensor(out=out, in0=in0, scalar=scalar, in1=in1, op0=op0, op1=op1),
             reads, writes)

    def ts(eng, out, in0, s1, op0, reads, writes, s2=None, op1=None):
        if op1 is None:
            P.op(eng, lambda e: e.tensor_scalar(out=out, in0=in0, scalar1=s1, scalar2=None, op0=op0), reads, writes)
        else:
            P.op(eng, lambda e: e.tensor_scalar(out=out, in0=in0, scalar1=s1, scalar2=s2, op0=op0, op1=op1), reads, writes)

    def cp(eng, out, in_, reads, writes):
        if eng == "scalar":
            P.op(eng, lambda e: e.copy(out=out, in_=in_), reads, writes)
        else:
            P.op(eng, lambda e: e.tensor_copy(out=out, in_=in_), reads, writes)

    cast_rr = {"i": 0}
    CAST_ENGS = ["pool", "pool", "scalar"]

    cached = set()

    def stage_cast(dram_ap, dst_ap, dst_keys, ncols=1024, eng=None, pid=None):
        if pid is not None and pid in cached:
            P.dma("sp", dst_ap, wscr[pid], reads=[("scr", pid)], writes=dst_keys)
            return
        stage_cast_(dram_ap, dst_ap, dst_keys, ncols, eng)
        if pid is not None:
            P.dma("pool", wscr[pid], dst_ap, reads=dst_keys, writes=[("scr", pid)])
            cached.add(pid)

    def stage_cast_(dram_ap, dst_ap, dst_keys, ncols=1024, eng=None):
        s = state["stg"] % NSTG
        state["stg"] += 1
        P.dma("sp", stg[:, s, 0:ncols], dram_ap, reads=[], writes=[("stg", s)])
        if eng is None:
            eng = CAST_ENGS[cast_rr["i"] % len(CAST_ENGS)]
            cast_rr["i"] += 1
        cp(eng, dst_ap, stg[:, s, 0:ncols], [("stg", s)], dst_keys)

    P.dma("sp", cst[:], cst_d, writes=["cst"])
    P.dma("sp", rowc[:], rowc_d, writes=["rowc"])
    P.dma("sp", pcorr[:], pcorr_d, writes=["pcorr"])
    P.op("pool", lambda e: e.memset(identf[:], 0.0), writes=["identf"])
    P.op("pool", lambda e: e.affine_select(out=identf[:], in_=identf[:], pattern=[[-1, 128]],
                                           compare_op=ALU.not_equal, fill=1.0, base=0, channel_multiplier=1),
         reads=["identf"], writes=["identf"])
    cp("vector", identb[:], identf[:], ["identf"], ["ident"])
    P.op("pool", lambda e: e.memset(ucar[:], 0.0), writes=["ucar%d" % i for i in range(8)])
    P.op("pool", lambda e: e.memset(ccar[:], 0.0), writes=["ccar%d" % i for i in range(8)])
    P.op("vector", lambda e: e.memset(vaug[:, :, :, 64:65], 1.0), writes=[("vaug", i) for i in range(9)])
    for i in range(2):
        stage_cast(wpl[i], wpool[:, i, :], [("wpool", i)], eng="vector")
    stage_cast(masks_d[0], maskb[:, 0:2, :].rearrange("p a b -> p (a b)"), ["mask"], eng="vector")
    stage_cast(masks_d[1][:, 0:512], maskb[:, 2, :], ["mask2"], ncols=512, eng="vector")
    P.op("vector", lambda e: e.memset(S[:], 0.0), writes=[("S", i) for i in range(NS)])
    act(esink[:], rowc[:, R_SINK:R_SINK + 16], AF.Exp, ["rowc"], ["esink"])

    def load_x(t0, nst):
        for st in range(nst):
            P.dma("sp", h[:, st, :], x_ext[t0 + st * 128: t0 + (st + 1) * 128, :], writes=[("h", st)])

    def norm_T(layer, sts):
        base = state["ssi"]
        state["ssi"] = (state["ssi"] + 8) % 24
        for st in sts:
            act(ybuf[:, st % 2, :], h[:, st, :], AF.Square, [("h", st)], [("ybuf", st % 2), ("ss", base + st)],
                accum=ss[:, base + st: base + st + 1])
        lo, hi = base + sts[0], base + sts[-1] + 1
        keys_ss = [("ss", base + st) for st in sts]
        keys_r = [("rstd", base + st) for st in sts]
        ts("pool", rstd[:, lo:hi], ss[:, lo:hi], 1.0 / DM, ALU.mult, keys_ss, keys_r, s2=EPS, op1=ALU.add)
        tt("pool", rstd[:, lo:hi], rstd[:, lo:hi], cst[:, C_MHALF:C_MHALF + 1].to_broadcast([128, hi - lo]),
           ALU.pow, keys_r + ["cst"], keys_r)
        for st in sts:
            sl = st % 2
            act(ybuf[:, sl, :], h[:, st, :], AF.Copy, [("h", st), ("rstd", base + st)], [("ybuf", sl)],
                scale=rstd[:, base + st: base + st + 1])
            b = nbank()
            pb = ps[b][:].bitcast(BF16)
            for kc in range(8):
                tr(pb[:, kc * 128:(kc + 1) * 128], ybuf[:, sl, kc * 128:(kc + 1) * 128], [("ybuf", sl)], [("ps", b)])
            tt("vector", yT[:, :, st * 128:(st + 1) * 128], pb.rearrange("p (k t) -> p k t", k=8),
               cst[:, C_G + layer * 8: C_G + layer * 8 + 8].unsqueeze(2).to_broadcast([128, 8, 128]), ALU.mult,
               [("ps", b), "cst"], [("yT", st)])

    def load_group(src, chunk_ids, gs, base):
        for ci, ch in enumerate(chunk_ids):
            stage_cast(src[ch], wg[:, gs, ci, :], [("wg", gs, ci)], pid=base + ch)

    def proj(gs, ci, t0, n, sts):
        b = nbank()
        w = wg[:, gs, ci, :].rearrange("p (k n) -> p k n", k=8)
        for kc in range(8):
            mm(ps[b][:, 0:n], w[:, kc, :], yT[:, kc, t0:t0 + n], kc == 0, kc == 7,
               [("wg", gs, ci)] + [("yT", st) for st in sts], [("ps", b)])
        return b

    def l0_group_a(g, gs, slices, first_real):
        w = 2 << g
        for sl, (t0, n, sts) in enumerate(slices):
            bu = [proj(gs, 0, t0, n, sts), proj(gs, 1, t0, n, sts)]
            bz = [proj(gs, 2, t0, n, sts), proj(gs, 3, t0, n, sts)]
            zs = []
            for i in range(2):
                c = 2 * g + i
                ub, ta, tb = nS(), nS(), nS()
                kc_ = "ucar%d" % c
                cp("pool", S[:, ub, 0:16], ucar[:, c, :], [kc_], [("S", ub)])
                cp("scalar", S[:, ub, 16:16 + n], ps[bu[i]][:, 0:n], [("ps", bu[i])], [("S", ub)])
                cp("pool", ucar[:, c, :], S[:, ub, n:n + 16], [("S", ub)], [kc_])
                src = ub
                lvl = 1
                dst = ta
                while (1 << lvl) <= w:
                    sh = 1 << (lvl - 1)
                    lo = 16 - (16 - (1 << lvl)) if (1 << lvl) < 16 else 16
                    lo = (1 << lvl)
                    tt("vector", S[:, dst, lo:16 + n], S[:, src, lo:16 + n], S[:, src, lo - sh:16 + n - sh], ALU.add,
                       [("S", src)], [("S", dst)])
                    src = dst
                    dst = tb if dst == ta else ta
                    lvl += 1
                if first_real and sl == 0:
                    tt("vector", S[:, src, 16:32], S[:, src, 16:32], pcorr[:, g * 16:(g + 1) * 16], ALU.mult,
                       [("S", src), "pcorr"], [("S", src)])
                stt(pooled[:, i, 0:n], S[:, src, 16:16 + n], 1.0 / w, S[:, ub, 16:16 + n], ALU.mult, ALU.subtract,
                    [("S", src), ("S", ub)], [("pooled", i)])
                z = nS()
                act(S[:, z, 0:n], ps[bz[i]][:, 0:n], AF.Silu, [("ps", bz[i])], [("S", z)])
                zs.append(z)
            for oc in range(2):
                c = 2 * g + oc
                b = nbank()
                for kc in range(2):
                    o = (g % 2) * 512 + kc * 256 + oc * 128
                    mm(ps[b][:, 0:n], wpool[:, g // 2, o:o + 128], pooled[:, kc, 0:n], kc == 0, kc == 1,
                       [("wpool", g // 2), ("pooled", 0), ("pooled", 1)], [("ps", b)])
                stt(big[:, c, t0:t0 + n], ps[b][:, 0:n], col(C_PS + c), S[:, zs[oc], 0:n], ALU.mult, ALU.mult,
                    [("ps", b), "cst", ("S", zs[oc])], [("big", c, t0 // 512)])

    def l0_group_b(j, gs, slices):
        for sl, (t0, n, sts) in enumerate(slices):
            bgc = proj(gs, 0, t0, n, sts)
            bhc = proj(gs, 1, t0, n, sts)
            bgb = proj(gs, 2, t0, n, sts)
            bzb = proj(gs, 3, t0, n, sts)
            hc, cu, v0, v1, zz, gz = nS(), nS(), nS(), nS(), nS(), nS()
            kc_ = "ccar%d" % j
            cp("scalar", S[:, hc, 0:n], ps[bhc][:, 0:n], [("ps", bhc)], [("S", hc)])
            cp("pool", S[:, cu, 0:2], ccar[:, j, :], [kc_], [("S", cu)])
            tt("vector", S[:, cu, 2:2 + n], ps[bgc][:, 0:n], S[:, hc, 0:n], ALU.mult, [("ps", bgc), ("S", hc)], [("S", cu)])
            cp("pool", ccar[:, j, :], S[:, cu, n:n + 2], [("S", cu)], [kc_])
            act(S[:, v0, 0:n], S[:, cu, 2:2 + n], AF.Copy, [("S", cu), "cst"], [("S", v0)], scale=col(C_CW + 16 + j))
            stt(S[:, v1, 0:n], S[:, cu, 1:1 + n], col(C_CW + 8 + j), S[:, v0, 0:n], ALU.mult, ALU.add,
                [("S", cu), ("S", v0), "cst"], [("S", v1)])
            stt(S[:, v0, 0:n], S[:, cu, 0:n], col(C_CW + j), S[:, v1, 0:n], ALU.mult, ALU.add,
                [("S", cu), ("S", v1), "cst"], [("S", v0)])
            act(S[:, zz, 0:n], ps[bzb][:, 0:n], AF.Silu, [("ps", bzb)], [("S", zz)])
            tt("vector", S[:, gz, 0:n], ps[bgb][:, 0:n], S[:, zz, 0:n], ALU.mult, [("ps", bgb), ("S", zz)], [("S", gz)])
            tt("pool", big[:, 8 + j, t0:t0 + n], S[:, gz, 0:n], S[:, v0, 0:n], ALU.mult, [("S", gz), ("S", v0)],
               [("big", 8 + j, t0 // 512)])

    def out_proj(st, nkc, coff):
        for nh in range(2):
            b = nbank()
            for kc in range(nkc):
                mm(ps[b][:, 0:512], big[:, coff + kc, st * 128:(st + 1) * 128], wout[:, kc, nh * 512:(nh + 1) * 512],
                   kc == 0, kc == nkc - 1, [("big", coff + kc, st // 4), ("wout", kc)], [("ps", b)])
            tt("vector", h[:, st, nh * 512:(nh + 1) * 512], ps[b][:, 0:512], h[:, st, nh * 512:(nh + 1) * 512], ALU.add,
               [("ps", b), ("h", st)], [("h", st)])

    def rope_tables(t0, n):
        a, k_, r, m = nS(), nS(), nS(), nS()
        posi = S[:, a, :].bitcast(I32)
        for hh in range(0, n, 512):
            nn = min(512, n - hh)
            ki = S[:, k_, 0:nn].bitcast(I32)
            P.dma("sp", posi[:, 0:nn], pos_bc[:, t0 + hh:t0 + hh + nn], writes=[("S", a)])
            ts("vector", S[:, r, 0:nn], posi[:, 0:nn], col(C_INVF), ALU.mult, [("S", a), "cst"], [("S", r)])
            ts("vector", ki, S[:, r, 0:nn], 1.0 / TWO_PI, ALU.mult, [("S", r)], [("S", k_)])
            stt(S[:, m, 0:nn], ki, -C1, S[:, r, 0:nn], ALU.mult, ALU.add, [("S", k_), ("S", r)], [("S", m)])
            stt(S[:, r, 0:nn], ki, -C2, S[:, m, 0:nn], ALU.mult, ALU.add, [("S", k_), ("S", m)], [("S", r)])

            def wrap(buf, tmp):
                ts("vector", S[:, tmp, 0:nn], S[:, buf, 0:nn], PI_SAFE, ALU.is_gt, [("S", buf)], [("S", tmp)], s2=-TWO_PI, op1=ALU.mult)
                tt("vector", S[:, buf, 0:nn], S[:, buf, 0:nn], S[:, tmp, 0:nn], ALU.add, [("S", buf), ("S", tmp)], [("S", buf)])
                ts("vector", S[:, tmp, 0:nn], S[:, buf, 0:nn], -PI_SAFE, ALU.is_lt, [("S", buf)], [("S", tmp)], s2=TWO_PI, op1=ALU.mult)
                tt("vector", S[:, buf, 0:nn], S[:, buf, 0:nn], S[:, tmp, 0:nn], ALU.add, [("S", buf), ("S", tmp)], [("S", buf)])
                ts("vector", S[:, buf, 0:nn], S[:, buf, 0:nn], PI_SAFE, ALU.min, [("S", buf)], [("S", buf)], s2=-PI_SAFE, op1=ALU.max)

            wrap(r, m)
            act(sinT[:, hh:hh + nn], S[:, r, 0:nn], AF.Sin, [("S", r), "cst"], [("sinT", hh // 512)], scale=col(C_SGN))
            ts("vector", S[:, r, 0:nn], S[:, r, 0:nn], 1.5707963267948966, ALU.add, [("S", r)], [("S", r)])
            wrap(r, m)
            act(cosT[:, hh:hh + nn], S[:, r, 0:nn], AF.Sin, [("S", r)], [("cosT", hh // 512)])

    def l1_rot_chunk(gs, ci, bias_col, dst_fn, slices):
        for sl, (t0, n, sts, tb) in enumerate(slices):
            b = proj(gs, ci, t0, n, sts)
            r, xp, t1 = nS(), nS(), nS()
            act(S[:, r, 0:n], ps[b][:, 0:n], AF.Identity, [("ps", b), "cst"], [("S", r)], bias=col(bias_col))
            for (d0, s0) in ((0, 8), (8, 0), (64, 72), (72, 64)):
                P.dma("scalar", S[d0:d0 + 8, xp, 0:n], S[s0:s0 + 8, r, 0:n], reads=[("S", r)], writes=[("S", xp)])
            stt(S[:, t1, 0:n], ps[b][:, 0:n], col(bias_col), cosT[:, tb:tb + n], ALU.add, ALU.mult,
                [("ps", b), "cst", ("cosT", tb // 512)], [("S", t1)])
            tt("pool", S[:, xp, 0:n], S[:, xp, 0:n], sinT[:, tb:tb + n], ALU.mult, [("S", xp), ("sinT", tb // 512)], [("S", xp)])
            dst, dkeys = dst_fn(t0, n)
            tt("vector", dst, S[:, t1, 0:n], S[:, xp, 0:n], ALU.add, [("S", t1), ("S", xp)], dkeys)

    def l1_z_chunk(gs, ci, c, slices):
        for sl, (t0, n, sts, tb) in enumerate(slices):
            b = proj(gs, ci, t0, n, sts)
            act(big[:, 8 + c, t0:t0 + n], ps[b][:, 0:n], AF.Silu, [("ps", b), "cst"], [("big", 8 + c, t0 // 512)],
                bias=col(C_B + 9 + c))

    def l1_v(gs, ci, st, blk):
        b = nbank()
        w = wg[:, gs, ci, :].rearrange("p (k n) -> p k n", k=8)
        for kc in range(8):
            mm(ps[b][:, 0:128], yT[:, kc, st * 128:(st + 1) * 128], w[:, kc, :], kc == 0, kc == 7,
               [("wg", gs, ci), ("yT", st)], [("ps", b)])
        tt("vector", vaug[:, blk, :, 0:64], ps[b][:, 0:128].rearrange("p (a d) -> p a d", a=2),
           rowc[:, R_BV:R_BV + 128].rearrange("p (a d) -> p a d", a=2), ALU.add, [("ps", b), "rowc"], [("vaug", blk)])

    def attn_scores(st, kv, first_block):
        slots = {}
        for kb in range(2):
            kcol = (st + kb) * 128
            for hg in range(2):
                b = nbank()
                mm(ps[b][:, 0:512], kT[kv * 64:(kv + 1) * 64, kcol:kcol + 128],
                   big[kv * 64:(kv + 1) * 64, hg * 4:(hg + 1) * 4, st * 128:(st + 1) * 128], True, False,
                   [("kT", st + kb)] + [("big", c, st // 4) for c in range(hg * 4, hg * 4 + 4)], [("ps", b)])
                mi = 1 if kb == 1 else (2 if first_block else 0)
                mm(ps[b][:, 0:512], identb[:], maskb[:, mi, :], False, True, ["ident", "mask", "mask2"], [("ps", b)])
                s = state["pt"] % NPT
                state["pt"] += 1
                act(PT[:, s, :], ps[b][:, 0:512], AF.Exp, [("ps", b)], [("PT", s)], scale=0.125)
                slots[(kb, hg)] = s
        return slots

    def attn_pv(st, kv, slots, asl):
        for hg in range(2):
            b = nbank()
            for hl in range(4):
                for kb in range(2):
                    s = slots[(kb, hg)]
                    mm(ps[b][:, hl * 128: hl * 128 + 65], PT[:, s, hl * 128:(hl + 1) * 128], vaug[:, st + kb, kv, :],
                       kb == 0, kb == 1, [("PT", s), ("vaug", st + kb)], [("ps", b)])
            pv = ps[b][:].rearrange("p (a d) -> p a d", a=4)
            es = esink[:].rearrange("p (c k) -> p c k", k=2)[:, hg * 4:(hg + 1) * 4, kv:kv + 1]
            sm = small[:, (kv * 2 + hg) * 4:(kv * 2 + hg) * 4 + 4]
            kk = ("small", kv * 2 + hg)
            tt("vector", sm.unsqueeze(2), pv[:, :, 64:65], es, ALU.add, [("ps", b), "esink"], [kk])
            P.op("vector", lambda e, sm=sm: e.reciprocal(out=sm, in_=sm), [kk], [kk])
            dst = attn[:, asl, :].rearrange("p (c k d) -> p c k d", c=8, k=2)[:, hg * 4:(hg + 1) * 4, kv, :]
            tt("vector", dst, pv[:, :, 0:64], sm.unsqueeze(2).to_broadcast([128, 4, 64]), ALU.mult,
               [("ps", b), kk], [("attn", asl, kv, hg)])

    def attn_finish(st, asl, out_row0, final):
        b = nbank()
        pb = ps[b][:].bitcast(BF16)
        akeys = [("attn", asl, kv, hg) for kv in range(2) for hg in range(2)]
        for c in range(8):
            tr(pb[:, c * 128:(c + 1) * 128], attn[:, asl, c * 128:(c + 1) * 128], akeys, [("ps", b)])
        gk = [("big", 8 + c, st // 4) for c in range(8)]
        tt("vector", big[:, 8:16, st * 128:(st + 1) * 128], pb.rearrange("p (k t) -> p k t", k=8),
           big[:, 8:16, st * 128:(st + 1) * 128], ALU.mult, [("ps", b)] + gk, gk)
        out_proj(st, 8, 8)
        if not final:
            return
        i = state["ssi"]
        state["ssi"] = (state["ssi"] + 1) % 24
        act(junk, h[:, st, :], AF.Square, [("h", st)], JK + [("ss", i)], accum=ss[:, i:i + 1])
        ts("pool", rstd[:, i:i + 1], ss[:, i:i + 1], 1.0 / DM, ALU.mult, [("ss", i)], [("rstd", i)], s2=EPS, op1=ALU.add)
        tt("pool", rstd[:, i:i + 1], rstd[:, i:i + 1], cst[:, C_MHALF:C_MHALF + 1], ALU.pow, [("rstd", i), "cst"], [("rstd", i)])
        stt(h[:, st, :], h[:, st, :], rstd[:, i:i + 1], rowc[:, R_FG:R_FG + 1024], ALU.mult, ALU.mult,
            [("h", st), ("rstd", i), "rowc"], [("h", st)])
        P.dma("pool", out_d[out_row0: out_row0 + 128, :], h[:, st, :], reads=[("h", st)], writes=[("out", out_row0)])

    A_GROUPS = [[4 * g + i for i in range(4)] for g in range(4)]
    B_GROUPS = [[16 + 4 * j + i for i in range(4)] for j in range(8)]
    L0_GROUPS = A_GROUPS + B_GROUPS
    L1_GROUPS = [[0, 1, 2, 3], [4, 5, 6, 7], [8, 17], [9, 10, 11, 12], [13, 14, 15, 16]]
    gslot = {"i": 0}

    def next_gs():
        g = gslot["i"] % 2
        gslot["i"] += 1
        return g

    first_real_done = False
    for (T0, NT, is_halo) in TILES[:ntiles]:
        nst = NT // 128
        load_x(T0, nst)
        if is_halo:
            slices0 = [(0, 256, [0, 1])]
        else:
            slices0 = [(0, 512, [0, 1, 2, 3]), (512, 512, [4, 5, 6, 7])]
        norm_T(0, list(range(nst)))
        first_real = (not is_halo) and (not first_real_done)
        pending = None
        gs_list = []
        for gi, chunks in enumerate(L0_GROUPS):
            gs = next_gs()
            load_group(wie, chunks, gs, PID["wie"])
            if gi == 1:
                for kc in range(16):
                    stage_cast(woe[kc], wout[:, kc, :], [("wout", kc)], pid=PID["woe"] + kc)
            if pending is not None:
                pg, pgs = pending
                if pg < 4:
                    l0_group_a(pg, pgs, slices0, first_real)
                else:
                    l0_group_b(pg - 4, pgs, slices0)
            pending = (gi, gs)
        l1_gs = []
        gs = next_gs()
        load_group(wio, L1_GROUPS[2] if is_halo else L1_GROUPS[0], gs, PID["wio"])
        l1_gs.append(gs)
        pg, pgs = pending
        l0_group_b(pg - 4, pgs, slices0)
        if not is_halo:
            first_real_done = True
        l0_sts = [1] if is_halo else list(range(nst))
        for st in l0_sts:
            out_proj(st, 16, 0)
        if mode == "L0":
            if not is_halo:
                for st in range(nst):
                    r0 = T0 - HALO + st * 128
                    P.dma("pool", out_d[r0:r0 + 128, :], h[:, st, :], reads=[("h", st)], writes=[("out", r0)])
            continue
        if dbg < 1:
            continue
        norm_T(1, l0_sts)
        if is_halo:
            if dbg < 2:
                continue
            rope_tables(T0 + 128, 128)
            gs = l1_gs[0]
            if dbg < 3:
                continue
            l1_rot_chunk(gs, 0, C_B + 8, lambda t0, n: (kT[:, 0:128], [("kT", 0)]), [(128, 128, [1], 0)])
            if dbg < 4:
                continue
            l1_v(gs, 1, 1, 0)
            continue
        if dbg < 4.5:
            continue
        rope_tables(T0, NT)
        if dbg < 4.6:
            continue
        slices1 = [(0, 512, [0, 1, 2, 3], 0), (512, 512, [4, 5, 6, 7], 512)]
        import os
        if os.environ.get("SL1") == "a":
            slices1 = [(0, 128, [0], 0)]
        if os.environ.get("SL1") == "b":
            slices1 = [(0, 512, [0, 1, 2, 3], 0)]
        if os.environ.get("SL1") == "c":
            slices1 = [(0, 256, [0, 1], 0)]
        import os
        if not os.environ.get("SKIP_BOUT"):
            for st in range(nst):
                tt("pool", h[:, st, :], h[:, st, :], rowc[:, R_BOUT:R_BOUT + 1024], ALU.add, [("h", st), "rowc"], [("h", st)])
        for gi in range(5):
            if dbg < 5 and dbg < [4.7, 4.8, 4.9, 4.95, 4.97][gi]:
                break
            if gi + 1 < 5:
                gs = next_gs()
                if not os.environ.get("NOLOAD1"):
                    load_group(wio, L1_GROUPS[gi + 1], gs, PID["wio"])
                l1_gs.append(gs)
            if gi == 1:
                for kc in range(8):
                    stage_cast(woo[kc], wout[:, kc, :], [("wout", kc)], pid=PID["woo"] + kc)
            gs = l1_gs[gi]
            if gi in (0, 1):
                for ci in range(int(os.environ.get("ONECI", "4"))):
                    c = gi * 4 + ci
                    l1_rot_chunk(gs, ci, C_B + (0 if os.environ.get("BIAS0") else c),
                                 lambda t0, n, c=c: (big[:, c, t0:t0 + n], [("big", c, t0 // 512)]), slices1)
            elif gi == 2:
                l1_rot_chunk(gs, 0, C_B + 8,
                             lambda t0, n: (kT[:, 128 + t0:128 + t0 + n], [("kT", 1 + t0 // 128 + i) for i in range(n // 128)]),
                             slices1)
                for st in range(nst):
                    l1_v(gs, 1, st, 1 + st)
            elif not os.environ.get("SKIP_Z"):
                for ci in range(4):
                    l1_z_chunk(gs, ci, (gi - 3) * 4 + ci, slices1)
        if dbg < 6:
            continue
        units = [(st, kv) for st in range(nst) for kv in range(2)]
        prev = None
        for ui, (st, kv) in enumerate(units):
            fb = (T0 == HALO and st == 0)
            slots = attn_scores(st, kv, fb)
            if prev is not None:
                pst, pkv, pslots = prev
                if dbg >= 7:
                    attn_pv(pst, pkv, pslots, pst % 2)
                if pkv == 1 and dbg >= 8:
                    attn_finish(pst, pst % 2, T0 - HALO + pst * 128, True)
            prev = (st, kv, slots)
        pst, pkv, pslots = prev
        if dbg >= 7:
            attn_pv(pst, pkv, pslots, pst % 2)
        if dbg >= 8:
            attn_finish(pst, pst % 2, T0 - HALO + pst * 128, True)
        cp("pool", kT[:, 0:128], kT[:, 1024:1152], [("kT", 8)], [("kT", 0)])
        cp("pool", vaug[:, 0, :, :], vaug[:, 8, :, :], [("vaug", 8)], [("vaug", 0)])

    P.emit()
    return nc, P


def _blk(w, cols):
    sub = w[:, cols]
    return np.ascontiguousarray(sub.reshape(8, 128, len(cols)).transpose(1, 0, 2).reshape(128, 8 * len(cols)))


def prepare(inputs):
    f = np.float32
    x = np.asarray(inputs["x"], f)
    pos = np.asarray(inputs["positions"]).astype(np.int32)
    norm_g = np.asarray(inputs["norm_g"], f)
    w_in_even = np.asarray(inputs["w_in_even"], f)[0]
    w_pool = np.asarray(inputs["w_pool"], f)[0]
    pool_scale = np.asarray(inputs["pool_scale"], f)[0]
    conv_w = np.asarray(inputs["conv_w"], f)[0]
    w_out_even = np.asarray(inputs["w_out_even"], f)[0]
    w_in_odd = np.asarray(inputs["w_in_odd"], f)[0]
    b_in_odd = np.asarray(inputs["b_in_odd"], f)[0]
    sinks = np.asarray(inputs["attn_sinks"], f)[0]
    w_out_odd = np.asarray(inputs["w_out_odd"], f)[0]
    b_out_odd = np.asarray(inputs["b_out_odd"], f)[0]
    fg = np.asarray(inputs["final_norm_g"], f)

    ar = np.arange(128)
    cols = []
    for g in range(4):
        cols += [(2 * g) * 128 + ar, (2 * g + 1) * 128 + ar, 4096 + (2 * g) * 128 + ar, 4096 + (2 * g + 1) * 128 + ar]
    for j in range(8):
        cols += [2048 + j * 128 + ar, 3072 + j * 128 + ar, 1024 + j * 128 + ar, 4096 + (8 + j) * 128 + ar]
    wie = np.stack([_blk(w_in_even, c) for c in cols])
    wpl = np.ascontiguousarray(w_pool.reshape(4, 2, 128, 256).transpose(2, 0, 1, 3).reshape(128, 2048))
    wpl = np.ascontiguousarray(wpl.reshape(128, 2, 1024).transpose(1, 0, 2))
    woe = np.ascontiguousarray(w_out_even.reshape(16, 128, 1024))
    perm = np.concatenate([np.concatenate([c * 64 + np.arange(64), (8 + c) * 64 + np.arange(64)]) for c in range(8)])
    ocols = [perm[c * 128:(c + 1) * 128] for c in range(8)]
    ocols += [1024 + ar]
    ocols += [1280 + perm[c * 128:(c + 1) * 128] for c in range(8)]
    ocols += [1152 + ar]
    wio = np.stack([_blk(w_in_odd, c) for c in ocols])
    woo = np.ascontiguousarray(w_out_odd[perm].reshape(8, 128, 1024))
    cst = np.zeros((128, NCST), f)
    cst[:, C_G:C_G + 16] = norm_g.reshape(2, 8, 128).transpose(2, 0, 1).reshape(128, 16)
    cst[:, C_PS:C_PS + 8] = pool_scale.reshape(8, 128).T
    cst[:, C_CW:C_CW + 24] = conv_w.reshape(3, 8, 128).transpose(2, 0, 1).reshape(128, 24)
    for i in range(17):
        cst[:, C_B + i] = b_in_odd[ocols[i]]
    d = ar % 64
    inv_freq = (np.float32(500000.0) ** (-np.arange(0, 16, 2, dtype=np.float32) / np.float32(16))).astype(f)
    cst[:, C_INVF] = np.where(d < 16, inv_freq[d % 8], 0.0)
    cst[:, C_SGN] = np.where(d < 8, -1.0, np.where(d < 16, 1.0, 0.0))
    cst[:, C_MHALF] = -0.5
    rowc = np.zeros((128, NROW), f)
    rowc[:, R_BOUT:R_BOUT + 1024] = b_out_odd[None]
    rowc[:, R_FG:R_FG + 1024] = fg[None]
    rowc[:, R_BV:R_BV + 128] = b_in_odd[1152:1280][None]
    sl = np.zeros(16, f)
    for c in range(8):
        for k in range(2):
            sl[2 * c + k] = sinks[c + 8 * k]
    rowc[:, R_SINK:R_SINK + 16] = sl[None]
    pm = np.zeros((128, 1024), f)
    for m in range(128):
        dd = m % 64
        if dd < 8:
            pm[m + 8, m] = 1.0
        elif dd < 16:
            pm[m - 8, m] = 1.0
    s_ = np.arange(128)[:, None]
    q_ = np.arange(128)[None, :]
    m_prev = np.where(q_ < s_, 0.0, NEG).astype(f)
    m_diag = np.where(q_ >= s_, 0.0, NEG).astype(f)
    m_all = np.full((128, 128), NEG, f)
    in_maps = []
    for c in range(NCORES):
        b, half = c // 2, c % 2
        t0 = half * TOK
        xe = np.zeros((TL, DM), f)
        pe = np.zeros((TL,), np.int32)
        if half == 0:
            xe[HALO:] = x[b, 0:TOK]
            pe[HALO:] = pos[b, 0:TOK]
        else:
            xe[:] = x[b, t0 - HALO:t0 + TOK]
            pe[:] = pos[b, t0 - HALO:t0 + TOK]
        pc = np.ones((4, 16), f)
        if half == 0:
            for g in range(4):
                w = 2 << g
                pc[g] = w / np.minimum(np.arange(16) + 1, w)
        masks = np.zeros((2, 128, 1024), f)
        masks[0, :, 0:512] = np.tile(m_prev, (1, 4))
        masks[0, :, 512:1024] = np.tile(m_diag, (1, 4))
        masks[1, :, 0:512] = np.tile(m_all if half == 0 else m_prev, (1, 4))
        in_maps.append({
            "x_ext": xe, "pos_bc": np.ascontiguousarray(np.broadcast_to(pe[None], (128, TL))),
            "wie": wie, "wpl": wpl, "woe": woe, "wio": wio, "woo": woo, "cst": cst, "rowc": rowc,
            "pcorr": np.ascontiguousarray(np.broadcast_to(pc.reshape(1, 64), (128, 64))),
            "masks": masks, "pmat": pm,
        })
    return in_maps


_CACHE = {}


def kernel(**inputs):
    mode = "fused"
    if mode not in _CACHE:
        _CACHE[mode] = build(mode)[0]
    nc = _CACHE[mode]
    in_maps = prepare(inputs)
    res = run_bass_kernel_spmd(nc, in_maps, core_ids=list(range(NCORES)))
    out = np.empty((4, 8192, DM), np.float32)
    for c in range(NCORES):
        b, half = c // 2, c % 2
        out[b, half * TOK:(half + 1) * TOK] = res.results[c]["out"]
    return out
```

```python
from contextlib import ExitStack
import numpy as np
import concourse.bass as bass
import concourse.mybir as mybir
from concourse.bass_utils import run_bass_kernel_spmd

F32 = mybir.dt.float32
BF16 = mybir.dt.bfloat16
I32 = mybir.dt.int32
AF = mybir.ActivationFunctionType
ALU = mybir.AluOpType
AX = mybir.AxisListType


class _I:
    __slots__ = ("eng", "fn", "kind", "deps", "signal", "sig", "idx", "rawdeps", "line", "rw")


class Prog:
    NDMA = 30
    SAME_ENGINE_RAW = True

    def __init__(self, nc):
        self.nc = nc
        self.stack = ExitStack()
        self.instrs = []
        self.lw = {}
        self.rd = {}
        self.trace = None

    def sbuf(self, name, shape, dtype):
        return self.stack.enter_context(self.nc.sbuf_tensor(name, shape, dtype))

    def psum(self, name, shape, dtype):
        return self.stack.enter_context(self.nc.psum_tensor(name, shape, dtype))

    def _add(self, eng, fn, reads, writes, kind):
        ins = _I()
        ins.eng, ins.fn, ins.kind = eng, fn, kind
        ins.idx = len(self.instrs)
        import sys as _sys
        f = _sys._getframe(2)
        ln = []
        while f is not None and len(ln) < 4:
            ln.append(str(f.f_lineno))
            f = f.f_back
        ins.line = "<".join(ln)
        ins.rw = (reads, writes)
        ins.signal = False
        ins.sig = None
        deps = {}
        for k in reads:
            w = self.lw.get(k)
            if w is not None:
                deps[w.idx] = (w, True)
        for k in writes:
            w = self.lw.get(k)
            if w is not None and w.idx not in deps:
                deps[w.idx] = (w, False)
            for r in self.rd.get(k, ()):
                if r.idx not in deps:
                    deps[r.idx] = (r, False)
        out = []
        for d, raw in deps.values():
            if d.kind == "op" and d.eng == eng and kind == "op":
                if eng == "tensor" or not self.SAME_ENGINE_RAW:
                    continue
            out.append(d)
        ins.deps = out
        for k in reads:
            lst = self.rd.setdefault(k, [])
            if kind == "op":
                lst[:] = [r for r in lst if not (r.kind == "op" and r.eng == eng)]
            lst.append(ins)
        for k in writes:
            self.lw[k] = ins
            self.rd[k] = []
        self.instrs.append(ins)
        return ins

    def op(self, eng, fn, reads=(), writes=()):
        return self._add(eng, fn, tuple(reads), tuple(writes), "op")

    def dma(self, eng, out, in_, reads=(), writes=(), is_output=False):
        return self._add(eng, lambda e: e.dma_start(out=out, in_=in_), tuple(reads), tuple(writes), "dma")

    def emit(self):
        nc = self.nc
        st = self.stack
        engs = ["tensor", "vector", "scalar", "pool", "sp"]
        esem = {e: st.enter_context(nc.semaphore("sem_" + e)) for e in engs}
        dsem = [st.enter_context(nc.semaphore("dsem%d" % i)) for i in range(self.NDMA)]
        for ins in self.instrs:
            for d in ins.deps:
                d.signal = True
        cnt = {e: 0 for e in engs}
        dcnt = [0] * self.NDMA
        dlast = [None] * self.NDMA
        pools = {"sp": list(range(0, self.NDMA - 14)), "pool": list(range(self.NDMA - 14, self.NDMA - 8)),
                 "scalar": list(range(self.NDMA - 8, self.NDMA))}
        nd = {"sp": 0, "pool": 0, "scalar": 0}
        for ins in self.instrs:
            if ins.kind == "dma":
                pl = pools[ins.eng]
                s = pl[nd[ins.eng] % len(pl)]
                nd[ins.eng] += 1
                if dlast[s] is not None:
                    ins.deps.append(dlast[s])
                dcnt[s] += 16
                ins.sig = (dsem[s], dcnt[s])
                dlast[s] = ins
                ins.signal = True
            elif ins.signal:
                cnt[ins.eng] += 1
                ins.sig = (esem[ins.eng], cnt[ins.eng])
        per = {e: [i for i in self.instrs if i.eng == e] for e in engs}
        self.stats = {e: len(per[e]) for e in engs}
        self.stats["signals"] = dict(cnt)
        block = st.enter_context(nc.Block())

        def run(e, lst, final):
            waited = {}
            for ins in lst:
                if self.trace is not None:
                    self.trace.append((ins.eng, ins.idx, ins.kind, ins.line, [(d.eng, d.idx, d.sig[1]) for d in ins.deps], ins.sig[1] if ins.sig else None, ins.rw))
                need = {}
                for d in ins.deps:
                    sem, val = d.sig
                    key = id(sem)
                    if waited.get(key, 0) >= val:
                        continue
                    if key not in need or need[key][1] < val:
                        need[key] = (sem, val)
                for key, (sem, val) in need.items():
                    e.wait_ge(sem, val)
                    waited[key] = val
                r = ins.fn(e)
                if ins.signal:
                    sem, val = ins.sig
                    r.then_inc(sem, 16 if ins.kind == "dma" else 1)
            if final:
                for s in range(self.NDMA):
                    if dcnt[s] and waited.get(id(dsem[s]), 0) < dcnt[s]:
                        e.wait_ge(dsem[s], dcnt[s])

        @block.tensor
        def _(e):
            run(e, per["tensor"], False)

        @block.vector
        def _(e):
            run(e, per["vector"], False)

        @block.scalar
        def _(e):
            run(e, per["scalar"], True)

        @block.gpsimd
        def _(e):
            run(e, per["pool"], True)

        @block.sync
        def _(e):
            run(e, per["sp"], True)

        st.close()


NCORES = 8
DM = 1024
TOK = 4096
HALO = 256
TL = TOK + HALO
EPS = 1e-5
NEG = -30000.0
TWO_PI = 6.283185307179586
C1 = 6.28125
C2 = TWO_PI - C1
PI_SAFE = 3.1415925
TILES = [(0, 256, True)] + [(HALO + 1024 * i, 1024, False) for i in range(4)]

C_G = 0
C_PS = 16
C_CW = 24
C_B = 48
C_INVF = 65
C_SGN = 66
C_MHALF = 67
NCST = 68
R_BOUT = 0
R_FG = 1024
R_BV = 2048
R_SINK = 2176
NROW = 2192


def build(mode="fused", ntiles=5, dbg=99):
    nc = bass.Bass("TRN2", target_bir_lowering=False)
    P = Prog(nc)

    def din(name, shape, dt=F32):
        return nc.dram_tensor(name, shape, dt, kind="ExternalInput").ap()

    x_ext = din("x_ext", [TL, DM])
    pos_bc = din("pos_bc", [128, TL], I32)
    wie = din("wie", [48, 128, 1024])
    wpl = din("wpl", [2, 128, 1024])
    woe = din("woe", [16, 128, 1024])
    wio = din("wio", [18, 128, 1024])
    woo = din("woo", [8, 128, 1024])
    cst_d = din("cst", [128, NCST])
    rowc_d = din("rowc", [128, NROW])
    pcorr_d = din("pcorr", [128, 64])
    masks_d = din("masks", [2, 128, 1024])
    pmat_d = din("pmat", [128, 1024])
    out_d = nc.dram_tensor("out", [TOK, DM], F32, kind="ExternalOutput").ap()
    wscr = nc.dram_tensor("wscr", [90, 128, 1024], BF16, kind="Internal").ap()
    PID = {"wie": 0, "woe": 48, "wio": 64, "woo": 82}

    NSTG = 3
    h = P.sbuf("h", [128, 8, 1024], F32)
    ybuf = P.sbuf("ybuf", [128, 2, 1024], BF16)
    yT = P.sbuf("yT", [128, 8, 1024], BF16)
    big = P.sbuf("big", [128, 16, 1024], BF16)
    stg = P.sbuf("stg", [128, NSTG, 1024], F32)
    wg = P.sbuf("wg", [128, 2, 4, 1024], BF16)
    wout = P.sbuf("wout", [128, 16, 1024], BF16)
    wpool = P.sbuf("wpool", [128, 2, 1024], BF16)
    cst = P.sbuf("cstt", [128, NCST], F32)
    rowc = P.sbuf("rowct", [128, NROW], F32)
    pcorr = P.sbuf("pcorrt", [128, 64], F32)
    maskb = P.sbuf("maskb", [128, 3, 512], BF16)
    identf = P.sbuf("identf", [128, 128], F32)
    identb = P.sbuf("identb", [128, 128], BF16)
    esink = P.sbuf("esink", [128, 16], F32)
    ucar = P.sbuf("ucar", [128, 8, 16], F32)
    ccar = P.sbuf("ccar", [128, 8, 2], F32)
    NS = 8
    S = P.sbuf("S", [128, NS, 528], F32)
    pooled = P.sbuf("pooled", [128, 2, 512], BF16)
    xbq = P.sbuf("xbq", [128, 2, 512], BF16)
    kT = P.sbuf("kT", [128, 128 + 1024], BF16)
    vaug = P.sbuf("vaug", [128, 9, 2, 65], BF16)
    cosT = P.sbuf("cosT", [128, 1024], F32)
    sinT = P.sbuf("sinT", [128, 1024], F32)
    NPT = 8
    PT = P.sbuf("PT", [128, NPT, 512], BF16)
    attn = P.sbuf("attn", [128, 2, 1024], BF16)
    ss = P.sbuf("ss", [128, 32], F32)
    rstd = P.sbuf("rstd", [128, 32], F32)
    small = P.sbuf("small", [128, 16], F32)
    ps = [P.psum("ps%d" % i, [128, 512], F32) for i in range(8)]
    junk = xbq[:].rearrange("p a b -> p (a b)")
    JK = [("xbq", 0), ("xbq", 1)]

    state = {"bank": 0, "stg": 0, "S": 0, "pt": 0, "ssi": 0, "xs": 0, "ysl": 0}

    def nbank():
        b = state["bank"] % 8
        state["bank"] += 1
        return b

    def nS():
        s = state["S"] % NS
        state["S"] += 1
        return s

    def col(i):
        return cst[:, i:i + 1]

    def mm(out, lhsT, rhs, start, stop, reads, writes):
        P.op("tensor", lambda e: e.matmul(out=out, lhsT=lhsT, rhs=rhs, start=start, stop=stop),
             reads, writes)

    def tr(out, in_, reads, writes):
        P.op("tensor", lambda e: e.transpose(out=out, in_=in_, identity=identb[:]), list(reads) + ["ident"], writes)

    def act(out, in_, func, reads, writes, scale=None, bias=None, accum=None):
        kw = {}
        if scale is not None:
            kw["scale"] = scale
        if bias is not None:
            kw["bias"] = bias
        if accum is not None:
            kw["accum_out"] = accum
        P.op("scalar", lambda e: e.activation(out=out, in_=in_, func=func, **kw), reads, writes)

    def tt(eng, out, in0, in1, op, reads, writes):
        P.op(eng, lambda e: e.tensor_tensor(out=out, in0=in0, in1=in1, op=op), reads, writes)

    def stt(out, in0, scalar, in1, op0, op1, reads, writes):
        P.op("vector", lambda e: e.scalar_tensor_tensor(out=out, in0=in0, scalar=scalar, in1=in1, op0=op0, op1=op1),
             reads, writes)

    def ts(eng, out, in0, s1, op0, reads, writes, s2=None, op1=None):
        if op1 is None:
            P.op(eng, lambda e: e.tensor_scalar(out=out, in0=in0, scalar1=s1, scalar2=None, op0=op0), reads, writes)
        else:
            P.op(eng, lambda e: e.tensor_scalar(out=out, in0=in0, scalar1=s1, scalar2=s2, op0=op0, op1=op1), reads, writes)

    def cp(eng, out, in_, reads, writes):
        if eng == "scalar":
            P.op(eng, lambda e: e.copy(out=out, in_=in_), reads, writes)
        else:
            P.op(eng, lambda e: e.tensor_copy(out=out, in_=in_), reads, writes)

    cast_rr = {"i": 0}
    CAST_ENGS = ["pool", "pool", "scalar"]

    cached = set()

    def stage_cast(dram_ap, dst_ap, dst_keys, ncols=1024, eng=None, pid=None):
        if pid is not None and pid in cached:
            P.dma("sp", dst_ap, wscr[pid], reads=[("scr", pid)], writes=dst_keys)
            return
        stage_cast_(dram_ap, dst_ap, dst_keys, ncols, eng)
        if pid is not None:
            P.dma("pool", wscr[pid], dst_ap, reads=dst_keys, writes=[("scr", pid)])
            cached.add(pid)

    def stage_cast_(dram_ap, dst_ap, dst_keys, ncols=1024, eng=None):
        s = state["stg"] % NSTG
        state["stg"] += 1
        P.dma("sp", stg[:, s, 0:ncols], dram_ap, reads=[], writes=[("stg", s)])
        if eng is None:
            eng = CAST_ENGS[cast_rr["i"] % len(CAST_ENGS)]
            cast_rr["i"] += 1
        cp(eng, dst_ap, stg[:, s, 0:ncols], [("stg", s)], dst_keys)

    P.dma("sp", cst[:], cst_d, writes=["cst"])
    P.dma("sp", rowc[:], rowc_d, writes=["rowc"])
    P.dma("sp", pcorr[:], pcorr_d, writes=["pcorr"])
    P.op("pool", lambda e: e.memset(identf[:], 0.0), writes=["identf"])
    P.op("pool", lambda e: e.affine_select(out=identf[:], in_=identf[:], pattern=[[-1, 128]],
                                           compare_op=ALU.not_equal, fill=1.0, base=0, channel_multiplier=1),
         reads=["identf"], writes=["identf"])
    cp("vector", identb[:], identf[:], ["identf"], ["ident"])
    P.op("pool", lambda e: e.memset(ucar[:], 0.0), writes=["ucar%d" % i for i in range(8)])
    P.op("pool", lambda e: e.memset(ccar[:], 0.0), writes=["ccar%d" % i for i in range(8)])
    P.op("vector", lambda e: e.memset(vaug[:, :, :, 64:65], 1.0), writes=[("vaug", i) for i in range(9)])
    for i in range(2):
        stage_cast(wpl[i], wpool[:, i, :], [("wpool", i)], eng="vector")
    stage_cast(masks_d[0], maskb[:, 0:2, :].rearrange("p a b -> p (a b)"), ["mask"], eng="vector")
    stage_cast(masks_d[1][:, 0:512], maskb[:, 2, :], ["mask2"], ncols=512, eng="vector")
    P.op("vector", lambda e: e.memset(S[:], 0.0), writes=[("S", i) for i in range(NS)])
    act(esink[:], rowc[:, R_SINK:R_SINK + 16], AF.Exp, ["rowc"], ["esink"])

    def load_x(t0, nst):
        for st in range(nst):
            P.dma("sp", h[:, st, :], x_ext[t0 + st * 128: t0 + (st + 1) * 128, :], writes=[("h", st)])

    def norm_T(layer, sts):
        base = state["ssi"]
        state["ssi"] = (state["ssi"] + 8) % 24
        for st in sts:
            act(ybuf[:, st % 2, :], h[:, st, :], AF.Square, [("h", st)], [("ybuf", st % 2), ("ss", base + st)],
                accum=ss[:, base + st: base + st + 1])
        lo, hi = base + sts[0], base + sts[-1] + 1
        keys_ss = [("ss", base + st) for st in sts]
        keys_r = [("rstd", base + st) for st in sts]
        ts("pool", rstd[:, lo:hi], ss[:, lo:hi], 1.0 / DM, ALU.mult, keys_ss, keys_r, s2=EPS, op1=ALU.add)
        tt("pool", rstd[:, lo:hi], rstd[:, lo:hi], cst[:, C_MHALF:C_MHALF + 1].to_broadcast([128, hi - lo]),
           ALU.pow, keys_r + ["cst"], keys_r)
        for st in sts:
            sl = st % 2
            act(ybuf[:, sl, :], h[:, st, :], AF.Copy, [("h", st), ("rstd", base + st)], [("ybuf", sl)],
                scale=rstd[:, base + st: base + st + 1])
            b = nbank()
            pb = ps[b][:].bitcast(BF16)
            for kc in range(8):
                tr(pb[:, kc * 128:(kc + 1) * 128], ybuf[:, sl, kc * 128:(kc + 1) * 128], [("ybuf", sl)], [("ps", b)])
            tt("vector", yT[:, :, st * 128:(st + 1) * 128], pb.rearrange("p (k t) -> p k t", k=8),
               cst[:, C_G + layer * 8: C_G + layer * 8 + 8].unsqueeze(2).to_broadcast([128, 8, 128]), ALU.mult,
               [("ps", b), "cst"], [("yT", st)])

    def norm_st(layer, st):
        i = state["ssi"]
        state["ssi"] = (state["ssi"] + 1) % 24
        sl = state["ysl"] % 2
        state["ysl"] += 1
        act(ybuf[:, sl, :], h[:, st, :], AF.Square, [("h", st)], [("ybuf", sl), ("ss", i)], accum=ss[:, i:i + 1])
        ts("pool", rstd[:, i:i + 1], ss[:, i:i + 1], 1.0 / DM, ALU.mult, [("ss", i)], [("rstd", i)], s2=EPS, op1=ALU.add)
        tt("pool", rstd[:, i:i + 1], rstd[:, i:i + 1], cst[:, C_MHALF:C_MHALF + 1], ALU.pow, [("rstd", i), "cst"], [("rstd", i)])
        act(ybuf[:, sl, :], h[:, st, :], AF.Copy, [("h", st), ("rstd", i)], [("ybuf", sl)], scale=rstd[:, i:i + 1])
        b = nbank()
        pb = ps[b][:].bitcast(BF16)
        for kc in range(8):
            tr(pb[:, kc * 128:(kc + 1) * 128], ybuf[:, sl, kc * 128:(kc + 1) * 128], [("ybuf", sl)], [("ps", b)])
        tt("vector", yT[:, :, st * 128:(st + 1) * 128], pb.rearrange("p (k t) -> p k t", k=8),
           cst[:, C_G + layer * 8: C_G + layer * 8 + 8].unsqueeze(2).to_broadcast([128, 8, 128]), ALU.mult,
           [("ps", b), "cst"], [("yT", st)])

    def load_group(src, chunk_ids, gs, base):
        for ci, ch in enumerate(chunk_ids):
            stage_cast(src[ch], wg[:, gs, ci, :], [("wg", gs, ci)], pid=base + ch)

    def proj(gs, ci, t0, n, sts):
        b = nbank()
        w = wg[:, gs, ci, :].rearrange("p (k n) -> p k n", k=8)
        for kc in range(8):
            mm(ps[b][:, 0:n], w[:, kc, :], yT[:, kc, t0:t0 + n], kc == 0, kc == 7,
               [("wg", gs, ci)] + [("yT", st) for st in sts], [("ps", b)])
        return b

    def l0_group_a(g, gs, slices, first_real):
        w = 2 << g
        for sl, (t0, n, sts) in enumerate(slices):
            bu = [proj(gs, 0, t0, n, sts), proj(gs, 1, t0, n, sts)]
            bz = [proj(gs, 2, t0, n, sts), proj(gs, 3, t0, n, sts)]
            zs = []
            for i in range(2):
                c = 2 * g + i
                ub, ta, tb = nS(), nS(), nS()
                kc_ = "ucar%d" % c
                cp("pool", S[:, ub, 0:16], ucar[:, c, :], [kc_], [("S", ub)])
                cp("scalar", S[:, ub, 16:16 + n], ps[bu[i]][:, 0:n], [("ps", bu[i])], [("S", ub)])
                cp("pool", ucar[:, c, :], S[:, ub, n:n + 16], [("S", ub)], [kc_])
                src = ub
                lvl = 1
                dst = ta
                while (1 << lvl) <= w:
                    sh = 1 << (lvl - 1)
                    lo = 16 - (16 - (1 << lvl)) if (1 << lvl) < 16 else 16
                    lo = (1 << lvl)
                    tt("vector", S[:, dst, lo:16 + n], S[:, src, lo:16 + n], S[:, src, lo - sh:16 + n - sh], ALU.add,
                       [("S", src)], [("S", dst)])
                    src = dst
                    dst = tb if dst == ta else ta
                    lvl += 1
                if first_real and sl == 0:
                    tt("vector", S[:, src, 16:32], S[:, src, 16:32], pcorr[:, g * 16:(g + 1) * 16], ALU.mult,
                       [("S", src), "pcorr"], [("S", src)])
                stt(pooled[:, i, 0:n], S[:, src, 16:16 + n], 1.0 / w, S[:, ub, 16:16 + n], ALU.mult, ALU.subtract,
                    [("S", src), ("S", ub)], [("pooled", i)])
                z = nS()
                act(S[:, z, 0:n], ps[bz[i]][:, 0:n], AF.Silu, [("ps", bz[i])], [("S", z)])
                zs.append(z)
            for oc in range(2):
                c = 2 * g + oc
                b = nbank()
                for kc in range(2):
                    o = (g % 2) * 512 + kc * 256 + oc * 128
                    mm(ps[b][:, 0:n], wpool[:, g // 2, o:o + 128], pooled[:, kc, 0:n], kc == 0, kc == 1,
                       [("wpool", g // 2), ("pooled", 0), ("pooled", 1)], [("ps", b)])
                stt(big[:, c, t0:t0 + n], ps[b][:, 0:n], col(C_PS + c), S[:, zs[oc], 0:n], ALU.mult, ALU.mult,
                    [("ps", b), "cst", ("S", zs[oc])], [("big", c, t0 // 512)])

    def l0_group_b(j, gs, slices):
        for sl, (t0, n, sts) in enumerate(slices):
            bgc = proj(gs, 0, t0, n, sts)
            bhc = proj(gs, 1, t0, n, sts)
            bgb = proj(gs, 2, t0, n, sts)
            bzb = proj(gs, 3, t0, n, sts)
            hc, cu, v0, v1, zz, gz = nS(), nS(), nS(), nS(), nS(), nS()
            kc_ = "ccar%d" % j
            cp("scalar", S[:, hc, 0:n], ps[bhc][:, 0:n], [("ps", bhc)], [("S", hc)])
            cp("pool", S[:, cu, 0:2], ccar[:, j, :], [kc_], [("S", cu)])
            tt("vector", S[:, cu, 2:2 + n], ps[bgc][:, 0:n], S[:, hc, 0:n], ALU.mult, [("ps", bgc), ("S", hc)], [("S", cu)])
            cp("pool", ccar[:, j, :], S[:, cu, n:n + 2], [("S", cu)], [kc_])
            act(S[:, v0, 0:n], S[:, cu, 2:2 + n], AF.Copy, [("S", cu), "cst"], [("S", v0)], scale=col(C_CW + 16 + j))
            stt(S[:, v1, 0:n], S[:, cu, 1:1 + n], col(C_CW + 8 + j), S[:, v0, 0:n], ALU.mult, ALU.add,
                [("S", cu), ("S", v0), "cst"], [("S", v1)])
            stt(S[:, v0, 0:n], S[:, cu, 0:n], col(C_CW + j), S[:, v1, 0:n], ALU.mult, ALU.add,
                [("S", cu), ("S", v1), "cst"], [("S", v0)])
            act(S[:, zz, 0:n], ps[bzb][:, 0:n], AF.Silu, [("ps", bzb)], [("S", zz)])
            tt("vector", S[:, gz, 0:n], ps[bgb][:, 0:n], S[:, zz, 0:n], ALU.mult, [("ps", bgb), ("S", zz)], [("S", gz)])
            tt("pool", big[:, 8 + j, t0:t0 + n], S[:, gz, 0:n], S[:, v0, 0:n], ALU.mult, [("S", gz), ("S", v0)],
               [("big", 8 + j, t0 // 512)])

    def out_proj(st, nkc, coff):
        for nh in range(2):
            b = nbank()
            for kc in range(nkc):
                mm(ps[b][:, 0:512], big[:, coff + kc, st * 128:(st + 1) * 128], wout[:, kc, nh * 512:(nh + 1) * 512],
                   kc == 0, kc == nkc - 1, [("big", coff + kc, st // 4), ("wout", kc)], [("ps", b)])
            tt("vector", h[:, st, nh * 512:(nh + 1) * 512], ps[b][:, 0:512], h[:, st, nh * 512:(nh + 1) * 512], ALU.add,
               [("ps", b), ("h", st)], [("h", st)])

    def rope_tables(t0, n):
        a, k_, r, m = nS(), nS(), nS(), nS()
        posi = S[:, a, :].bitcast(I32)
        for hh in range(0, n, 512):
            nn = min(512, n - hh)
            ki = S[:, k_, 0:nn].bitcast(I32)
            P.dma("sp", posi[:, 0:nn], pos_bc[:, t0 + hh:t0 + hh + nn], writes=[("S", a)])
            ts("vector", S[:, r, 0:nn], posi[:, 0:nn], col(C_INVF), ALU.mult, [("S", a), "cst"], [("S", r)])
            ts("vector", ki, S[:, r, 0:nn], 1.0 / TWO_PI, ALU.mult, [("S", r)], [("S", k_)])
            stt(S[:, m, 0:nn], ki, -C1, S[:, r, 0:nn], ALU.mult, ALU.add, [("S", k_), ("S", r)], [("S", m)])
            stt(S[:, r, 0:nn], ki, -C2, S[:, m, 0:nn], ALU.mult, ALU.add, [("S", k_), ("S", m)], [("S", r)])

            def wrap(buf, tmp):
                ts("vector", S[:, tmp, 0:nn], S[:, buf, 0:nn], PI_SAFE, ALU.is_gt, [("S", buf)], [("S", tmp)], s2=-TWO_PI, op1=ALU.mult)
                tt("vector", S[:, buf, 0:nn], S[:, buf, 0:nn], S[:, tmp, 0:nn], ALU.add, [("S", buf), ("S", tmp)], [("S", buf)])
                ts("vector", S[:, tmp, 0:nn], S[:, buf, 0:nn], -PI_SAFE, ALU.is_lt, [("S", buf)], [("S", tmp)], s2=TWO_PI, op1=ALU.mult)
                tt("vector", S[:, buf, 0:nn], S[:, buf, 0:nn], S[:, tmp, 0:nn], ALU.add, [("S", buf), ("S", tmp)], [("S", buf)])
                ts("vector", S[:, buf, 0:nn], S[:, buf, 0:nn], PI_SAFE, ALU.min, [("S", buf)], [("S", buf)], s2=-PI_SAFE, op1=ALU.max)

            wrap(r, m)
            act(sinT[:, hh:hh + nn], S[:, r, 0:nn], AF.Sin, [("S", r), "cst"], [("sinT", hh // 512)], scale=col(C_SGN))
            ts("vector", S[:, r, 0:nn], S[:, r, 0:nn], 1.5707963267948966, ALU.add, [("S", r)], [("S", r)])
            wrap(r, m)
            act(cosT[:, hh:hh + nn], S[:, r, 0:nn], AF.Sin, [("S", r)], [("cosT", hh // 512)])

    def l1_rot_chunk(gs, ci, bias_col, dst_fn, slices):
        for sl, (t0, n, sts, tb) in enumerate(slices):
            b = proj(gs, ci, t0, n, sts)
            r, xp, t1 = nS(), nS(), nS()
            act(S[:, r, 0:n], ps[b][:, 0:n], AF.Identity, [("ps", b), "cst"], [("S", r)], bias=col(bias_col))
            for (d0, s0) in ((0, 8), (8, 0), (64, 72), (72, 64)):
                P.dma("scalar", S[d0:d0 + 8, xp, 0:n], S[s0:s0 + 8, r, 0:n], reads=[("S", r)], writes=[("S", xp)])
            stt(S[:, t1, 0:n], ps[b][:, 0:n], col(bias_col), cosT[:, tb:tb + n], ALU.add, ALU.mult,
                [("ps", b), "cst", ("cosT", tb // 512)], [("S", t1)])
            tt("pool", S[:, xp, 0:n], S[:, xp, 0:n], sinT[:, tb:tb + n], ALU.mult, [("S", xp), ("sinT", tb // 512)], [("S", xp)])
            dst, dkeys = dst_fn(t0, n)
            tt("vector", dst, S[:, t1, 0:n], S[:, xp, 0:n], ALU.add, [("S", t1), ("S", xp)], dkeys)

    def l1_z_chunk(gs, ci, c, slices):
        for sl, (t0, n, sts, tb) in enumerate(slices):
            b = proj(gs, ci, t0, n, sts)
            act(big[:, 8 + c, t0:t0 + n], ps[b][:, 0:n], AF.Silu, [("ps", b), "cst"], [("big", 8 + c, t0 // 512)],
                bias=col(C_B + 9 + c))

    def l1_v(gs, ci, st, blk):
        b = nbank()
        w = wg[:, gs, ci, :].rearrange("p (k n) -> p k n", k=8)
        for kc in range(8):
            mm(ps[b][:, 0:128], yT[:, kc, st * 128:(st + 1) * 128], w[:, kc, :], kc == 0, kc == 7,
               [("wg", gs, ci), ("yT", st)], [("ps", b)])
        tt("vector", vaug[:, blk, :, 0:64], ps[b][:, 0:128].rearrange("p (a d) -> p a d", a=2),
           rowc[:, R_BV:R_BV + 128].rearrange("p (a d) -> p a d", a=2), ALU.add, [("ps", b), "rowc"], [("vaug", blk)])

    def attn_scores(st, kv, first_block):
        slots = {}
        for kb in range(2):
            kcol = (st + kb) * 128
            for hg in range(2):
                b = nbank()
                mm(ps[b][:, 0:512], kT[kv * 64:(kv + 1) * 64, kcol:kcol + 128],
                   big[kv * 64:(kv + 1) * 64, hg * 4:(hg + 1) * 4, st * 128:(st + 1) * 128], True, False,
                   [("kT", st + kb)] + [("big", c, st // 4) for c in range(hg * 4, hg * 4 + 4)], [("ps", b)])
                mi = 1 if kb == 1 else (2 if first_block else 0)
                mm(ps[b][:, 0:512], identb[:], maskb[:, mi, :], False, True, ["ident", "mask", "mask2"], [("ps", b)])
                s = state["pt"] % NPT
                state["pt"] += 1
                act(PT[:, s, :], ps[b][:, 0:512], AF.Exp, [("ps", b)], [("PT", s)], scale=0.125)
                slots[(kb, hg)] = s
        return slots

    def attn_pv(st, kv, slots, asl):
        for hg in range(2):
            b = nbank()
            for hl in range(4):
                for kb in range(2):
                    s = slots[(kb, hg)]
                    mm(ps[b][:, hl * 128: hl * 128 + 65], PT[:, s, hl * 128:(hl + 1) * 128], vaug[:, st + kb, kv, :],
                       kb == 0, kb == 1, [("PT", s), ("vaug", st + kb)], [("ps", b)])
            pv = ps[b][:].rearrange("p (a d) -> p a d", a=4)
            es = esink[:].rearrange("p (c k) -> p c k", k=2)[:, hg * 4:(hg + 1) * 4, kv:kv + 1]
            sm = small[:, (kv * 2 + hg) * 4:(kv * 2 + hg) * 4 + 4]
            kk = ("small", kv * 2 + hg)
            tt("vector", sm.unsqueeze(2), pv[:, :, 64:65], es, ALU.add, [("ps", b), "esink"], [kk])
            P.op("vector", lambda e, sm=sm: e.reciprocal(out=sm, in_=sm), [kk], [kk])
            dst = attn[:, asl, :].rearrange("p (c k d) -> p c k d", c=8, k=2)[:, hg * 4:(hg + 1) * 4, kv, :]
            tt("vector", dst, pv[:, :, 0:64], sm.unsqueeze(2).to_broadcast([128, 4, 64]), ALU.mult,
               [("ps", b), kk], [("attn", asl, kv, hg)])

    def attn_finish(st, asl, out_row0, final):
        b = nbank()
        pb = ps[b][:].bitcast(BF16)
        akeys = [("attn", asl, kv, hg) for kv in range(2) for hg in range(2)]
        for c in range(8):
            tr(pb[:, c * 128:(c + 1) * 128], attn[:, asl, c * 128:(c + 1) * 128], akeys, [("ps", b)])
        gk = [("big", 8 + c, st // 4) for c in range(8)]
        tt("vector", big[:, 8:16, st * 128:(st + 1) * 128], pb.rearrange("p (k t) -> p k t", k=8),
           big[:, 8:16, st * 128:(st + 1) * 128], ALU.mult, [("ps", b)] + gk, gk)
        out_proj(st, 8, 8)
        if not final:
            return
        i = state["ssi"]
        state["ssi"] = (state["ssi"] + 1) % 24
        act(junk, h[:, st, :], AF.Square, [("h", st)], JK + [("ss", i)], accum=ss[:, i:i + 1])
        ts("pool", rstd[:, i:i + 1], ss[:, i:i + 1], 1.0 / DM, ALU.mult, [("ss", i)], [("rstd", i)], s2=EPS, op1=ALU.add)
        tt("pool", rstd[:, i:i + 1], rstd[:, i:i + 1], cst[:, C_MHALF:C_MHALF + 1], ALU.pow, [("rstd", i), "cst"], [("rstd", i)])
        stt(h[:, st, :], h[:, st, :], rstd[:, i:i + 1], rowc[:, R_FG:R_FG + 1024], ALU.mult, ALU.mult,
            [("h", st), ("rstd", i), "rowc"], [("h", st)])
        P.dma("pool", out_d[out_row0: out_row0 + 128, :], h[:, st, :], reads=[("h", st)], writes=[("out", out_row0)])

    A_GROUPS = [[4 * g + i for i in range(4)] for g in range(4)]
    B_GROUPS = [[16 + 4 * j + i for i in range(4)] for j in range(8)]
    L0_GROUPS = A_GROUPS + B_GROUPS
    L1_GROUPS = [[0, 1, 2, 3], [4, 5, 6, 7], [8, 17], [9, 10, 11, 12], [13, 14, 15, 16]]
    gslot = {"i": 0}

    def next_gs():
        g = gslot["i"] % 2
        gslot["i"] += 1
        return g

    first_real_done = False
    for (T0, NT, is_halo) in TILES[:ntiles]:
        nst = NT // 128
        load_x(T0, nst)
        if is_halo:
            slices0 = [(0, 256, [0, 1])]
        else:
            slices0 = [(0, 512, [0, 1, 2, 3]), (512, 512, [4, 5, 6, 7])]
        norm_T(0, list(range(nst)))
        first_real = (not is_halo) and (not first_real_done)
        pending = None
        gs_list = []
        for gi, chunks in enumerate(L0_GROUPS):
            gs = next_gs()
            load_group(wie, chunks, gs, PID["wie"])
            if gi == 1:
                for kc in range(16):
                    stage_cast(woe[kc], wout[:, kc, :], [("wout", kc)], pid=PID["woe"] + kc)
            if pending is not None:
                pg, pgs = pending
                if pg < 4:
                    l0_group_a(pg, pgs, slices0, first_real)
                else:
                    l0_group_b(pg - 4, pgs, slices0)
            pending = (gi, gs)
        l1_gs = []
        gs = next_gs()
        load_group(wio, L1_GROUPS[2] if is_halo else L1_GROUPS[0], gs, PID["wio"])
        l1_gs.append(gs)
        pg, pgs = pending
        l0_group_b(pg - 4, pgs, slices0)
        if not is_halo:
            first_real_done = True
        l0_sts = [1] if is_halo else list(range(nst))
        pipe_norm = (mode != "L0") and dbg >= 1
        for i, st in enumerate(l0_sts):
            out_proj(st, 16, 0)
            if pipe_norm and i >= 1:
                norm_st(1, l0_sts[i - 1])
        if pipe_norm:
            norm_st(1, l0_sts[-1])
        if mode == "L0":
            if not is_halo:
                for st in range(nst):
                    r0 = T0 - HALO + st * 128
                    P.dma("pool", out_d[r0:r0 + 128, :], h[:, st, :], reads=[("h", st)], writes=[("out", r0)])
            continue
        if dbg < 1:
            continue
        if is_halo:
            if dbg < 2:
                continue
            rope_tables(T0 + 128, 128)
            gs = l1_gs[0]
            if dbg < 3:
                continue
            l1_rot_chunk(gs, 0, C_B + 8, lambda t0, n: (kT[:, 0:128], [("kT", 0)]), [(128, 128, [1], 0)])
            if dbg < 4:
                continue
            l1_v(gs, 1, 1, 0)
            continue
        if dbg < 4.5:
            continue
        rope_tables(T0, NT)
        if dbg < 4.6:
            continue
        slices1 = [(0, 512, [0, 1, 2, 3], 0), (512, 512, [4, 5, 6, 7], 512)]
        import os
        if os.environ.get("SL1") == "a":
            slices1 = [(0, 128, [0], 0)]
        if os.environ.get("SL1") == "b":
            slices1 = [(0, 512, [0, 1, 2, 3], 0)]
        if os.environ.get("SL1") == "c":
            slices1 = [(0, 256, [0, 1], 0)]
        import os
        if not os.environ.get("SKIP_BOUT"):
            for st in range(nst):
                tt("pool", h[:, st, :], h[:, st, :], rowc[:, R_BOUT:R_BOUT + 1024], ALU.add, [("h", st), "rowc"], [("h", st)])
        for gi in range(5):
            if dbg < 5 and dbg < [4.7, 4.8, 4.9, 4.95, 4.97][gi]:
                break
            if gi + 1 < 5:
                gs = next_gs()
                if not os.environ.get("NOLOAD1"):
                    load_group(wio, L1_GROUPS[gi + 1], gs, PID["wio"])
                l1_gs.append(gs)
            if gi == 1:
                for kc in range(8):
                    stage_cast(woo[kc], wout[:, kc, :], [("wout", kc)], pid=PID["woo"] + kc)
            gs = l1_gs[gi]
            if gi in (0, 1):
                for ci in range(int(os.environ.get("ONECI", "4"))):
                    c = gi * 4 + ci
                    l1_rot_chunk(gs, ci, C_B + (0 if os.environ.get("BIAS0") else c),
                                 lambda t0, n, c=c: (big[:, c, t0:t0 + n], [("big", c, t0 // 512)]), slices1)
            elif gi == 2:
                l1_rot_chunk(gs, 0, C_B + 8,
                             lambda t0, n: (kT[:, 128 + t0:128 + t0 + n], [("kT", 1 + t0 // 128 + i) for i in range(n // 128)]),
                             slices1)
                for st in range(nst):
                    l1_v(gs, 1, st, 1 + st)
            elif not os.environ.get("SKIP_Z"):
                for ci in range(4):
                    l1_z_chunk(gs, ci, (gi - 3) * 4 + ci, slices1)
        if dbg < 6:
            continue
        units = [(st, kv) for st in range(nst) for kv in range(2)]
        prev = None
        for ui, (st, kv) in enumerate(units):
            fb = (T0 == HALO and st == 0)
            slots = attn_scores(st, kv, fb)
            if prev is not None:
                pst, pkv, pslots = prev
                if dbg >= 7:
                    attn_pv(pst, pkv, pslots, pst % 2)
                if pkv == 1 and dbg >= 8:
                    attn_finish(pst, pst % 2, T0 - HALO + pst * 128, True)
            prev = (st, kv, slots)
        pst, pkv, pslots = prev
        if dbg >= 7:
            attn_pv(pst, pkv, pslots, pst % 2)
        if dbg >= 8:
            attn_finish(pst, pst % 2, T0 - HALO + pst * 128, True)
        cp("pool", kT[:, 0:128], kT[:, 1024:1152], [("kT", 8)], [("kT", 0)])
        cp("pool", vaug[:, 0, :, :], vaug[:, 8, :, :], [("vaug", 8)], [("vaug", 0)])

    P.emit()
    return nc, P


def _blk(w, cols):
    sub = w[:, cols]
    return np.ascontiguousarray(sub.reshape(8, 128, len(cols)).transpose(1, 0, 2).reshape(128, 8 * len(cols)))


def prepare(inputs):
    f = np.float32
    x = np.asarray(inputs["x"], f)
    pos = np.asarray(inputs["positions"]).astype(np.int32)
    norm_g = np.asarray(inputs["norm_g"], f)
    w_in_even = np.asarray(inputs["w_in_even"], f)[0]
    w_pool = np.asarray(inputs["w_pool"], f)[0]
    pool_scale = np.asarray(inputs["pool_scale"], f)[0]
    conv_w = np.asarray(inputs["conv_w"], f)[0]
    w_out_even = np.asarray(inputs["w_out_even"], f)[0]
    w_in_odd = np.asarray(inputs["w_in_odd"], f)[0]
    b_in_odd = np.asarray(inputs["b_in_odd"], f)[0]
    sinks = np.asarray(inputs["attn_sinks"], f)[0]
    w_out_odd = np.asarray(inputs["w_out_odd"], f)[0]
    b_out_odd = np.asarray(inputs["b_out_odd"], f)[0]
    fg = np.asarray(inputs["final_norm_g"], f)

    ar = np.arange(128)
    cols = []
    for g in range(4):
        cols += [(2 * g) * 128 + ar, (2 * g + 1) * 128 + ar, 4096 + (2 * g) * 128 + ar, 4096 + (2 * g + 1) * 128 + ar]
    for j in range(8):
        cols += [2048 + j * 128 + ar, 3072 + j * 128 + ar, 1024 + j * 128 + ar, 4096 + (8 + j) * 128 + ar]
    wie = np.stack([_blk(w_in_even, c) for c in cols])
    wpl = np.ascontiguousarray(w_pool.reshape(4, 2, 128, 256).transpose(2, 0, 1, 3).reshape(128, 2048))
    wpl = np.ascontiguousarray(wpl.reshape(128, 2, 1024).transpose(1, 0, 2))
    woe = np.ascontiguousarray(w_out_even.reshape(16, 128, 1024))
    perm = np.concatenate([np.concatenate([c * 64 + np.arange(64), (8 + c) * 64 + np.arange(64)]) for c in range(8)])
    ocols = [perm[c * 128:(c + 1) * 128] for c in range(8)]
    ocols += [1024 + ar]
    ocols += [1280 + perm[c * 128:(c + 1) * 128] for c in range(8)]
    ocols += [1152 + ar]
    wio = np.stack([_blk(w_in_odd, c) for c in ocols])
    woo = np.ascontiguousarray(w_out_odd[perm].reshape(8, 128, 1024))
    cst = np.zeros((128, NCST), f)
    cst[:, C_G:C_G + 16] = norm_g.reshape(2, 8, 128).transpose(2, 0, 1).reshape(128, 16)
    cst[:, C_PS:C_PS + 8] = pool_scale.reshape(8, 128).T
    cst[:, C_CW:C_CW + 24] = conv_w.reshape(3, 8, 128).transpose(2, 0, 1).reshape(128, 24)
    for i in range(17):
        cst[:, C_B + i] = b_in_odd[ocols[i]]
    d = ar % 64
    inv_freq = (np.float32(500000.0) ** (-np.arange(0, 16, 2, dtype=np.float32) / np.float32(16))).astype(f)
    cst[:, C_INVF] = np.where(d < 16, inv_freq[d % 8], 0.0)
    cst[:, C_SGN] = np.where(d < 8, -1.0, np.where(d < 16, 1.0, 0.0))
    cst[:, C_MHALF] = -0.5
    rowc = np.zeros((128, NROW), f)
    rowc[:, R_BOUT:R_BOUT + 1024] = b_out_odd[None]
    rowc[:, R_FG:R_FG + 1024] = fg[None]
    rowc[:, R_BV:R_BV + 128] = b_in_odd[1152:1280][None]
    sl = np.zeros(16, f)
    for c in range(8):
        for k in range(2):
            sl[2 * c + k] = sinks[c + 8 * k]
    rowc[:, R_SINK:R_SINK + 16] = sl[None]
    pm = np.zeros((128, 1024), f)
    for m in range(128):
        dd = m % 64
        if dd < 8:
            pm[m + 8, m] = 1.0
        elif dd < 16:
            pm[m - 8, m] = 1.0
    s_ = np.arange(128)[:, None]
    q_ = np.arange(128)[None, :]
    m_prev = np.where(q_ < s_, 0.0, NEG).astype(f)
    m_diag = np.where(q_ >= s_, 0.0, NEG).astype(f)
    m_all = np.full((128, 128), NEG, f)
    in_maps = []
    for c in range(NCORES):
        b, half = c // 2, c % 2
        t0 = half * TOK
        xe = np.zeros((TL, DM), f)
        pe = np.zeros((TL,), np.int32)
        if half == 0:
            xe[HALO:] = x[b, 0:TOK]
            pe[HALO:] = pos[b, 0:TOK]
        else:
            xe[:] = x[b, t0 - HALO:t0 + TOK]
            pe[:] = pos[b, t0 - HALO:t0 + TOK]
        pc = np.ones((4, 16), f)
        if half == 0:
            for g in range(4):
                w = 2 << g
                pc[g] = w / np.minimum(np.arange(16) + 1, w)
        masks = np.zeros((2, 128, 1024), f)
        masks[0, :, 0:512] = np.tile(m_prev, (1, 4))
        masks[0, :, 512:1024] = np.tile(m_diag, (1, 4))
        masks[1, :, 0:512] = np.tile(m_all if half == 0 else m_prev, (1, 4))
        in_maps.append({
            "x_ext": xe, "pos_bc": np.ascontiguousarray(np.broadcast_to(pe[None], (128, TL))),
            "wie": wie, "wpl": wpl, "woe": woe, "wio": wio, "woo": woo, "cst": cst, "rowc": rowc,
            "pcorr": np.ascontiguousarray(np.broadcast_to(pc.reshape(1, 64), (128, 64))),
            "masks": masks, "pmat": pm,
        })
    return in_maps


_CACHE = {}


def kernel(**inputs):
    mode = "fused"
    if mode not in _CACHE:
        _CACHE[mode] = build(mode)[0]
    nc = _CACHE[mode]
    in_maps = prepare(inputs)
    res = run_bass_kernel_spmd(nc, in_maps, core_ids=list(range(NCORES)))
    out = np.empty((4, 8192, DM), np.float32)
    for c in range(NCORES):
        b, half = c // 2, c % 2
        out[b, half * TOK:(half + 1) * TOK] = res.results[c]["out"]
    return out
```

```python
from contextlib import ExitStack
import numpy as np
import concourse.bass as bass
import concourse.mybir as mybir
from concourse.bass_utils import run_bass_kernel_spmd

F32 = mybir.dt.float32
BF16 = mybir.dt.bfloat16
I32 = mybir.dt.int32
AF = mybir.ActivationFunctionType
ALU = mybir.AluOpType
AX = mybir.AxisListType


class _I:
    __slots__ = ("eng", "fn", "kind", "deps", "signal", "sig", "idx", "rawdeps", "line", "rw")


class Prog:
    NDMA = 30
    SAME_ENGINE_RAW = True

    def __init__(self, nc):
        self.nc = nc
        self.stack = ExitStack()
        self.instrs = []
        self.lw = {}
        self.rd = {}
        self.trace = None

    def sbuf(self, name, shape, dtype):
        return self.stack.enter_context(self.nc.sbuf_tensor(name, shape, dtype))

    def psum(self, name, shape, dtype):
        return self.stack.enter_context(self.nc.psum_tensor(name, shape, dtype))

    def _add(self, eng, fn, reads, writes, kind):
        ins = _I()
        ins.eng, ins.fn, ins.kind = eng, fn, kind
        ins.idx = len(self.instrs)
        import sys as _sys
        f = _sys._getframe(2)
        ln = []
        while f is not None and len(ln) < 4:
            ln.append(str(f.f_lineno))
            f = f.f_back
        ins.line = "<".join(ln)
        ins.rw = (reads, writes)
        ins.signal = False
        ins.sig = None
        deps = {}
        for k in reads:
            w = self.lw.get(k)
            if w is not None:
                deps[w.idx] = (w, True)
        for k in writes:
            w = self.lw.get(k)
            if w is not None and w.idx not in deps:
                deps[w.idx] = (w, False)
            for r in self.rd.get(k, ()):
                if r.idx not in deps:
                    deps[r.idx] = (r, False)
        out = []
        for d, raw in deps.values():
            if d.kind == "op" and d.eng == eng and kind == "op":
                if eng == "tensor" or not self.SAME_ENGINE_RAW:
                    continue
            out.append(d)
        ins.deps = out
        for k in reads:
            lst = self.rd.setdefault(k, [])
            if kind == "op":
                lst[:] = [r for r in lst if not (r.kind == "op" and r.eng == eng)]
            lst.append(ins)
        for k in writes:
            self.lw[k] = ins
            self.rd[k] = []
        self.instrs.append(ins)
        return ins

    def op(self, eng, fn, reads=(), writes=()):
        return self._add(eng, fn, tuple(reads), tuple(writes), "op")

    def dma(self, eng, out, in_, reads=(), writes=(), is_output=False):
        return self._add(eng, lambda e: e.dma_start(out=out, in_=in_), tuple(reads), tuple(writes), "dma")

    def emit(self):
        nc = self.nc
        st = self.stack
        engs = ["tensor", "vector", "scalar", "pool", "sp"]
        esem = {e: st.enter_context(nc.semaphore("sem_" + e)) for e in engs}
        dsem = [st.enter_context(nc.semaphore("dsem%d" % i)) for i in range(self.NDMA)]
        for ins in self.instrs:
            for d in ins.deps:
                d.signal = True
        cnt = {e: 0 for e in engs}
        dcnt = [0] * self.NDMA
        dlast = [None] * self.NDMA
        pools = {"sp": list(range(0, self.NDMA - 14)), "pool": list(range(self.NDMA - 14, self.NDMA - 8)),
                 "scalar": list(range(self.NDMA - 8, self.NDMA))}
        nd = {"sp": 0, "pool": 0, "scalar": 0}
        for ins in self.instrs:
            if ins.kind == "dma":
                pl = pools[ins.eng]
                s = pl[nd[ins.eng] % len(pl)]
                nd[ins.eng] += 1
                if dlast[s] is not None:
                    ins.deps.append(dlast[s])
                dcnt[s] += 16
                ins.sig = (dsem[s], dcnt[s])
                dlast[s] = ins
                ins.signal = True
            elif ins.signal:
                cnt[ins.eng] += 1
                ins.sig = (esem[ins.eng], cnt[ins.eng])
        per = {e: [i for i in self.instrs if i.eng == e] for e in engs}
        self.stats = {e: len(per[e]) for e in engs}
        self.stats["signals"] = dict(cnt)
        block = st.enter_context(nc.Block())

        def run(e, lst, final):
            waited = {}
            for ins in lst:
                if self.trace is not None:
                    self.trace.append((ins.eng, ins.idx, ins.kind, ins.line, [(d.eng, d.idx, d.sig[1]) for d in ins.deps], ins.sig[1] if ins.sig else None, ins.rw))
                need = {}
                for d in ins.deps:
                    sem, val = d.sig
                    key = id(sem)
                    if waited.get(key, 0) >= val:
                        continue
                    if key not in need or need[key][1] < val:
                        need[key] = (sem, val)
                for key, (sem, val) in need.items():
                    e.wait_ge(sem, val)
                    waited[key] = val
                r = ins.fn(e)
                if ins.signal:
                    sem, val = ins.sig
                    r.then_inc(sem, 16 if ins.kind == "dma" else 1)
            if final:
                for s in range(self.NDMA):
                    if dcnt[s] and waited.get(id(dsem[s]), 0) < dcnt[s]:
                        e.wait_ge(dsem[s], dcnt[s])

        @block.tensor
        def _(e):
            run(e, per["tensor"], False)

        @block.vector
        def _(e):
            run(e, per["vector"], False)

        @block.scalar
        def _(e):
            run(e, per["scalar"], True)

        @block.gpsimd
        def _(e):
            run(e, per["pool"], True)

        @block.sync
        def _(e):
            run(e, per["sp"], True)

        st.close()


NCORES = 8
DM = 1024
TOK = 4096
HALO = 256
TL = TOK + HALO
EPS = 1e-5
NEG = -30000.0
TWO_PI = 6.283185307179586
C1 = 6.28125
C2 = TWO_PI - C1
PI_SAFE = 3.1415925
TILES = [(0, 256, True)] + [(HALO + 1024 * i, 1024, False) for i in range(4)]

C_G = 0
C_PS = 16
C_CW = 24
C_B = 48
C_INVF = 65
C_SGN = 66
C_MHALF = 67
NCST = 68
R_BOUT = 0
R_FG = 1024
R_BV = 2048
R_SINK = 2176
NROW = 2192


def build(mode="fused", ntiles=5, dbg=99):
    nc = bass.Bass("TRN2", target_bir_lowering=False)
    P = Prog(nc)

    def din(name, shape, dt=F32):
        return nc.dram_tensor(name, shape, dt, kind="ExternalInput").ap()

    x_ext = din("x_ext", [TL, DM])
    pos_bc = din("pos_bc", [128, TL], I32)
    wie = din("wie", [48, 128, 1024])
    wpl = din("wpl", [2, 128, 1024])
    woe = din("woe", [16, 128, 1024])
    wio = din("wio", [18, 128, 1024])
    woo = din("woo", [8, 128, 1024])
    cst_d = din("cst", [128, NCST])
    rowc_d = din("rowc", [128, NROW])
    pcorr_d = din("pcorr", [128, 64])
    masks_d = din("masks", [2, 128, 1024])
    pmat_d = din("pmat", [128, 1024])
    out_d = nc.dram_tensor("out", [TOK, DM], F32, kind="ExternalOutput").ap()
    wscr = nc.dram_tensor("wscr", [90, 128, 1024], BF16, kind="Internal").ap()
    PID = {"wie": 0, "woe": 48, "wio": 64, "woo": 82}

    NSTG = 3
    h = P.sbuf("h", [128, 8, 1024], F32)
    ybuf = P.sbuf("ybuf", [128, 2, 1024], BF16)
    yT = P.sbuf("yT", [128, 8, 1024], BF16)
    big = P.sbuf("big", [128, 16, 1024], BF16)
    stg = P.sbuf("stg", [128, NSTG, 1024], F32)
    wg = P.sbuf("wg", [128, 2, 4, 1024], BF16)
    wout = P.sbuf("wout", [128, 16, 1024], BF16)
    wpool = P.sbuf("wpool", [128, 2, 1024], BF16)
    cst = P.sbuf("cstt", [128, NCST], F32)
    rowc = P.sbuf("rowct", [128, NROW], F32)
    pcorr = P.sbuf("pcorrt", [128, 64], F32)
    maskb = P.sbuf("maskb", [128, 3, 512], BF16)
    identf = P.sbuf("identf", [128, 128], F32)
    identb = P.sbuf("identb", [128, 128], BF16)
    esink = P.sbuf("esink", [128, 16], F32)
    ucar = P.sbuf("ucar", [128, 8, 16], F32)
    ccar = P.sbuf("ccar", [128, 8, 2], F32)
    NS = 8
    S = P.sbuf("S", [128, NS, 528], F32)
    pooled = P.sbuf("pooled", [128, 2, 512], BF16)
    xbq = P.sbuf("xbq", [128, 2, 512], BF16)
    kT = P.sbuf("kT", [128, 128 + 1024], BF16)
    vaug = P.sbuf("vaug", [128, 9, 2, 65], BF16)
    cosT = P.sbuf("cosT", [128, 1024], F32)
    sinT = P.sbuf("sinT", [128, 1024], F32)
    NPT = 8
    PT = P.sbuf("PT", [128, NPT, 512], BF16)
    attn = P.sbuf("attn", [128, 2, 1024], BF16)
    ss = P.sbuf("ss", [128, 32], F32)
    rstd = P.sbuf("rstd", [128, 32], F32)
    small = P.sbuf("small", [128, 16], F32)
    ps = [P.psum("ps%d" % i, [128, 512], F32) for i in range(8)]
    junk = xbq[:].rearrange("p a b -> p (a b)")
    JK = [("xbq", 0), ("xbq", 1)]

    state = {"bank": 0, "stg": 0, "S": 0, "pt": 0, "ssi": 0, "xs": 0, "ysl": 0}

    def nbank():
        b = state["bank"] % 8
        state["bank"] += 1
        return b

    def nS():
        s = state["S"] % NS
        state["S"] += 1
        return s

    def col(i):
        return cst[:, i:i + 1]

    def mm(out, lhsT, rhs, start, stop, reads, writes):
        P.op("tensor", lambda e: e.matmul(out=out, lhsT=lhsT, rhs=rhs, start=start, stop=stop),
             reads, writes)

    def tr(out, in_, reads, writes):
        P.op("tensor", lambda e: e.transpose(out=out, in_=in_, identity=identb[:]), list(reads) + ["ident"], writes)

    def act(out, in_, func, reads, writes, scale=None, bias=None, accum=None):
        kw = {}
        if scale is not None:
            kw["scale"] = scale
        if bias is not None:
            kw["bias"] = bias
        if accum is not None:
            kw["accum_out"] = accum
        P.op("scalar", lambda e: e.activation(out=out, in_=in_, func=func, **kw), reads, writes)

    def tt(eng, out, in0, in1, op, reads, writes):
        P.op(eng, lambda e: e.tensor_tensor(out=out, in0=in0, in1=in1, op=op), reads, writes)

    def stt(out, in0, scalar, in1, op0, op1, reads, writes):
        P.op("vector", lambda e: e.scalar_tensor_tensor(out=out, in0=in0, scalar=scalar, in1=in1, op0=op0, op1=op1),
             reads, writes)

    def ts(eng, out, in0, s1, op0, reads, writes, s2=None, op1=None):
        if op1 is None:
            P.op(eng, lambda e: e.tensor_scalar(out=out, in0=in0, scalar1=s1, scalar2=None, op0=op0), reads, writes)
        else:
            P.op(eng, lambda e: e.tensor_scalar(out=out, in0=in0, scalar1=s1, scalar2=s2, op0=op0, op1=op1), reads, writes)

    def cp(eng, out, in_, reads, writes):
        if eng == "scalar":
            P.op(eng, lambda e: e.copy(out=out, in_=in_), reads, writes)
        else:
            P.op(eng, lambda e: e.tensor_copy(out=out, in_=in_), reads, writes)

    cast_rr = {"i": 0}
    CAST_ENGS = ["pool", "pool", "scalar"]

    cached = set()

    def stage_cast(dram_ap, dst_ap, dst_keys, ncols=1024, eng=None, pid=None):
        if pid is not None and pid in cached:
            P.dma("sp", dst_ap, wscr[pid], reads=[("scr", pid)], writes=dst_keys)
            return
        stage_cast_(dram_ap, dst_ap, dst_keys, ncols, eng)
        if pid is not None:
            P.dma("pool", wscr[pid], dst_ap, reads=dst_keys, writes=[("scr", pid)])
            cached.add(pid)

    def stage_cast_(dram_ap, dst_ap, dst_keys, ncols=1024, eng=None):
        s = state["stg"] % NSTG
        state["stg"] += 1
        P.dma("sp", stg[:, s, 0:ncols], dram_ap, reads=[], writes=[("stg", s)])
        if eng is None:
            eng = CAST_ENGS[cast_rr["i"] % len(CAST_ENGS)]
            cast_rr["i"] += 1
        cp(eng, dst_ap, stg[:, s, 0:ncols], [("stg", s)], dst_keys)

    P.dma("sp", cst[:], cst_d, writes=["cst"])
    P.dma("sp", rowc[:], rowc_d, writes=["rowc"])
    P.dma("sp", pcorr[:], pcorr_d, writes=["pcorr"])
    P.op("pool", lambda e: e.memset(identf[:], 0.0), writes=["identf"])
    P.op("pool", lambda e: e.affine_select(out=identf[:], in_=identf[:], pattern=[[-1, 128]],
                                           compare_op=ALU.not_equal, fill=1.0, base=0, channel_multiplier=1),
         reads=["identf"], writes=["identf"])
    cp("vector", identb[:], identf[:], ["identf"], ["ident"])
    P.op("pool", lambda e: e.memset(ucar[:], 0.0), writes=["ucar%d" % i for i in range(8)])
    P.op("pool", lambda e: e.memset(ccar[:], 0.0), writes=["ccar%d" % i for i in range(8)])
    P.op("vector", lambda e: e.memset(vaug[:, :, :, 64:65], 1.0), writes=[("vaug", i) for i in range(9)])
    for i in range(2):
        stage_cast(wpl[i], wpool[:, i, :], [("wpool", i)], eng="vector")
    stage_cast(masks_d[0], maskb[:, 0:2, :].rearrange("p a b -> p (a b)"), ["mask"], eng="vector")
    stage_cast(masks_d[1][:, 0:512], maskb[:, 2, :], ["mask2"], ncols=512, eng="vector")
    P.op("vector", lambda e: e.memset(S[:], 0.0), writes=[("S", i) for i in range(NS)])
    act(esink[:], rowc[:, R_SINK:R_SINK + 16], AF.Exp, ["rowc"], ["esink"])

    def load_x(t0, nst):
        for st in range(nst):
            P.dma("sp", h[:, st, :], x_ext[t0 + st * 128: t0 + (st + 1) * 128, :], writes=[("h", st)])

    def norm_T(layer, sts):
        base = state["ssi"]
        state["ssi"] = (state["ssi"] + 8) % 24
        for st in sts:
            act(ybuf[:, st % 2, :], h[:, st, :], AF.Square, [("h", st)], [("ybuf", st % 2), ("ss", base + st)],
                accum=ss[:, base + st: base + st + 1])
        lo, hi = base + sts[0], base + sts[-1] + 1
        keys_ss = [("ss", base + st) for st in sts]
        keys_r = [("rstd", base + st) for st in sts]
        ts("pool", rstd[:, lo:hi], ss[:, lo:hi], 1.0 / DM, ALU.mult, keys_ss, keys_r, s2=EPS, op1=ALU.add)
        tt("pool", rstd[:, lo:hi], rstd[:, lo:hi], cst[:, C_MHALF:C_MHALF + 1].to_broadcast([128, hi - lo]),
           ALU.pow, keys_r + ["cst"], keys_r)
        for st in sts:
            sl = st % 2
            act(ybuf[:, sl, :], h[:, st, :], AF.Copy, [("h", st), ("rstd", base + st)], [("ybuf", sl)],
                scale=rstd[:, base + st: base + st + 1])
            b = nbank()
            pb = ps[b][:].bitcast(BF16)
            for kc in range(8):
                tr(pb[:, kc * 128:(kc + 1) * 128], ybuf[:, sl, kc * 128:(kc + 1) * 128], [("ybuf", sl)], [("ps", b)])
            tt("vector", yT[:, :, st * 128:(st + 1) * 128], pb.rearrange("p (k t) -> p k t", k=8),
               cst[:, C_G + layer * 8: C_G + layer * 8 + 8].unsqueeze(2).to_broadcast([128, 8, 128]), ALU.mult,
               [("ps", b), "cst"], [("yT", st)])

    def norm_st(layer, st):
        i = state["ssi"]
        state["ssi"] = (state["ssi"] + 1) % 24
        sl = state["ysl"] % 2
        state["ysl"] += 1
        act(ybuf[:, sl, :], h[:, st, :], AF.Square, [("h", st)], [("ybuf", sl), ("ss", i)], accum=ss[:, i:i + 1])
        ts("pool", rstd[:, i:i + 1], ss[:, i:i + 1], 1.0 / DM, ALU.mult, [("ss", i)], [("rstd", i)], s2=EPS, op1=ALU.add)
        tt("pool", rstd[:, i:i + 1], rstd[:, i:i + 1], cst[:, C_MHALF:C_MHALF + 1], ALU.pow, [("rstd", i), "cst"], [("rstd", i)])
        act(ybuf[:, sl, :], h[:, st, :], AF.Copy, [("h", st), ("rstd", i)], [("ybuf", sl)], scale=rstd[:, i:i + 1])
        b = nbank()
        pb = ps[b][:].bitcast(BF16)
        for kc in range(8):
            tr(pb[:, kc * 128:(kc + 1) * 128], ybuf[:, sl, kc * 128:(kc + 1) * 128], [("ybuf", sl)], [("ps", b)])
        tt("vector", yT[:, :, st * 128:(st + 1) * 128], pb.rearrange("p (k t) -> p k t", k=8),
           cst[:, C_G + layer * 8: C_G + layer * 8 + 8].unsqueeze(2).to_broadcast([128, 8, 128]), ALU.mult,
           [("ps", b), "cst"], [("yT", st)])

    def load_group(src, chunk_ids, gs, base):
        for ci, ch in enumerate(chunk_ids):
            stage_cast(src[ch], wg[:, gs, ci, :], [("wg", gs, ci)], pid=base + ch)

    def proj(gs, ci, t0, n, sts):
        b = nbank()
        w = wg[:, gs, ci, :].rearrange("p (k n) -> p k n", k=8)
        for kc in range(8):
            mm(ps[b][:, 0:n], w[:, kc, :], yT[:, kc, t0:t0 + n], kc == 0, kc == 7,
               [("wg", gs, ci)] + [("yT", st) for st in sts], [("ps", b)])
        return b

    def l0_group_a(g, gs, slices, first_real):
        w = 2 << g
        for sl, (t0, n, sts) in enumerate(slices):
            bu = [proj(gs, 0, t0, n, sts), proj(gs, 1, t0, n, sts)]
            bz = [proj(gs, 2, t0, n, sts), proj(gs, 3, t0, n, sts)]
            zs = []
            for i in range(2):
                c = 2 * g + i
                ub, ta, tb = nS(), nS(), nS()
                kc_ = "ucar%d" % c
                cp("pool", S[:, ub, 0:16], ucar[:, c, :], [kc_], [("S", ub)])
                cp("scalar", S[:, ub, 16:16 + n], ps[bu[i]][:, 0:n], [("ps", bu[i])], [("S", ub)])
                cp("pool", ucar[:, c, :], S[:, ub, n:n + 16], [("S", ub)], [kc_])
                src = ub
                lvl = 1
                dst = ta
                while (1 << lvl) <= w:
                    sh = 1 << (lvl - 1)
                    lo = 16 - (16 - (1 << lvl)) if (1 << lvl) < 16 else 16
                    lo = (1 << lvl)
                    tt("vector", S[:, dst, lo:16 + n], S[:, src, lo:16 + n], S[:, src, lo - sh:16 + n - sh], ALU.add,
                       [("S", src)], [("S", dst)])
                    src = dst
                    dst = tb if dst == ta else ta
                    lvl += 1
                if first_real and sl == 0:
                    tt("vector", S[:, src, 16:32], S[:, src, 16:32], pcorr[:, g * 16:(g + 1) * 16], ALU.mult,
                       [("S", src), "pcorr"], [("S", src)])
                stt(pooled[:, i, 0:n], S[:, src, 16:16 + n], 1.0 / w, S[:, ub, 16:16 + n], ALU.mult, ALU.subtract,
                    [("S", src), ("S", ub)], [("pooled", i)])
                z = nS()
                act(S[:, z, 0:n], ps[bz[i]][:, 0:n], AF.Silu, [("ps", bz[i])], [("S", z)])
                zs.append(z)
            for oc in range(2):
                c = 2 * g + oc
                b = nbank()
                for kc in range(2):
                    o = (g % 2) * 512 + kc * 256 + oc * 128
                    mm(ps[b][:, 0:n], wpool[:, g // 2, o:o + 128], pooled[:, kc, 0:n], kc == 0, kc == 1,
                       [("wpool", g // 2), ("pooled", 0), ("pooled", 1)], [("ps", b)])
                stt(big[:, c, t0:t0 + n], ps[b][:, 0:n], col(C_PS + c), S[:, zs[oc], 0:n], ALU.mult, ALU.mult,
                    [("ps", b), "cst", ("S", zs[oc])], [("big", c, t0 // 512)])

    def l0_group_b(j, gs, slices):
        for sl, (t0, n, sts) in enumerate(slices):
            bgc = proj(gs, 0, t0, n, sts)
            bhc = proj(gs, 1, t0, n, sts)
            bgb = proj(gs, 2, t0, n, sts)
            bzb = proj(gs, 3, t0, n, sts)
            hc, cu, v0, v1, zz, gz = nS(), nS(), nS(), nS(), nS(), nS()
            kc_ = "ccar%d" % j
            cp("scalar", S[:, hc, 0:n], ps[bhc][:, 0:n], [("ps", bhc)], [("S", hc)])
            cp("pool", S[:, cu, 0:2], ccar[:, j, :], [kc_], [("S", cu)])
            tt("vector", S[:, cu, 2:2 + n], ps[bgc][:, 0:n], S[:, hc, 0:n], ALU.mult, [("ps", bgc), ("S", hc)], [("S", cu)])
            cp("pool", ccar[:, j, :], S[:, cu, n:n + 2], [("S", cu)], [kc_])
            act(S[:, v0, 0:n], S[:, cu, 2:2 + n], AF.Copy, [("S", cu), "cst"], [("S", v0)], scale=col(C_CW + 16 + j))
            stt(S[:, v1, 0:n], S[:, cu, 1:1 + n], col(C_CW + 8 + j), S[:, v0, 0:n], ALU.mult, ALU.add,
                [("S", cu), ("S", v0), "cst"], [("S", v1)])
            stt(S[:, v0, 0:n], S[:, cu, 0:n], col(C_CW + j), S[:, v1, 0:n], ALU.mult, ALU.add,
                [("S", cu), ("S", v1), "cst"], [("S", v0)])
            act(S[:, zz, 0:n], ps[bzb][:, 0:n], AF.Silu, [("ps", bzb)], [("S", zz)])
            tt("vector", S[:, gz, 0:n], ps[bgb][:, 0:n], S[:, zz, 0:n], ALU.mult, [("ps", bgb), ("S", zz)], [("S", gz)])
            tt("pool", big[:, 8 + j, t0:t0 + n], S[:, gz, 0:n], S[:, v0, 0:n], ALU.mult, [("S", gz), ("S", v0)],
               [("big", 8 + j, t0 // 512)])

    def out_proj(st, nkc, coff):
        for nh in range(2):
            b = nbank()
            for kc in range(nkc):
                mm(ps[b][:, 0:512], big[:, coff + kc, st * 128:(st + 1) * 128], wout[:, kc, nh * 512:(nh + 1) * 512],
                   kc == 0, kc == nkc - 1, [("big", coff + kc, st // 4), ("wout", kc)], [("ps", b)])
            tt("vector", h[:, st, nh * 512:(nh + 1) * 512], ps[b][:, 0:512], h[:, st, nh * 512:(nh + 1) * 512], ALU.add,
               [("ps", b), ("h", st)], [("h", st)])

    def rope_tables(t0, n):
        a, k_, r, m = nS(), nS(), nS(), nS()
        posi = S[:, a, :].bitcast(I32)
        for hh in range(0, n, 512):
            nn = min(512, n - hh)
            ki = S[:, k_, 0:nn].bitcast(I32)
            P.dma("sp", posi[:, 0:nn], pos_bc[:, t0 + hh:t0 + hh + nn], writes=[("S", a)])
            ts("vector", S[:, r, 0:nn], posi[:, 0:nn], col(C_INVF), ALU.mult, [("S", a), "cst"], [("S", r)])
            ts("vector", ki, S[:, r, 0:nn], 1.0 / TWO_PI, ALU.mult, [("S", r)], [("S", k_)])
            stt(S[:, m, 0:nn], ki, -C1, S[:, r, 0:nn], ALU.mult, ALU.add, [("S", k_), ("S", r)], [("S", m)])
            stt(S[:, r, 0:nn], ki, -C2, S[:, m, 0:nn], ALU.mult, ALU.add, [("S", k_), ("S", m)], [("S", r)])

            def wrap(buf, tmp):
                ts("vector", S[:, tmp, 0:nn], S[:, buf, 0:nn], PI_SAFE, ALU.is_gt, [("S", buf)], [("S", tmp)], s2=-TWO_PI, op1=ALU.mult)
                tt("vector", S[:, buf, 0:nn], S[:, buf, 0:nn], S[:, tmp, 0:nn], ALU.add, [("S", buf), ("S", tmp)], [("S", buf)])
                ts("vector", S[:, tmp, 0:nn], S[:, buf, 0:nn], -PI_SAFE, ALU.is_lt, [("S", buf)], [("S", tmp)], s2=TWO_PI, op1=ALU.mult)
                tt("vector", S[:, buf, 0:nn], S[:, buf, 0:nn], S[:, tmp, 0:nn], ALU.add, [("S", buf), ("S", tmp)], [("S", buf)])
                ts("vector", S[:, buf, 0:nn], S[:, buf, 0:nn], PI_SAFE, ALU.min, [("S", buf)], [("S", buf)], s2=-PI_SAFE, op1=ALU.max)

            wrap(r, m)
            act(sinT[:, hh:hh + nn], S[:, r, 0:nn], AF.Sin, [("S", r), "cst"], [("sinT", hh // 512)], scale=col(C_SGN))
            ts("vector", S[:, r, 0:nn], S[:, r, 0:nn], 1.5707963267948966, ALU.add, [("S", r)], [("S", r)])
            wrap(r, m)
            act(cosT[:, hh:hh + nn], S[:, r, 0:nn], AF.Sin, [("S", r)], [("cosT", hh // 512)])

    def l1_rot_chunk(gs, ci, bias_col, dst_fn, slices):
        for sl, (t0, n, sts, tb) in enumerate(slices):
            b = proj(gs, ci, t0, n, sts)
            r, xp, t1 = nS(), nS(), nS()
            act(S[:, r, 0:n], ps[b][:, 0:n], AF.Identity, [("ps", b), "cst"], [("S", r)], bias=col(bias_col))
            for qi, (d0, s0) in enumerate(((0, 8), (8, 0), (64, 72), (72, 64))):
                P.dma("scalar" if qi % 2 == 0 else "sp", S[d0:d0 + 8, xp, 0:n], S[s0:s0 + 8, r, 0:n],
                      reads=[("S", r)], writes=[("S", xp)])
            stt(S[:, t1, 0:n], ps[b][:, 0:n], col(bias_col), cosT[:, tb:tb + n], ALU.add, ALU.mult,
                [("ps", b), "cst", ("cosT", tb // 512)], [("S", t1)])
            tt("pool", S[:, xp, 0:n], S[:, xp, 0:n], sinT[:, tb:tb + n], ALU.mult, [("S", xp), ("sinT", tb // 512)], [("S", xp)])
            dst, dkeys = dst_fn(t0, n)
            tt("vector", dst, S[:, t1, 0:n], S[:, xp, 0:n], ALU.add, [("S", t1), ("S", xp)], dkeys)

    def l1_z_chunk(gs, ci, c, slices):
        for sl, (t0, n, sts, tb) in enumerate(slices):
            b = proj(gs, ci, t0, n, sts)
            act(big[:, 8 + c, t0:t0 + n], ps[b][:, 0:n], AF.Silu, [("ps", b), "cst"], [("big", 8 + c, t0 // 512)],
                bias=col(C_B + 9 + c))

    def l1_v(gs, ci, st, blk):
        b = nbank()
        w = wg[:, gs, ci, :].rearrange("p (k n) -> p k n", k=8)
        for kc in range(8):
            mm(ps[b][:, 0:128], yT[:, kc, st * 128:(st + 1) * 128], w[:, kc, :], kc == 0, kc == 7,
               [("wg", gs, ci), ("yT", st)], [("ps", b)])
        tt("vector", vaug[:, blk, :, 0:64], ps[b][:, 0:128].rearrange("p (a d) -> p a d", a=2),
           rowc[:, R_BV:R_BV + 128].rearrange("p (a d) -> p a d", a=2), ALU.add, [("ps", b), "rowc"], [("vaug", blk)])

    def attn_scores(st, kv, first_block):
        slots = {}
        for kb in range(2):
            kcol = (st + kb) * 128
            for hg in range(2):
                b = nbank()
                mm(ps[b][:, 0:512], kT[kv * 64:(kv + 1) * 64, kcol:kcol + 128],
                   big[kv * 64:(kv + 1) * 64, hg * 4:(hg + 1) * 4, st * 128:(st + 1) * 128], True, False,
                   [("kT", st + kb)] + [("big", c, st // 4) for c in range(hg * 4, hg * 4 + 4)], [("ps", b)])
                mi = 1 if kb == 1 else (2 if first_block else 0)
                mm(ps[b][:, 0:512], identb[:], maskb[:, mi, :], False, True, ["ident", "mask", "mask2"], [("ps", b)])
                s = state["pt"] % NPT
                state["pt"] += 1
                act(PT[:, s, :], ps[b][:, 0:512], AF.Exp, [("ps", b)], [("PT", s)], scale=0.125)
                slots[(kb, hg)] = s
        return slots

    def attn_pv(st, kv, slots, asl):
        for hg in range(2):
            b = nbank()
            for hl in range(4):
                for kb in range(2):
                    s = slots[(kb, hg)]
                    mm(ps[b][:, hl * 128: hl * 128 + 65], PT[:, s, hl * 128:(hl + 1) * 128], vaug[:, st + kb, kv, :],
                       kb == 0, kb == 1, [("PT", s), ("vaug", st + kb)], [("ps", b)])
            pv = ps[b][:].rearrange("p (a d) -> p a d", a=4)
            es = esink[:].rearrange("p (c k) -> p c k", k=2)[:, hg * 4:(hg + 1) * 4, kv:kv + 1]
            sm = small[:, (kv * 2 + hg) * 4:(kv * 2 + hg) * 4 + 4]
            kk = ("small", kv * 2 + hg)
            tt("vector", sm.unsqueeze(2), pv[:, :, 64:65], es, ALU.add, [("ps", b), "esink"], [kk])
            P.op("vector", lambda e, sm=sm: e.reciprocal(out=sm, in_=sm), [kk], [kk])
            dst = attn[:, asl, :].rearrange("p (c k d) -> p c k d", c=8, k=2)[:, hg * 4:(hg + 1) * 4, kv, :]
            tt("vector", dst, pv[:, :, 0:64], sm.unsqueeze(2).to_broadcast([128, 4, 64]), ALU.mult,
               [("ps", b), kk], [("attn", asl, kv, hg)])

    def attn_finish(st, asl, out_row0, final):
        b = nbank()
        pb = ps[b][:].bitcast(BF16)
        akeys = [("attn", asl, kv, hg) for kv in range(2) for hg in range(2)]
        for c in range(8):
            tr(pb[:, c * 128:(c + 1) * 128], attn[:, asl, c * 128:(c + 1) * 128], akeys, [("ps", b)])
        gk = [("big", 8 + c, st // 4) for c in range(8)]
        tt("vector", big[:, 8:16, st * 128:(st + 1) * 128], pb.rearrange("p (k t) -> p k t", k=8),
           big[:, 8:16, st * 128:(st + 1) * 128], ALU.mult, [("ps", b)] + gk, gk)
        out_proj(st, 8, 8)
        if not final:
            return
        i = state["ssi"]
        state["ssi"] = (state["ssi"] + 1) % 24
        act(junk, h[:, st, :], AF.Square, [("h", st)], JK + [("ss", i)], accum=ss[:, i:i + 1])
        ts("pool", rstd[:, i:i + 1], ss[:, i:i + 1], 1.0 / DM, ALU.mult, [("ss", i)], [("rstd", i)], s2=EPS, op1=ALU.add)
        tt("pool", rstd[:, i:i + 1], rstd[:, i:i + 1], cst[:, C_MHALF:C_MHALF + 1], ALU.pow, [("rstd", i), "cst"], [("rstd", i)])
        stt(h[:, st, :], h[:, st, :], rstd[:, i:i + 1], rowc[:, R_FG:R_FG + 1024], ALU.mult, ALU.mult,
            [("h", st), ("rstd", i), "rowc"], [("h", st)])
        P.dma("pool", out_d[out_row0: out_row0 + 128, :], h[:, st, :], reads=[("h", st)], writes=[("out", out_row0)])

    A_GROUPS = [[4 * g + i for i in range(4)] for g in range(4)]
    B_GROUPS = [[16 + 4 * j + i for i in range(4)] for j in range(8)]
    L0_GROUPS = A_GROUPS + B_GROUPS
    L1_GROUPS = [[0, 1, 2, 3], [4, 5, 6, 7], [8, 17], [9, 10, 11, 12], [13, 14, 15, 16]]
    gslot = {"i": 0}

    def next_gs():
        g = gslot["i"] % 2
        gslot["i"] += 1
        return g

    first_real_done = False
    for (T0, NT, is_halo) in TILES[:ntiles]:
        nst = NT // 128
        load_x(T0, nst)
        if is_halo:
            slices0 = [(0, 256, [0, 1])]
        else:
            slices0 = [(0, 512, [0, 1, 2, 3]), (512, 512, [4, 5, 6, 7])]
        norm_T(0, list(range(nst)))
        first_real = (not is_halo) and (not first_real_done)
        pending = None
        gs_list = []
        for gi, chunks in enumerate(L0_GROUPS):
            gs = next_gs()
            load_group(wie, chunks, gs, PID["wie"])
            if gi == 1:
                for kc in range(16):
                    stage_cast(woe[kc], wout[:, kc, :], [("wout", kc)], pid=PID["woe"] + kc)
            if pending is not None:
                pg, pgs = pending
                if pg < 4:
                    l0_group_a(pg, pgs, slices0, first_real)
                else:
                    l0_group_b(pg - 4, pgs, slices0)
            pending = (gi, gs)
        l1_gs = []
        gs = next_gs()
        load_group(wio, L1_GROUPS[2] if is_halo else L1_GROUPS[0], gs, PID["wio"])
        l1_gs.append(gs)
        pg, pgs = pending
        l0_group_b(pg - 4, pgs, slices0)
        if not is_halo:
            first_real_done = True
        l0_sts = [1] if is_halo else list(range(nst))
        pipe_norm = (mode != "L0") and dbg >= 1
        def norm_bias(st_):
            norm_st(1, st_)
            if not is_halo:
                tt("pool", h[:, st_, :], h[:, st_, :], rowc[:, R_BOUT:R_BOUT + 1024], ALU.add, [("h", st_), "rowc"], [("h", st_)])

        for i, st in enumerate(l0_sts):
            out_proj(st, 16, 0)
            if pipe_norm and i >= 1:
                norm_bias(l0_sts[i - 1])
        if pipe_norm:
            norm_bias(l0_sts[-1])
        if mode == "L0":
            if not is_halo:
                for st in range(nst):
                    r0 = T0 - HALO + st * 128
                    P.dma("pool", out_d[r0:r0 + 128, :], h[:, st, :], reads=[("h", st)], writes=[("out", r0)])
            continue
        if dbg < 1:
            continue
        if is_halo:
            if dbg < 2:
                continue
            rope_tables(T0 + 128, 128)
            gs = l1_gs[0]
            if dbg < 3:
                continue
            l1_rot_chunk(gs, 0, C_B + 8, lambda t0, n: (kT[:, 0:128], [("kT", 0)]), [(128, 128, [1], 0)])
            if dbg < 4:
                continue
            l1_v(gs, 1, 1, 0)
            continue
        if dbg < 4.5:
            continue
        rope_tables(T0, NT)
        if dbg < 4.6:
            continue
        slices1 = [(0, 512, [0, 1, 2, 3], 0), (512, 512, [4, 5, 6, 7], 512)]
        import os
        if os.environ.get("SL1") == "a":
            slices1 = [(0, 128, [0], 0)]
        if os.environ.get("SL1") == "b":
            slices1 = [(0, 512, [0, 1, 2, 3], 0)]
        if os.environ.get("SL1") == "c":
            slices1 = [(0, 256, [0, 1], 0)]
        import os
        for gi in range(5):
            if dbg < 5 and dbg < [4.7, 4.8, 4.9, 4.95, 4.97][gi]:
                break
            if gi + 1 < 5:
                gs = next_gs()
                if not os.environ.get("NOLOAD1"):
                    load_group(wio, L1_GROUPS[gi + 1], gs, PID["wio"])
                l1_gs.append(gs)
            if gi == 1:
                for kc in range(8):
                    stage_cast(woo[kc], wout[:, kc, :], [("wout", kc)], pid=PID["woo"] + kc)
            gs = l1_gs[gi]
            if gi in (0, 1):
                for ci in range(int(os.environ.get("ONECI", "4"))):
                    c = gi * 4 + ci
                    l1_rot_chunk(gs, ci, C_B + (0 if os.environ.get("BIAS0") else c),
                                 lambda t0, n, c=c: (big[:, c, t0:t0 + n], [("big", c, t0 // 512)]), slices1)
            elif gi == 2:
                l1_rot_chunk(gs, 0, C_B + 8,
                             lambda t0, n: (kT[:, 128 + t0:128 + t0 + n], [("kT", 1 + t0 // 128 + i) for i in range(n // 128)]),
                             slices1)
                for st in range(nst):
                    l1_v(gs, 1, st, 1 + st)
            elif not os.environ.get("SKIP_Z"):
                for ci in range(4):
                    l1_z_chunk(gs, ci, (gi - 3) * 4 + ci, slices1)
        if dbg < 6:
            continue
        units = [(st, kv) for st in range(nst) for kv in range(2)]
        prev = None
        for ui, (st, kv) in enumerate(units):
            fb = (T0 == HALO and st == 0)
            slots = attn_scores(st, kv, fb)
            if prev is not None:
                pst, pkv, pslots = prev
                if dbg >= 7:
                    attn_pv(pst, pkv, pslots, pst % 2)
                if pkv == 1 and dbg >= 8:
                    attn_finish(pst, pst % 2, T0 - HALO + pst * 128, True)
            prev = (st, kv, slots)
        pst, pkv, pslots = prev
        if dbg >= 7:
            attn_pv(pst, pkv, pslots, pst % 2)
        if dbg >= 8:
            attn_finish(pst, pst % 2, T0 - HALO + pst * 128, True)
        cp("pool", kT[:, 0:128], kT[:, 1024:1152], [("kT", 8)], [("kT", 0)])
        cp("pool", vaug[:, 0, :, :], vaug[:, 8, :, :], [("vaug", 8)], [("vaug", 0)])

    P.emit()
    return nc, P


def _blk(w, cols):
    sub = w[:, cols]
    return np.ascontiguousarray(sub.reshape(8, 128, len(cols)).transpose(1, 0, 2).reshape(128, 8 * len(cols)))


def prepare(inputs):
    f = np.float32
    x = np.asarray(inputs["x"], f)
    pos = np.asarray(inputs["positions"]).astype(np.int32)
    norm_g = np.asarray(inputs["norm_g"], f)
    w_in_even = np.asarray(inputs["w_in_even"], f)[0]
    w_pool = np.asarray(inputs["w_pool"], f)[0]
    pool_scale = np.asarray(inputs["pool_scale"], f)[0]
    conv_w = np.asarray(inputs["conv_w"], f)[0]
    w_out_even = np.asarray(inputs["w_out_even"], f)[0]
    w_in_odd = np.asarray(inputs["w_in_odd"], f)[0]
    b_in_odd = np.asarray(inputs["b_in_odd"], f)[0]
    sinks = np.asarray(inputs["attn_sinks"], f)[0]
    w_out_odd = np.asarray(inputs["w_out_odd"], f)[0]
    b_out_odd = np.asarray(inputs["b_out_odd"], f)[0]
    fg = np.asarray(inputs["final_norm_g"], f)

    ar = np.arange(128)
    cols = []
    for g in range(4):
        cols += [(2 * g) * 128 + ar, (2 * g + 1) * 128 + ar, 4096 + (2 * g) * 128 + ar, 4096 + (2 * g + 1) * 128 + ar]
    for j in range(8):
        cols += [2048 + j * 128 + ar, 3072 + j * 128 + ar, 1024 + j * 128 + ar, 4096 + (8 + j) * 128 + ar]
    wie = np.stack([_blk(w_in_even, c) for c in cols])
    wpl = np.ascontiguousarray(w_pool.reshape(4, 2, 128, 256).transpose(2, 0, 1, 3).reshape(128, 2048))
    wpl = np.ascontiguousarray(wpl.reshape(128, 2, 1024).transpose(1, 0, 2))
    woe = np.ascontiguousarray(w_out_even.reshape(16, 128, 1024))
    perm = np.concatenate([np.concatenate([c * 64 + np.arange(64), (8 + c) * 64 + np.arange(64)]) for c in range(8)])
    ocols = [perm[c * 128:(c + 1) * 128] for c in range(8)]
    ocols += [1024 + ar]
    ocols += [1280 + perm[c * 128:(c + 1) * 128] for c in range(8)]
    ocols += [1152 + ar]
    wio = np.stack([_blk(w_in_odd, c) for c in ocols])
    woo = np.ascontiguousarray(w_out_odd[perm].reshape(8, 128, 1024))
    cst = np.zeros((128, NCST), f)
    cst[:, C_G:C_G + 16] = norm_g.reshape(2, 8, 128).transpose(2, 0, 1).reshape(128, 16)
    cst[:, C_PS:C_PS + 8] = pool_scale.reshape(8, 128).T
    cst[:, C_CW:C_CW + 24] = conv_w.reshape(3, 8, 128).transpose(2, 0, 1).reshape(128, 24)
    for i in range(17):
        cst[:, C_B + i] = b_in_odd[ocols[i]]
    d = ar % 64
    inv_freq = (np.float32(500000.0) ** (-np.arange(0, 16, 2, dtype=np.float32) / np.float32(16))).astype(f)
    cst[:, C_INVF] = np.where(d < 16, inv_freq[d % 8], 0.0)
    cst[:, C_SGN] = np.where(d < 8, -1.0, np.where(d < 16, 1.0, 0.0))
    cst[:, C_MHALF] = -0.5
    rowc = np.zeros((128, NROW), f)
    rowc[:, R_BOUT:R_BOUT + 1024] = b_out_odd[None]
    rowc[:, R_FG:R_FG + 1024] = fg[None]
    rowc[:, R_BV:R_BV + 128] = b_in_odd[1152:1280][None]
    sl = np.zeros(16, f)
    for c in range(8):
        for k in range(2):
            sl[2 * c + k] = sinks[c + 8 * k]
    rowc[:, R_SINK:R_SINK + 16] = sl[None]
    pm = np.zeros((128, 1024), f)
    for m in range(128):
        dd = m % 64
        if dd < 8:
            pm[m + 8, m] = 1.0
        elif dd < 16:
            pm[m - 8, m] = 1.0
    s_ = np.arange(128)[:, None]
    q_ = np.arange(128)[None, :]
    m_prev = np.where(q_ < s_, 0.0, NEG).astype(f)
    m_diag = np.where(q_ >= s_, 0.0, NEG).astype(f)
    m_all = np.full((128, 128), NEG, f)
    in_maps = []
    for c in range(NCORES):
        b, half = c // 2, c % 2
        t0 = half * TOK
        xe = np.zeros((TL, DM), f)
        pe = np.zeros((TL,), np.int32)
        if half == 0:
            xe[HALO:] = x[b, 0:TOK]
            pe[HALO:] = pos[b, 0:TOK]
        else:
            xe[:] = x[b, t0 - HALO:t0 + TOK]
            pe[:] = pos[b, t0 - HALO:t0 + TOK]
        pc = np.ones((4, 16), f)
        if half == 0:
            for g in range(4):
                w = 2 << g
                pc[g] = w / np.minimum(np.arange(16) + 1, w)
        masks = np.zeros((2, 128, 1024), f)
        masks[0, :, 0:512] = np.tile(m_prev, (1, 4))
        masks[0, :, 512:1024] = np.tile(m_diag, (1, 4))
        masks[1, :, 0:512] = np.tile(m_all if half == 0 else m_prev, (1, 4))
        in_maps.append({
            "x_ext": xe, "pos_bc": np.ascontiguousarray(np.broadcast_to(pe[None], (128, TL))),
            "wie": wie, "wpl": wpl, "woe": woe, "wio": wio, "woo": woo, "cst": cst, "rowc": rowc,
            "pcorr": np.ascontiguousarray(np.broadcast_to(pc.reshape(1, 64), (128, 64))),
            "masks": masks, "pmat": pm,
        })
    return in_maps


_CACHE = {}


def kernel(**inputs):
    mode = "fused"
    if mode not in _CACHE:
        _CACHE[mode] = build(mode)[0]
    nc = _CACHE[mode]
    in_maps = prepare(inputs)
    res = run_bass_kernel_spmd(nc, in_maps, core_ids=list(range(NCORES)))
    out = np.empty((4, 8192, DM), np.float32)
    for c in range(NCORES):
        b, half = c // 2, c % 2
        out[b, half * TOK:(half + 1) * TOK] = res.results[c]["out"]
    return out
```

```python
from contextlib import ExitStack
import numpy as np
import concourse.bass as bass
import concourse.mybir as mybir
from concourse.bass_utils import run_bass_kernel_spmd

F32 = mybir.dt.float32
BF16 = mybir.dt.bfloat16
I32 = mybir.dt.int32
AF = mybir.ActivationFunctionType
ALU = mybir.AluOpType
AX = mybir.AxisListType


class _I:
    __slots__ = ("eng", "fn", "kind", "deps", "signal", "sig", "idx", "rawdeps", "line", "rw")


class Prog:
    NDMA = 30
    SAME_ENGINE_RAW = True

    def __init__(self, nc):
        self.nc = nc
        self.stack = ExitStack()
        self.instrs = []
        self.lw = {}
        self.rd = {}
        self.trace = None

    def sbuf(self, name, shape, dtype):
        return self.stack.enter_context(self.nc.sbuf_tensor(name, shape, dtype))

    def psum(self, name, shape, dtype):
        return self.stack.enter_context(self.nc.psum_tensor(name, shape, dtype))

    def _add(self, eng, fn, reads, writes, kind):
        ins = _I()
        ins.eng, ins.fn, ins.kind = eng, fn, kind
        ins.idx = len(self.instrs)
        import sys as _sys
        f = _sys._getframe(2)
        ln = []
        while f is not None and len(ln) < 4:
            ln.append(str(f.f_lineno))
            f = f.f_back
        ins.line = "<".join(ln)
        ins.rw = (reads, writes)
        ins.signal = False
        ins.sig = None
        deps = {}
        for k in reads:
            w = self.lw.get(k)
            if w is not None:
                deps[w.idx] = (w, True)
        for k in writes:
            w = self.lw.get(k)
            if w is not None and w.idx not in deps:
                deps[w.idx] = (w, False)
            for r in self.rd.get(k, ()):
                if r.idx not in deps:
                    deps[r.idx] = (r, False)
        out = []
        for d, raw in deps.values():
            if d.kind == "op" and d.eng == eng and kind == "op":
                if eng == "tensor" or not self.SAME_ENGINE_RAW:
                    continue
            out.append(d)
        ins.deps = out
        for k in reads:
            lst = self.rd.setdefault(k, [])
            if kind == "op":
                lst[:] = [r for r in lst if not (r.kind == "op" and r.eng == eng)]
            lst.append(ins)
        for k in writes:
            self.lw[k] = ins
            self.rd[k] = []
        self.instrs.append(ins)
        return ins

    def op(self, eng, fn, reads=(), writes=()):
        return self._add(eng, fn, tuple(reads), tuple(writes), "op")

    def dma(self, eng, out, in_, reads=(), writes=(), is_output=False):
        return self._add(eng, lambda e: e.dma_start(out=out, in_=in_), tuple(reads), tuple(writes), "dma")

    def emit(self):
        nc = self.nc
        st = self.stack
        engs = ["tensor", "vector", "scalar", "pool", "sp"]
        esem = {e: st.enter_context(nc.semaphore("sem_" + e)) for e in engs}
        dsem = [st.enter_context(nc.semaphore("dsem%d" % i)) for i in range(self.NDMA)]
        for ins in self.instrs:
            for d in ins.deps:
                d.signal = True
        cnt = {e: 0 for e in engs}
        dcnt = [0] * self.NDMA
        dlast = [None] * self.NDMA
        pools = {"sp": list(range(0, self.NDMA - 14)), "pool": list(range(self.NDMA - 14, self.NDMA - 8)),
                 "scalar": list(range(self.NDMA - 8, self.NDMA))}
        nd = {"sp": 0, "pool": 0, "scalar": 0}
        for ins in self.instrs:
            if ins.kind == "dma":
                pl = pools[ins.eng]
                s = pl[nd[ins.eng] % len(pl)]
                nd[ins.eng] += 1
                if dlast[s] is not None:
                    ins.deps.append(dlast[s])
                dcnt[s] += 16
                ins.sig = (dsem[s], dcnt[s])
                dlast[s] = ins
                ins.signal = True
            elif ins.signal:
                cnt[ins.eng] += 1
                ins.sig = (esem[ins.eng], cnt[ins.eng])
        per = {e: [i for i in self.instrs if i.eng == e] for e in engs}
        self.stats = {e: len(per[e]) for e in engs}
        self.stats["signals"] = dict(cnt)
        block = st.enter_context(nc.Block())

        def run(e, lst, final):
            waited = {}
            for ins in lst:
                if self.trace is not None:
                    self.trace.append((ins.eng, ins.idx, ins.kind, ins.line, [(d.eng, d.idx, d.sig[1]) for d in ins.deps], ins.sig[1] if ins.sig else None, ins.rw))
                need = {}
                for d in ins.deps:
                    sem, val = d.sig
                    key = id(sem)
                    if waited.get(key, 0) >= val:
                        continue
                    if key not in need or need[key][1] < val:
                        need[key] = (sem, val)
                for key, (sem, val) in need.items():
                    e.wait_ge(sem, val)
                    waited[key] = val
                r = ins.fn(e)
                if ins.signal:
                    sem, val = ins.sig
                    r.then_inc(sem, 16 if ins.kind == "dma" else 1)
            if final:
                for s in range(self.NDMA):
                    if dcnt[s] and waited.get(id(dsem[s]), 0) < dcnt[s]:
                        e.wait_ge(dsem[s], dcnt[s])

        @block.tensor
        def _(e):
            run(e, per["tensor"], False)

        @block.vector
        def _(e):
            run(e, per["vector"], False)

        @block.scalar
        def _(e):
            run(e, per["scalar"], True)

        @block.gpsimd
        def _(e):
            run(e, per["pool"], True)

        @block.sync
        def _(e):
            run(e, per["sp"], True)

        st.close()


NCORES = 8
DM = 1024
TOK = 4096
HALO = 256
TL = TOK + HALO
EPS = 1e-5
NEG = -30000.0
TWO_PI = 6.283185307179586
C1 = 6.28125
C2 = TWO_PI - C1
PI_SAFE = 3.1415925
TILES = [(0, 256, True)] + [(HALO + 1024 * i, 1024, False) for i in range(4)]

C_G = 0
C_PS = 16
C_CW = 24
C_B = 48
C_INVF = 65
C_SGN = 66
C_MHALF = 67
NCST = 68
R_BOUT = 0
R_FG = 1024
R_BV = 2048
R_SINK = 2176
NROW = 2192


def build(mode="fused", ntiles=5, dbg=99):
    nc = bass.Bass("TRN2", target_bir_lowering=False)
    P = Prog(nc)

    def din(name, shape, dt=F32):
        return nc.dram_tensor(name, shape, dt, kind="ExternalInput").ap()

    x_ext = din("x_ext", [TL, DM])
    pos_bc = din("pos_bc", [128, TL], I32)
    wie = din("wie", [48, 128, 1024])
    wpl = din("wpl", [2, 128, 1024])
    woe = din("woe", [16, 128, 1024])
    wio = din("wio", [18, 128, 1024])
    woo = din("woo", [8, 128, 1024])
    cst_d = din("cst", [128, NCST])
    rowc_d = din("rowc", [128, NROW])
    pcorr_d = din("pcorr", [128, 64])
    masks_d = din("masks", [2, 128, 1024])
    pmat_d = din("pmat", [128, 1024])
    out_d = nc.dram_tensor("out", [TOK, DM], F32, kind="ExternalOutput").ap()
    wscr = nc.dram_tensor("wscr", [90, 128, 1024], BF16, kind="Internal").ap()
    PID = {"wie": 0, "woe": 48, "wio": 64, "woo": 82}

    NSTG = 3
    h = P.sbuf("h", [128, 8, 1024], F32)
    ybuf = P.sbuf("ybuf", [128, 2, 1024], BF16)
    yT = P.sbuf("yT", [128, 8, 1024], BF16)
    big = P.sbuf("big", [128, 16, 1024], BF16)
    stg = P.sbuf("stg", [128, NSTG, 1024], F32)
    wg = P.sbuf("wg", [128, 2, 4, 1024], BF16)
    wout = P.sbuf("wout", [128, 16, 1024], BF16)
    wpool = P.sbuf("wpool", [128, 2, 1024], BF16)
    cst = P.sbuf("cstt", [128, NCST], F32)
    rowc = P.sbuf("rowct", [128, NROW], F32)
    pcorr = P.sbuf("pcorrt", [128, 64], F32)
    maskb = P.sbuf("maskb", [128, 3, 512], BF16)
    identf = P.sbuf("identf", [128, 128], F32)
    identb = P.sbuf("identb", [128, 128], BF16)
    esink = P.sbuf("esink", [128, 16], F32)
    ucar = P.sbuf("ucar", [128, 8, 16], F32)
    ccar = P.sbuf("ccar", [128, 8, 2], F32)
    NS = 8
    S = P.sbuf("S", [128, NS, 528], F32)
    pooled = P.sbuf("pooled", [128, 2, 512], BF16)
    xbq = P.sbuf("xbq", [128, 2, 512], BF16)
    kT = P.sbuf("kT", [128, 128 + 1024], BF16)
    vaug = P.sbuf("vaug", [128, 9, 2, 65], BF16)
    cosT = P.sbuf("cosT", [128, 1024], F32)
    sinT = P.sbuf("sinT", [128, 1024], F32)
    NPT = 8
    PT = P.sbuf("PT", [128, NPT, 512], BF16)
    attn = P.sbuf("attn", [128, 2, 1024], BF16)
    ss = P.sbuf("ss", [128, 32], F32)
    rstd = P.sbuf("rstd", [128, 32], F32)
    small = P.sbuf("small", [128, 16], F32)
    ps = [P.psum("ps%d" % i, [128, 512], F32) for i in range(8)]
    junk = xbq[:].rearrange("p a b -> p (a b)")
    JK = [("xbq", 0), ("xbq", 1)]

    state = {"bank": 0, "stg": 0, "S": 0, "pt": 0, "ssi": 0, "xs": 0, "ysl": 0}

    def nbank():
        b = state["bank"] % 8
        state["bank"] += 1
        return b

    def nS():
        s = state["S"] % NS
        state["S"] += 1
        return s

    def col(i):
        return cst[:, i:i + 1]

    def mm(out, lhsT, rhs, start, stop, reads, writes):
        P.op("tensor", lambda e: e.matmul(out=out, lhsT=lhsT, rhs=rhs, start=start, stop=stop),
             reads, writes)

    def tr(out, in_, reads, writes):
        P.op("tensor", lambda e: e.transpose(out=out, in_=in_, identity=identb[:]), list(reads) + ["ident"], writes)

    def act(out, in_, func, reads, writes, scale=None, bias=None, accum=None):
        kw = {}
        if scale is not None:
            kw["scale"] = scale
        if bias is not None:
            kw["bias"] = bias
        if accum is not None:
            kw["accum_out"] = accum
        P.op("scalar", lambda e: e.activation(out=out, in_=in_, func=func, **kw), reads, writes)

    def tt(eng, out, in0, in1, op, reads, writes):
        P.op(eng, lambda e: e.tensor_tensor(out=out, in0=in0, in1=in1, op=op), reads, writes)

    def stt(out, in0, scalar, in1, op0, op1, reads, writes):
        P.op("vector", lambda e: e.scalar_tensor_tensor(out=out, in0=in0, scalar=scalar, in1=in1, op0=op0, op1=op1),
             reads, writes)

    def ts(eng, out, in0, s1, op0, reads, writes, s2=None, op1=None):
        if op1 is None:
            P.op(eng, lambda e: e.tensor_scalar(out=out, in0=in0, scalar1=s1, scalar2=None, op0=op0), reads, writes)
        else:
            P.op(eng, lambda e: e.tensor_scalar(out=out, in0=in0, scalar1=s1, scalar2=s2, op0=op0, op1=op1), reads, writes)

    def cp(eng, out, in_, reads, writes):
        if eng == "scalar":
            P.op(eng, lambda e: e.copy(out=out, in_=in_), reads, writes)
        else:
            P.op(eng, lambda e: e.tensor_copy(out=out, in_=in_), reads, writes)

    cast_rr = {"i": 0}
    CAST_ENGS = ["scalar", "scalar", "pool"]

    cached = set()

    def stage_cast(dram_ap, dst_ap, dst_keys, ncols=1024, eng=None, pid=None):
        if pid is not None and pid in cached:
            P.dma("sp", dst_ap, wscr[pid], reads=[("scr", pid)], writes=dst_keys)
            return
        stage_cast_(dram_ap, dst_ap, dst_keys, ncols, eng)
        if pid is not None:
            P.dma("pool", wscr[pid], dst_ap, reads=dst_keys, writes=[("scr", pid)])
            cached.add(pid)

    def stage_cast_(dram_ap, dst_ap, dst_keys, ncols=1024, eng=None):
        s = state["stg"] % NSTG
        state["stg"] += 1
        P.dma("sp", stg[:, s, 0:ncols], dram_ap, reads=[], writes=[("stg", s)])
        if eng is None:
            eng = CAST_ENGS[cast_rr["i"] % len(CAST_ENGS)]
            cast_rr["i"] += 1
        cp(eng, dst_ap, stg[:, s, 0:ncols], [("stg", s)], dst_keys)

    P.dma("sp", cst[:], cst_d, writes=["cst"])
    P.dma("sp", rowc[:], rowc_d, writes=["rowc"])
    P.dma("sp", pcorr[:], pcorr_d, writes=["pcorr"])
    P.op("pool", lambda e: e.memset(identf[:], 0.0), writes=["identf"])
    P.op("pool", lambda e: e.affine_select(out=identf[:], in_=identf[:], pattern=[[-1, 128]],
                                           compare_op=ALU.not_equal, fill=1.0, base=0, channel_multiplier=1),
         reads=["identf"], writes=["identf"])
    cp("vector", identb[:], identf[:], ["identf"], ["ident"])
    P.op("pool", lambda e: e.memset(ucar[:], 0.0), writes=["ucar%d" % i for i in range(8)])
    P.op("pool", lambda e: e.memset(ccar[:], 0.0), writes=["ccar%d" % i for i in range(8)])
    P.op("vector", lambda e: e.memset(vaug[:, :, :, 64:65], 1.0), writes=[("vaug", i) for i in range(9)])
    for i in range(2):
        stage_cast(wpl[i], wpool[:, i, :], [("wpool", i)], eng="vector")
    stage_cast(masks_d[0], maskb[:, 0:2, :].rearrange("p a b -> p (a b)"), ["mask"], eng="vector")
    stage_cast(masks_d[1][:, 0:512], maskb[:, 2, :], ["mask2"], ncols=512, eng="vector")
    P.op("vector", lambda e: e.memset(S[:], 0.0), writes=[("S", i) for i in range(NS)])
    act(esink[:], rowc[:, R_SINK:R_SINK + 16], AF.Exp, ["rowc"], ["esink"])

    def load_x(t0, nst):
        for st in range(nst):
            P.dma("sp", h[:, st, :], x_ext[t0 + st * 128: t0 + (st + 1) * 128, :], writes=[("h", st)])

    def norm_T(layer, sts):
        base = state["ssi"]
        state["ssi"] = (state["ssi"] + 8) % 24
        for st in sts:
            act(ybuf[:, st % 2, :], h[:, st, :], AF.Square, [("h", st)], [("ybuf", st % 2), ("ss", base + st)],
                accum=ss[:, base + st: base + st + 1])
        lo, hi = base + sts[0], base + sts[-1] + 1
        keys_ss = [("ss", base + st) for st in sts]
        keys_r = [("rstd", base + st) for st in sts]
        ts("pool", rstd[:, lo:hi], ss[:, lo:hi], 1.0 / DM, ALU.mult, keys_ss, keys_r, s2=EPS, op1=ALU.add)
        tt("pool", rstd[:, lo:hi], rstd[:, lo:hi], cst[:, C_MHALF:C_MHALF + 1].to_broadcast([128, hi - lo]),
           ALU.pow, keys_r + ["cst"], keys_r)
        for st in sts:
            sl = st % 2
            act(ybuf[:, sl, :], h[:, st, :], AF.Copy, [("h", st), ("rstd", base + st)], [("ybuf", sl)],
                scale=rstd[:, base + st: base + st + 1])
            b = nbank()
            pb = ps[b][:].bitcast(BF16)
            for kc in range(8):
                tr(pb[:, kc * 128:(kc + 1) * 128], ybuf[:, sl, kc * 128:(kc + 1) * 128], [("ybuf", sl)], [("ps", b)])
            tt("vector", yT[:, :, st * 128:(st + 1) * 128], pb.rearrange("p (k t) -> p k t", k=8),
               cst[:, C_G + layer * 8: C_G + layer * 8 + 8].unsqueeze(2).to_broadcast([128, 8, 128]), ALU.mult,
               [("ps", b), "cst"], [("yT", st)])

    def norm_st(layer, st):
        i = state["ssi"]
        state["ssi"] = (state["ssi"] + 1) % 24
        sl = state["ysl"] % 2
        state["ysl"] += 1
        act(ybuf[:, sl, :], h[:, st, :], AF.Square, [("h", st)], [("ybuf", sl), ("ss", i)], accum=ss[:, i:i + 1])
        ts("pool", rstd[:, i:i + 1], ss[:, i:i + 1], 1.0 / DM, ALU.mult, [("ss", i)], [("rstd", i)], s2=EPS, op1=ALU.add)
        tt("pool", rstd[:, i:i + 1], rstd[:, i:i + 1], cst[:, C_MHALF:C_MHALF + 1], ALU.pow, [("rstd", i), "cst"], [("rstd", i)])
        act(ybuf[:, sl, :], h[:, st, :], AF.Copy, [("h", st), ("rstd", i)], [("ybuf", sl)], scale=rstd[:, i:i + 1])
        b = nbank()
        pb = ps[b][:].bitcast(BF16)
        for kc in range(8):
            tr(pb[:, kc * 128:(kc + 1) * 128], ybuf[:, sl, kc * 128:(kc + 1) * 128], [("ybuf", sl)], [("ps", b)])
        tt("vector", yT[:, :, st * 128:(st + 1) * 128], pb.rearrange("p (k t) -> p k t", k=8),
           cst[:, C_G + layer * 8: C_G + layer * 8 + 8].unsqueeze(2).to_broadcast([128, 8, 128]), ALU.mult,
           [("ps", b), "cst"], [("yT", st)])

    def load_group(src, chunk_ids, gs, base):
        for ci, ch in enumerate(chunk_ids):
            stage_cast(src[ch], wg[:, gs, ci, :], [("wg", gs, ci)], pid=base + ch)

    def proj(gs, ci, t0, n, sts):
        b = nbank()
        w = wg[:, gs, ci, :].rearrange("p (k n) -> p k n", k=8)
        for kc in range(8):
            mm(ps[b][:, 0:n], w[:, kc, :], yT[:, kc, t0:t0 + n], kc == 0, kc == 7,
               [("wg", gs, ci)] + [("yT", st) for st in sts], [("ps", b)])
        return b

    def l0_group_a(g, gs, slices, first_real):
        w = 2 << g
        for sl, (t0, n, sts) in enumerate(slices):
            bu = [proj(gs, 0, t0, n, sts), proj(gs, 1, t0, n, sts)]
            bz = [proj(gs, 2, t0, n, sts), proj(gs, 3, t0, n, sts)]
            zs = []
            for i in range(2):
                c = 2 * g + i
                ub, ta, tb = nS(), nS(), nS()
                kc_ = "ucar%d" % c
                cp("pool", S[:, ub, 0:16], ucar[:, c, :], [kc_], [("S", ub)])
                cp("scalar", S[:, ub, 16:16 + n], ps[bu[i]][:, 0:n], [("ps", bu[i])], [("S", ub)])
                cp("pool", ucar[:, c, :], S[:, ub, n:n + 16], [("S", ub)], [kc_])
                src = ub
                lvl = 1
                dst = ta
                while (1 << lvl) <= w:
                    sh = 1 << (lvl - 1)
                    lo = 16 - (16 - (1 << lvl)) if (1 << lvl) < 16 else 16
                    lo = (1 << lvl)
                    tt("vector", S[:, dst, lo:16 + n], S[:, src, lo:16 + n], S[:, src, lo - sh:16 + n - sh], ALU.add,
                       [("S", src)], [("S", dst)])
                    src = dst
                    dst = tb if dst == ta else ta
                    lvl += 1
                if first_real and sl == 0:
                    tt("vector", S[:, src, 16:32], S[:, src, 16:32], pcorr[:, g * 16:(g + 1) * 16], ALU.mult,
                       [("S", src), "pcorr"], [("S", src)])
                stt(pooled[:, i, 0:n], S[:, src, 16:16 + n], 1.0 / w, S[:, ub, 16:16 + n], ALU.mult, ALU.subtract,
                    [("S", src), ("S", ub)], [("pooled", i)])
                z = nS()
                act(S[:, z, 0:n], ps[bz[i]][:, 0:n], AF.Silu, [("ps", bz[i])], [("S", z)])
                zs.append(z)
            for oc in range(2):
                c = 2 * g + oc
                b = nbank()
                for kc in range(2):
                    o = (g % 2) * 512 + kc * 256 + oc * 128
                    mm(ps[b][:, 0:n], wpool[:, g // 2, o:o + 128], pooled[:, kc, 0:n], kc == 0, kc == 1,
                       [("wpool", g // 2), ("pooled", 0), ("pooled", 1)], [("ps", b)])
                stt(big[:, c, t0:t0 + n], ps[b][:, 0:n], col(C_PS + c), S[:, zs[oc], 0:n], ALU.mult, ALU.mult,
                    [("ps", b), "cst", ("S", zs[oc])], [("big", c, t0 // 512)])

    def l0_group_b(j, gs, slices):
        for sl, (t0, n, sts) in enumerate(slices):
            bgc = proj(gs, 0, t0, n, sts)
            bhc = proj(gs, 1, t0, n, sts)
            bgb = proj(gs, 2, t0, n, sts)
            bzb = proj(gs, 3, t0, n, sts)
            hc, cu, v0, v1, zz, gz = nS(), nS(), nS(), nS(), nS(), nS()
            kc_ = "ccar%d" % j
            cp("scalar", S[:, hc, 0:n], ps[bhc][:, 0:n], [("ps", bhc)], [("S", hc)])
            cp("pool", S[:, cu, 0:2], ccar[:, j, :], [kc_], [("S", cu)])
            tt("vector", S[:, cu, 2:2 + n], ps[bgc][:, 0:n], S[:, hc, 0:n], ALU.mult, [("ps", bgc), ("S", hc)], [("S", cu)])
            cp("pool", ccar[:, j, :], S[:, cu, n:n + 2], [("S", cu)], [kc_])
            act(S[:, v0, 0:n], S[:, cu, 2:2 + n], AF.Copy, [("S", cu), "cst"], [("S", v0)], scale=col(C_CW + 16 + j))
            stt(S[:, v1, 0:n], S[:, cu, 1:1 + n], col(C_CW + 8 + j), S[:, v0, 0:n], ALU.mult, ALU.add,
                [("S", cu), ("S", v0), "cst"], [("S", v1)])
            stt(S[:, v0, 0:n], S[:, cu, 0:n], col(C_CW + j), S[:, v1, 0:n], ALU.mult, ALU.add,
                [("S", cu), ("S", v1), "cst"], [("S", v0)])
            act(S[:, zz, 0:n], ps[bzb][:, 0:n], AF.Silu, [("ps", bzb)], [("S", zz)])
            tt("vector", S[:, gz, 0:n], ps[bgb][:, 0:n], S[:, zz, 0:n], ALU.mult, [("ps", bgb), ("S", zz)], [("S", gz)])
            tt("pool", big[:, 8 + j, t0:t0 + n], S[:, gz, 0:n], S[:, v0, 0:n], ALU.mult, [("S", gz), ("S", v0)],
               [("big", 8 + j, t0 // 512)])

    def out_proj(st, nkc, coff):
        for nh in range(2):
            b = nbank()
            for kc in range(nkc):
                mm(ps[b][:, 0:512], big[:, coff + kc, st * 128:(st + 1) * 128], wout[:, kc, nh * 512:(nh + 1) * 512],
                   kc == 0, kc == nkc - 1, [("big", coff + kc, st // 4), ("wout", kc)], [("ps", b)])
            tt("vector", h[:, st, nh * 512:(nh + 1) * 512], ps[b][:, 0:512], h[:, st, nh * 512:(nh + 1) * 512], ALU.add,
               [("ps", b), ("h", st)], [("h", st)])

    def rope_tables(t0, n):
        a, k_, r, m = nS(), nS(), nS(), nS()
        posi = S[:, a, :].bitcast(I32)
        for hh in range(0, n, 512):
            nn = min(512, n - hh)
            ki = S[:, k_, 0:nn].bitcast(I32)
            P.dma("sp", posi[:, 0:nn], pos_bc[:, t0 + hh:t0 + hh + nn], writes=[("S", a)])
            ts("vector", S[:, r, 0:nn], posi[:, 0:nn], col(C_INVF), ALU.mult, [("S", a), "cst"], [("S", r)])
            ts("vector", ki, S[:, r, 0:nn], 1.0 / TWO_PI, ALU.mult, [("S", r)], [("S", k_)])
            stt(S[:, m, 0:nn], ki, -C1, S[:, r, 0:nn], ALU.mult, ALU.add, [("S", k_), ("S", r)], [("S", m)])
            stt(S[:, r, 0:nn], ki, -C2, S[:, m, 0:nn], ALU.mult, ALU.add, [("S", k_), ("S", m)], [("S", r)])

            def wrap(buf, tmp):
                ts("vector", S[:, tmp, 0:nn], S[:, buf, 0:nn], PI_SAFE, ALU.is_gt, [("S", buf)], [("S", tmp)], s2=-TWO_PI, op1=ALU.mult)
                tt("vector", S[:, buf, 0:nn], S[:, buf, 0:nn], S[:, tmp, 0:nn], ALU.add, [("S", buf), ("S", tmp)], [("S", buf)])
                ts("vector", S[:, tmp, 0:nn], S[:, buf, 0:nn], -PI_SAFE, ALU.is_lt, [("S", buf)], [("S", tmp)], s2=TWO_PI, op1=ALU.mult)
                tt("vector", S[:, buf, 0:nn], S[:, buf, 0:nn], S[:, tmp, 0:nn], ALU.add, [("S", buf), ("S", tmp)], [("S", buf)])
                ts("vector", S[:, buf, 0:nn], S[:, buf, 0:nn], PI_SAFE, ALU.min, [("S", buf)], [("S", buf)], s2=-PI_SAFE, op1=ALU.max)

            wrap(r, m)
            act(sinT[:, hh:hh + nn], S[:, r, 0:nn], AF.Sin, [("S", r), "cst"], [("sinT", hh // 512)], scale=col(C_SGN))
            ts("vector", S[:, r, 0:nn], S[:, r, 0:nn], 1.5707963267948966, ALU.add, [("S", r)], [("S", r)])
            wrap(r, m)
            act(cosT[:, hh:hh + nn], S[:, r, 0:nn], AF.Sin, [("S", r)], [("cosT", hh // 512)])

    def l1_rot_chunk(gs, ci, bias_col, dst_fn, slices):
        for sl, (t0, n, sts, tb) in enumerate(slices):
            b = proj(gs, ci, t0, n, sts)
            r, xp = nS(), nS()
            t1 = r
            act(S[:, r, 0:n], ps[b][:, 0:n], AF.Identity, [("ps", b), "cst"], [("S", r)], bias=col(bias_col))
            def rows2(buf, p0):
                a_ = S[p0:p0 + 1, buf, 0:n]
                pst = a_.ap[0][0]
                return type(a_)(tensor=a_.tensor, offset=a_.offset, ap=[[2 * pst, 16], [1, n]])
            P.dma("scalar", rows2(xp, 48), rows2(r, 49), reads=[("S", r)], writes=[("S", xp)])
            P.dma("sp", rows2(xp, 49), rows2(r, 48), reads=[("S", r)], writes=[("S", xp)])
            stt(S[:, t1, 0:n], ps[b][:, 0:n], col(bias_col), cosT[:, tb:tb + n], ALU.add, ALU.mult,
                [("ps", b), "cst", ("cosT", tb // 512)], [("S", t1)])
            tt("pool", S[:, xp, 0:n], S[:, xp, 0:n], sinT[:, tb:tb + n], ALU.mult, [("S", xp), ("sinT", tb // 512)], [("S", xp)])
            dst, dkeys = dst_fn(t0, n)
            tt("vector", dst, S[:, t1, 0:n], S[:, xp, 0:n], ALU.add, [("S", t1), ("S", xp)], dkeys)

    def l1_z_chunk(gs, ci, c, slices):
        for sl, (t0, n, sts, tb) in enumerate(slices):
            b = proj(gs, ci, t0, n, sts)
            act(big[:, 8 + c, t0:t0 + n], ps[b][:, 0:n], AF.Silu, [("ps", b), "cst"], [("big", 8 + c, t0 // 512)],
                bias=col(C_B + 9 + c))

    def l1_v(gs, ci, st, blk):
        b = nbank()
        w = wg[:, gs, ci, :].rearrange("p (k n) -> p k n", k=8)
        for kc in range(8):
            mm(ps[b][:, 0:128], yT[:, kc, st * 128:(st + 1) * 128], w[:, kc, :], kc == 0, kc == 7,
               [("wg", gs, ci), ("yT", st)], [("ps", b)])
        tt("vector", vaug[:, blk, :, 0:64], ps[b][:, 0:128].rearrange("p (a d) -> p a d", a=2),
           rowc[:, R_BV:R_BV + 128].rearrange("p (a d) -> p a d", a=2), ALU.add, [("ps", b), "rowc"], [("vaug", blk)])

    def attn_scores(st, kv, first_block):
        slots = {}
        for kb in range(2):
            kcol = (st + kb) * 128
            for hg in range(2):
                b = nbank()
                mm(ps[b][:, 0:512], kT[kv * 64:(kv + 1) * 64, kcol:kcol + 128],
                   big[kv * 64:(kv + 1) * 64, hg * 4:(hg + 1) * 4, st * 128:(st + 1) * 128], True, False,
                   [("kT", st + kb)] + [("big", c, st // 4) for c in range(hg * 4, hg * 4 + 4)], [("ps", b)])
                mi = 1 if kb == 1 else (2 if first_block else 0)
                mm(ps[b][:, 0:512], identb[:], maskb[:, mi, :], False, True, ["ident", "mask", "mask2"], [("ps", b)])
                s = state["pt"] % NPT
                state["pt"] += 1
                act(PT[:, s, :], ps[b][:, 0:512], AF.Exp, [("ps", b)], [("PT", s)], scale=0.125)
                slots[(kb, hg)] = s
        return slots

    def attn_pv(st, kv, slots, asl):
        for hg in range(2):
            b = nbank()
            for hl in range(4):
                for kb in range(2):
                    s = slots[(kb, hg)]
                    mm(ps[b][:, hl * 128: hl * 128 + 65], PT[:, s, hl * 128:(hl + 1) * 128], vaug[:, st + kb, kv, :],
                       kb == 0, kb == 1, [("PT", s), ("vaug", st + kb)], [("ps", b)])
            pv = ps[b][:].rearrange("p (a d) -> p a d", a=4)
            es = esink[:].rearrange("p (c k) -> p c k", k=2)[:, hg * 4:(hg + 1) * 4, kv:kv + 1]
            sm = small[:, (kv * 2 + hg) * 4:(kv * 2 + hg) * 4 + 4]
            kk = ("small", kv * 2 + hg)
            tt("vector", sm.unsqueeze(2), pv[:, :, 64:65], es, ALU.add, [("ps", b), "esink"], [kk])
            P.op("vector", lambda e, sm=sm: e.reciprocal(out=sm, in_=sm), [kk], [kk])
            dst = attn[:, asl, :].rearrange("p (c k d) -> p c k d", c=8, k=2)[:, hg * 4:(hg + 1) * 4, kv, :]
            tt("vector", dst, pv[:, :, 0:64], sm.unsqueeze(2).to_broadcast([128, 4, 64]), ALU.mult,
               [("ps", b), kk], [("attn", asl, kv, hg)])

    def attn_finish(st, asl, out_row0, final):
        b = nbank()
        pb = ps[b][:].bitcast(BF16)
        akeys = [("attn", asl, kv, hg) for kv in range(2) for hg in range(2)]
        for c in range(8):
            tr(pb[:, c * 128:(c + 1) * 128], attn[:, asl, c * 128:(c + 1) * 128], akeys, [("ps", b)])
        gk = [("big", 8 + c, st // 4) for c in range(8)]
        tt("vector", big[:, 8:16, st * 128:(st + 1) * 128], pb.rearrange("p (k t) -> p k t", k=8),
           big[:, 8:16, st * 128:(st + 1) * 128], ALU.mult, [("ps", b)] + gk, gk)
        out_proj(st, 8, 8)
        if not final:
            return
        i = state["ssi"]
        state["ssi"] = (state["ssi"] + 1) % 24
        act(junk, h[:, st, :], AF.Square, [("h", st)], JK + [("ss", i)], accum=ss[:, i:i + 1])
        ts("pool", rstd[:, i:i + 1], ss[:, i:i + 1], 1.0 / DM, ALU.mult, [("ss", i)], [("rstd", i)], s2=EPS, op1=ALU.add)
        tt("pool", rstd[:, i:i + 1], rstd[:, i:i + 1], cst[:, C_MHALF:C_MHALF + 1], ALU.pow, [("rstd", i), "cst"], [("rstd", i)])
        stt(h[:, st, :], h[:, st, :], rstd[:, i:i + 1], rowc[:, R_FG:R_FG + 1024], ALU.mult, ALU.mult,
            [("h", st), ("rstd", i), "rowc"], [("h", st)])
        P.dma("pool", out_d[out_row0: out_row0 + 128, :], h[:, st, :], reads=[("h", st)], writes=[("out", out_row0)])

    A_GROUPS = [[4 * g + i for i in range(4)] for g in range(4)]
    B_GROUPS = [[16 + 4 * j + i for i in range(4)] for j in range(8)]
    L0_GROUPS = A_GROUPS + B_GROUPS
    L0_ORDER = [3, 4, 2, 5, 1, 6, 0, 7, 8, 9, 10, 11]
    L1_GROUPS = [[0, 1, 2, 3], [4, 5, 6, 7], [8, 17], [9, 10, 11, 12], [13, 14, 15, 16]]
    gslot = {"i": 0}

    def next_gs():
        g = gslot["i"] % 2
        gslot["i"] += 1
        return g

    first_real_done = False
    for (T0, NT, is_halo) in TILES[:ntiles]:
        nst = NT // 128
        load_x(T0, nst)
        if is_halo:
            slices0 = [(0, 256, [0, 1])]
        else:
            slices0 = [(0, 512, [0, 1, 2, 3]), (512, 512, [4, 5, 6, 7])]
        norm_T(0, list(range(nst)))
        first_real = (not is_halo) and (not first_real_done)
        pending = None
        gs_list = []
        for gi, gid in enumerate(L0_ORDER):
            chunks = L0_GROUPS[gid]
            gs = next_gs()
            load_group(wie, chunks, gs, PID["wie"])
            if gi == 1:
                for kc in range(16):
                    stage_cast(woe[kc], wout[:, kc, :], [("wout", kc)], pid=PID["woe"] + kc)
            if pending is not None:
                pg, pgs = pending
                if pg < 4:
                    l0_group_a(pg, pgs, slices0, first_real)
                else:
                    l0_group_b(pg - 4, pgs, slices0)
            pending = (gid, gs)
        l1_gs = []
        gs = next_gs()
        load_group(wio, L1_GROUPS[2] if is_halo else L1_GROUPS[0], gs, PID["wio"])
        l1_gs.append(gs)
        pg, pgs = pending
        l0_group_b(pg - 4, pgs, slices0)
        if not is_halo:
            first_real_done = True
        l0_sts = [1] if is_halo else list(range(nst))
        pipe_norm = (mode != "L0") and dbg >= 1
        def norm_bias(st_):
            norm_st(1, st_)
            if not is_halo:
                tt("pool", h[:, st_, :], h[:, st_, :], rowc[:, R_BOUT:R_BOUT + 1024], ALU.add, [("h", st_), "rowc"], [("h", st_)])

        for i, st in enumerate(l0_sts):
            out_proj(st, 16, 0)
            if pipe_norm and i >= 1:
                norm_bias(l0_sts[i - 1])
        if pipe_norm:
            norm_bias(l0_sts[-1])
        if mode == "L0":
            if not is_halo:
                for st in range(nst):
                    r0 = T0 - HALO + st * 128
                    P.dma("pool", out_d[r0:r0 + 128, :], h[:, st, :], reads=[("h", st)], writes=[("out", r0)])
            continue
        if dbg < 1:
            continue
        if is_halo:
            if dbg < 2:
                continue
            rope_tables(T0 + 128, 128)
            gs = l1_gs[0]
            if dbg < 3:
                continue
            l1_rot_chunk(gs, 0, C_B + 8, lambda t0, n: (kT[:, 0:128], [("kT", 0)]), [(128, 128, [1], 0)])
            if dbg < 4:
                continue
            l1_v(gs, 1, 1, 0)
            continue
        if dbg < 4.5:
            continue
        rope_tables(T0, NT)
        if dbg < 4.6:
            continue
        slices1 = [(0, 512, [0, 1, 2, 3], 0), (512, 512, [4, 5, 6, 7], 512)]
        import os
        if os.environ.get("SL1") == "a":
            slices1 = [(0, 128, [0], 0)]
        if os.environ.get("SL1") == "b":
            slices1 = [(0, 512, [0, 1, 2, 3], 0)]
        if os.environ.get("SL1") == "c":
            slices1 = [(0, 256, [0, 1], 0)]
        import os
        for gi in range(5):
            if dbg < 5 and dbg < [4.7, 4.8, 4.9, 4.95, 4.97][gi]:
                break
            if gi + 1 < 5:
                gs = next_gs()
                if not os.environ.get("NOLOAD1"):
                    load_group(wio, L1_GROUPS[gi + 1], gs, PID["wio"])
                l1_gs.append(gs)
            if gi == 1:
                for kc in range(8):
                    stage_cast(woo[kc], wout[:, kc, :], [("wout", kc)], pid=PID["woo"] + kc)
            gs = l1_gs[gi]
            if gi in (0, 1):
                for ci in range(int(os.environ.get("ONECI", "4"))):
                    c = gi * 4 + ci
                    l1_rot_chunk(gs, ci, C_B + (0 if os.environ.get("BIAS0") else c),
                                 lambda t0, n, c=c: (big[:, c, t0:t0 + n], [("big", c, t0 // 512)]), slices1)
            elif gi == 2:
                l1_rot_chunk(gs, 0, C_B + 8,
                             lambda t0, n: (kT[:, 128 + t0:128 + t0 + n], [("kT", 1 + t0 // 128 + i) for i in range(n // 128)]),
                             slices1)
                for st in range(nst):
                    l1_v(gs, 1, st, 1 + st)
            elif not os.environ.get("SKIP_Z"):
                for ci in range(4):
                    l1_z_chunk(gs, ci, (gi - 3) * 4 + ci, slices1)
        if dbg < 6:
            continue
        units = [(st, kv) for st in range(nst) for kv in range(2)]
        prev = None
        for ui, (st, kv) in enumerate(units):
            fb = (T0 == HALO and st == 0)
            slots = attn_scores(st, kv, fb)
            if prev is not None:
                pst, pkv, pslots = prev
                if dbg >= 7:
                    attn_pv(pst, pkv, pslots, pst % 2)
                if pkv == 1 and dbg >= 8:
                    attn_finish(pst, pst % 2, T0 - HALO + pst * 128, True)
            prev = (st, kv, slots)
        pst, pkv, pslots = prev
        if dbg >= 7:
            attn_pv(pst, pkv, pslots, pst % 2)
        if dbg >= 8:
            attn_finish(pst, pst % 2, T0 - HALO + pst * 128, True)
        cp("pool", kT[:, 0:128], kT[:, 1024:1152], [("kT", 8)], [("kT", 0)])
        cp("pool", vaug[:, 0, :, :], vaug[:, 8, :, :], [("vaug", 8)], [("vaug", 0)])

    P.emit()
    return nc, P


def _blk(w, cols):
    sub = w[:, cols]
    return np.ascontiguousarray(sub.reshape(8, 128, len(cols)).transpose(1, 0, 2).reshape(128, 8 * len(cols)))


def prepare(inputs):
    f = np.float32
    x = np.asarray(inputs["x"], f)
    pos = np.asarray(inputs["positions"]).astype(np.int32)
    norm_g = np.asarray(inputs["norm_g"], f)
    w_in_even = np.asarray(inputs["w_in_even"], f)[0]
    w_pool = np.asarray(inputs["w_pool"], f)[0]
    pool_scale = np.asarray(inputs["pool_scale"], f)[0]
    conv_w = np.asarray(inputs["conv_w"], f)[0]
    w_out_even = np.asarray(inputs["w_out_even"], f)[0]
    w_in_odd = np.asarray(inputs["w_in_odd"], f)[0]
    b_in_odd = np.asarray(inputs["b_in_odd"], f)[0]
    sinks = np.asarray(inputs["attn_sinks"], f)[0]
    w_out_odd = np.asarray(inputs["w_out_odd"], f)[0]
    b_out_odd = np.asarray(inputs["b_out_odd"], f)[0]
    fg = np.asarray(inputs["final_norm_g"], f)

    ar = np.arange(128)
    cols = []
    for g in range(4):
        cols += [(2 * g) * 128 + ar, (2 * g + 1) * 128 + ar, 4096 + (2 * g) * 128 + ar, 4096 + (2 * g + 1) * 128 + ar]
    for j in range(8):
        cols += [2048 + j * 128 + ar, 3072 + j * 128 + ar, 1024 + j * 128 + ar, 4096 + (8 + j) * 128 + ar]
    wie = np.stack([_blk(w_in_even, c) for c in cols])
    wpl = np.ascontiguousarray(w_pool.reshape(4, 2, 128, 256).transpose(2, 0, 1, 3).reshape(128, 2048))
    wpl = np.ascontiguousarray(wpl.reshape(128, 2, 1024).transpose(1, 0, 2))
    woe = np.ascontiguousarray(w_out_even.reshape(16, 128, 1024))
    perm = np.concatenate([np.concatenate([c * 64 + np.arange(64), (8 + c) * 64 + np.arange(64)]) for c in range(8)])
    rot_il = np.stack([np.arange(8), 8 + np.arange(8)], 1).reshape(-1)
    dimA = np.concatenate([16 + np.arange(48), rot_il])
    dimB = np.concatenate([rot_il, 16 + np.arange(48)])
    ocols = [np.concatenate([c * 64 + dimA, (8 + c) * 64 + dimB]) for c in range(8)]
    ocols += [1024 + np.concatenate([dimA, 64 + dimB])]
    ocols += [1280 + perm[c * 128:(c + 1) * 128] for c in range(8)]
    ocols += [1152 + ar]
    wio = np.stack([_blk(w_in_odd, c) for c in ocols])
    woo = np.ascontiguousarray(w_out_odd[perm].reshape(8, 128, 1024))
    cst = np.zeros((128, NCST), f)
    cst[:, C_G:C_G + 16] = norm_g.reshape(2, 8, 128).transpose(2, 0, 1).reshape(128, 16)
    cst[:, C_PS:C_PS + 8] = pool_scale.reshape(8, 128).T
    cst[:, C_CW:C_CW + 24] = conv_w.reshape(3, 8, 128).transpose(2, 0, 1).reshape(128, 24)
    for i in range(17):
        cst[:, C_B + i] = b_in_odd[ocols[i]]
    inv_freq = (np.float32(500000.0) ** (-np.arange(0, 16, 2, dtype=np.float32) / np.float32(16))).astype(f)
    rotp = (ar >= 48) & (ar < 80)
    jj = ((ar - 48) % 16) // 2
    cst[:, C_INVF] = np.where(rotp, inv_freq[np.clip(jj, 0, 7)], 0.0)
    cst[:, C_SGN] = np.where(rotp, np.where((ar - 48) % 2 == 0, -1.0, 1.0), 0.0)
    cst[:, C_MHALF] = -0.5
    rowc = np.zeros((128, NROW), f)
    rowc[:, R_BOUT:R_BOUT + 1024] = b_out_odd[None]
    rowc[:, R_FG:R_FG + 1024] = fg[None]
    rowc[:, R_BV:R_BV + 128] = b_in_odd[1152:1280][None]
    sl = np.zeros(16, f)
    for c in range(8):
        for k in range(2):
            sl[2 * c + k] = sinks[c + 8 * k]
    rowc[:, R_SINK:R_SINK + 16] = sl[None]
    pm = np.zeros((128, 1024), f)
    for m in range(128):
        dd = m % 64
        if dd < 8:
            pm[m + 8, m] = 1.0
        elif dd < 16:
            pm[m - 8, m] = 1.0
    s_ = np.arange(128)[:, None]
    q_ = np.arange(128)[None, :]
    m_prev = np.where(q_ < s_, 0.0, NEG).astype(f)
    m_diag = np.where(q_ >= s_, 0.0, NEG).astype(f)
    m_all = np.full((128, 128), NEG, f)
    in_maps = []
    for c in range(NCORES):
        b, half = c // 2, c % 2
        t0 = half * TOK
        xe = np.zeros((TL, DM), f)
        pe = np.zeros((TL,), np.int32)
        if half == 0:
            xe[HALO:] = x[b, 0:TOK]
            pe[HALO:] = pos[b, 0:TOK]
        else:
            xe[:] = x[b, t0 - HALO:t0 + TOK]
            pe[:] = pos[b, t0 - HALO:t0 + TOK]
        pc = np.ones((4, 16), f)
        if half == 0:
            for g in range(4):
                w = 2 << g
                pc[g] = w / np.minimum(np.arange(16) + 1, w)
        masks = np.zeros((2, 128, 1024), f)
        masks[0, :, 0:512] = np.tile(m_prev, (1, 4))
        masks[0, :, 512:1024] = np.tile(m_diag, (1, 4))
        masks[1, :, 0:512] = np.tile(m_all if half == 0 else m_prev, (1, 4))
        in_maps.append({
            "x_ext": xe, "pos_bc": np.ascontiguousarray(np.broadcast_to(pe[None], (128, TL))),
            "wie": wie, "wpl": wpl, "woe": woe, "wio": wio, "woo": woo, "cst": cst, "rowc": rowc,
            "pcorr": np.ascontiguousarray(np.broadcast_to(pc.reshape(1, 64), (128, 64))),
            "masks": masks, "pmat": pm,
        })
    return in_maps


_CACHE = {}


def kernel(**inputs):
    mode = "fused"
    if mode not in _CACHE:
        _CACHE[mode] = build(mode)[0]
    nc = _CACHE[mode]
    in_maps = prepare(inputs)
    res = run_bass_kernel_spmd(nc, in_maps, core_ids=list(range(NCORES)))
    out = np.empty((4, 8192, DM), np.float32)
    for c in range(NCORES):
        b, half = c // 2, c % 2
        out[b, half * TOK:(half + 1) * TOK] = res.results[c]["out"]
    return out
```

```python
from contextlib import ExitStack
import numpy as np
import concourse.bass as bass
import concourse.mybir as mybir
from concourse.bass_utils import run_bass_kernel_spmd

F32 = mybir.dt.float32
BF16 = mybir.dt.bfloat16
I32 = mybir.dt.int32
AF = mybir.ActivationFunctionType
ALU = mybir.AluOpType
AX = mybir.AxisListType


class _I:
    __slots__ = ("eng", "fn", "kind", "deps", "signal", "sig", "idx", "rawdeps", "line", "rw")


class Prog:
    NDMA = 30
    SAME_ENGINE_RAW = True

    def __init__(self, nc):
        self.nc = nc
        self.stack = ExitStack()
        self.instrs = []
        self.lw = {}
        self.rd = {}
        self.trace = None

    def sbuf(self, name, shape, dtype):
        return self.stack.enter_context(self.nc.sbuf_tensor(name, shape, dtype))

    def psum(self, name, shape, dtype):
        return self.stack.enter_context(self.nc.psum_tensor(name, shape, dtype))

    def _add(self, eng, fn, reads, writes, kind):
        ins = _I()
        ins.eng, ins.fn, ins.kind = eng, fn, kind
        ins.idx = len(self.instrs)
        import sys as _sys
        f = _sys._getframe(2)
        ln = []
        while f is not None and len(ln) < 4:
            ln.append(str(f.f_lineno))
            f = f.f_back
        ins.line = "<".join(ln)
        ins.rw = (reads, writes)
        ins.signal = False
        ins.sig = None
        deps = {}
        for k in reads:
            w = self.lw.get(k)
            if w is not None:
                deps[w.idx] = (w, True)
        for k in writes:
            w = self.lw.get(k)
            if w is not None and w.idx not in deps:
                deps[w.idx] = (w, False)
            for r in self.rd.get(k, ()):
                if r.idx not in deps:
                    deps[r.idx] = (r, False)
        out = []
        for d, raw in deps.values():
            if d.kind == "op" and d.eng == eng and kind == "op":
                if eng == "tensor" or not self.SAME_ENGINE_RAW:
                    continue
            out.append(d)
        ins.deps = out
        for k in reads:
            lst = self.rd.setdefault(k, [])
            if kind == "op":
                lst[:] = [r for r in lst if not (r.kind == "op" and r.eng == eng)]
            lst.append(ins)
        for k in writes:
            self.lw[k] = ins
            self.rd[k] = []
        self.instrs.append(ins)
        return ins

    def op(self, eng, fn, reads=(), writes=()):
        return self._add(eng, fn, tuple(reads), tuple(writes), "op")

    def dma(self, eng, out, in_, reads=(), writes=(), is_output=False):
        return self._add(eng, lambda e: e.dma_start(out=out, in_=in_), tuple(reads), tuple(writes), "dma")

    def emit(self):
        nc = self.nc
        st = self.stack
        engs = ["tensor", "vector", "scalar", "pool", "sp"]
        esem = {e: st.enter_context(nc.semaphore("sem_" + e)) for e in engs}
        dsem = [st.enter_context(nc.semaphore("dsem%d" % i)) for i in range(self.NDMA)]
        for ins in self.instrs:
            for d in ins.deps:
                d.signal = True
        cnt = {e: 0 for e in engs}
        dcnt = [0] * self.NDMA
        dlast = [None] * self.NDMA
        pools = {"sp": list(range(0, self.NDMA - 14)), "pool": list(range(self.NDMA - 14, self.NDMA - 8)),
                 "scalar": list(range(self.NDMA - 8, self.NDMA))}
        nd = {"sp": 0, "pool": 0, "scalar": 0}
        for ins in self.instrs:
            if ins.kind == "dma":
                pl = pools[ins.eng]
                s = pl[nd[ins.eng] % len(pl)]
                nd[ins.eng] += 1
                if dlast[s] is not None:
                    ins.deps.append(dlast[s])
                dcnt[s] += 16
                ins.sig = (dsem[s], dcnt[s])
                dlast[s] = ins
                ins.signal = True
            elif ins.signal:
                cnt[ins.eng] += 1
                ins.sig = (esem[ins.eng], cnt[ins.eng])
        per = {e: [i for i in self.instrs if i.eng == e] for e in engs}
        self.stats = {e: len(per[e]) for e in engs}
        self.stats["signals"] = dict(cnt)
        block = st.enter_context(nc.Block())

        def run(e, lst, final):
            waited = {}
            for ins in lst:
                if self.trace is not None:
                    self.trace.append((ins.eng, ins.idx, ins.kind, ins.line, [(d.eng, d.idx, d.sig[1]) for d in ins.deps], ins.sig[1] if ins.sig else None, ins.rw))
                need = {}
                for d in ins.deps:
                    sem, val = d.sig
                    key = id(sem)
                    if waited.get(key, 0) >= val:
                        continue
                    if key not in need or need[key][1] < val:
                        need[key] = (sem, val)
                for key, (sem, val) in need.items():
                    e.wait_ge(sem, val)
                    waited[key] = val
                r = ins.fn(e)
                if ins.signal:
                    sem, val = ins.sig
                    r.then_inc(sem, 16 if ins.kind == "dma" else 1)
            if final:
                for s in range(self.NDMA):
                    if dcnt[s] and waited.get(id(dsem[s]), 0) < dcnt[s]:
                        e.wait_ge(dsem[s], dcnt[s])

        @block.tensor
        def _(e):
            run(e, per["tensor"], False)

        @block.vector
        def _(e):
            run(e, per["vector"], False)

        @block.scalar
        def _(e):
            run(e, per["scalar"], True)

        @block.gpsimd
        def _(e):
            run(e, per["pool"], True)

        @block.sync
        def _(e):
            run(e, per["sp"], True)

        st.close()


NCORES = 8
DM = 1024
TOK = 4096
HALO = 256
TL = TOK + HALO
EPS = 1e-5
NEG = -30000.0
TWO_PI = 6.283185307179586
C1 = 6.28125
C2 = TWO_PI - C1
PI_SAFE = 3.1415925
TILES = [(0, 256, True)] + [(HALO + 1024 * i, 1024, False) for i in range(4)]

C_G = 0
C_PS = 16
C_CW = 24
C_B = 48
C_INVF = 65
C_SGN = 66
C_MHALF = 67
NCST = 68
R_BOUT = 0
R_FG = 1024
R_BV = 2048
R_SINK = 2176
NROW = 2192


def build(mode="fused", ntiles=5, dbg=99):
    nc = bass.Bass("TRN2", target_bir_lowering=False)
    P = Prog(nc)

    def din(name, shape, dt=F32):
        return nc.dram_tensor(name, shape, dt, kind="ExternalInput").ap()

    x_ext = din("x_ext", [TL, DM])
    pos_bc = din("pos_bc", [128, TL], I32)
    wie = din("wie", [48, 128, 1024])
    wpl = din("wpl", [2, 128, 1024])
    woe = din("woe", [16, 128, 1024])
    wio = din("wio", [18, 128, 1024])
    woo = din("woo", [8, 128, 1024])
    cst_d = din("cst", [128, NCST])
    rowc_d = din("rowc", [128, NROW])
    pcorr_d = din("pcorr", [128, 64])
    masks_d = din("masks", [2, 128, 1024])
    pmat_d = din("pmat", [128, 1024])
    out_d = nc.dram_tensor("out", [TOK, DM], F32, kind="ExternalOutput").ap()
    wscr = nc.dram_tensor("wscr", [90, 128, 1024], BF16, kind="Internal").ap()
    PID = {"wie": 0, "woe": 48, "wio": 64, "woo": 82}

    NSTG = 3
    h = P.sbuf("h", [128, 8, 1024], F32)
    ybuf = P.sbuf("ybuf", [128, 2, 1024], BF16)
    yT = P.sbuf("yT", [128, 8, 1024], BF16)
    big = P.sbuf("big", [128, 16, 1024], BF16)
    stg = P.sbuf("stg", [128, NSTG, 1024], F32)
    wg = P.sbuf("wg", [128, 2, 4, 1024], BF16)
    wout = P.sbuf("wout", [128, 16, 1024], BF16)
    wpool = P.sbuf("wpool", [128, 2, 1024], BF16)
    cst = P.sbuf("cstt", [128, NCST], F32)
    rowc = P.sbuf("rowct", [128, NROW], F32)
    pcorr = P.sbuf("pcorrt", [128, 64], F32)
    maskb = P.sbuf("maskb", [128, 3, 512], BF16)
    identf = P.sbuf("identf", [128, 128], F32)
    identb = P.sbuf("identb", [128, 128], BF16)
    esink = P.sbuf("esink", [128, 16], F32)
    ucar = P.sbuf("ucar", [128, 8, 16], F32)
    ccar = P.sbuf("ccar", [128, 8, 2], F32)
    NS = 8
    S = P.sbuf("S", [128, NS, 528], F32)
    pooled = P.sbuf("pooled", [128, 2, 512], BF16)
    xbq = P.sbuf("xbq", [128, 2, 512], BF16)
    kT = P.sbuf("kT", [128, 128 + 1024], BF16)
    vaug = P.sbuf("vaug", [128, 9, 2, 65], BF16)
    cosT = P.sbuf("cosT", [128, 1024], F32)
    sinT = P.sbuf("sinT", [128, 1024], F32)
    NPT = 8
    PT = P.sbuf("PT", [128, NPT, 512], BF16)
    attn = P.sbuf("attn", [128, 2, 1024], BF16)
    ss = P.sbuf("ss", [128, 32], F32)
    rstd = P.sbuf("rstd", [128, 32], F32)
    small = P.sbuf("small", [128, 16], F32)
    ps = [P.psum("ps%d" % i, [128, 512], F32) for i in range(8)]
    junk = xbq[:].rearrange("p a b -> p (a b)")
    JK = [("xbq", 0), ("xbq", 1)]

    state = {"bank": 0, "stg": 0, "S": 0, "pt": 0, "ssi": 0, "xs": 0, "ysl": 0}

    def nbank():
        b = state["bank"] % 8
        state["bank"] += 1
        return b

    def nS():
        s = state["S"] % NS
        state["S"] += 1
        return s

    def col(i):
        return cst[:, i:i + 1]

    def mm(out, lhsT, rhs, start, stop, reads, writes):
        P.op("tensor", lambda e: e.matmul(out=out, lhsT=lhsT, rhs=rhs, start=start, stop=stop),
             reads, writes)

    def tr(out, in_, reads, writes):
        P.op("tensor", lambda e: e.transpose(out=out, in_=in_, identity=identb[:]), list(reads) + ["ident"], writes)

    def act(out, in_, func, reads, writes, scale=None, bias=None, accum=None):
        kw = {}
        if scale is not None:
            kw["scale"] = scale
        if bias is not None:
            kw["bias"] = bias
        if accum is not None:
            kw["accum_out"] = accum
        P.op("scalar", lambda e: e.activation(out=out, in_=in_, func=func, **kw), reads, writes)

    def tt(eng, out, in0, in1, op, reads, writes):
        P.op(eng, lambda e: e.tensor_tensor(out=out, in0=in0, in1=in1, op=op), reads, writes)

    def stt(out, in0, scalar, in1, op0, op1, reads, writes):
        P.op("vector", lambda e: e.scalar_tensor_tensor(out=out, in0=in0, scalar=scalar, in1=in1, op0=op0, op1=op1),
             reads, writes)

    def ts(eng, out, in0, s1, op0, reads, writes, s2=None, op1=None):
        if op1 is None:
            P.op(eng, lambda e: e.tensor_scalar(out=out, in0=in0, scalar1=s1, scalar2=None, op0=op0), reads, writes)
        else:
            P.op(eng, lambda e: e.tensor_scalar(out=out, in0=in0, scalar1=s1, scalar2=s2, op0=op0, op1=op1), reads, writes)

    def cp(eng, out, in_, reads, writes):
        if eng == "scalar":
            P.op(eng, lambda e: e.copy(out=out, in_=in_), reads, writes)
        else:
            P.op(eng, lambda e: e.tensor_copy(out=out, in_=in_), reads, writes)

    cast_rr = {"i": 0}
    CAST_ENGS = ["scalar", "scalar", "pool"]

    cached = set()

    def stage_cast(dram_ap, dst_ap, dst_keys, ncols=1024, eng=None, pid=None):
        if pid is not None and pid in cached:
            P.dma("sp", dst_ap, wscr[pid], reads=[("scr", pid)], writes=dst_keys)
            return
        stage_cast_(dram_ap, dst_ap, dst_keys, ncols, eng)
        if pid is not None:
            P.dma("pool", wscr[pid], dst_ap, reads=dst_keys, writes=[("scr", pid)])
            cached.add(pid)

    def stage_cast_(dram_ap, dst_ap, dst_keys, ncols=1024, eng=None):
        s = state["stg"] % NSTG
        state["stg"] += 1
        P.dma("sp", stg[:, s, 0:ncols], dram_ap, reads=[], writes=[("stg", s)])
        if eng is None:
            eng = CAST_ENGS[cast_rr["i"] % len(CAST_ENGS)]
            cast_rr["i"] += 1
        cp(eng, dst_ap, stg[:, s, 0:ncols], [("stg", s)], dst_keys)

    P.dma("sp", cst[:], cst_d, writes=["cst"])
    P.dma("sp", rowc[:], rowc_d, writes=["rowc"])
    P.dma("sp", pcorr[:], pcorr_d, writes=["pcorr"])
    P.op("pool", lambda e: e.memset(identf[:], 0.0), writes=["identf"])
    P.op("pool", lambda e: e.affine_select(out=identf[:], in_=identf[:], pattern=[[-1, 128]],
                                           compare_op=ALU.not_equal, fill=1.0, base=0, channel_multiplier=1),
         reads=["identf"], writes=["identf"])
    cp("vector", identb[:], identf[:], ["identf"], ["ident"])
    P.op("pool", lambda e: e.memset(ucar[:], 0.0), writes=["ucar%d" % i for i in range(8)])
    P.op("pool", lambda e: e.memset(ccar[:], 0.0), writes=["ccar%d" % i for i in range(8)])
    P.op("vector", lambda e: e.memset(vaug[:, :, :, 64:65], 1.0), writes=[("vaug", i) for i in range(9)])
    for i in range(2):
        stage_cast(wpl[i], wpool[:, i, :], [("wpool", i)], eng="vector")
    stage_cast(masks_d[0], maskb[:, 0:2, :].rearrange("p a b -> p (a b)"), ["mask"], eng="vector")
    stage_cast(masks_d[1][:, 0:512], maskb[:, 2, :], ["mask2"], ncols=512, eng="vector")
    P.op("vector", lambda e: e.memset(S[:], 0.0), writes=[("S", i) for i in range(NS)])
    act(esink[:], rowc[:, R_SINK:R_SINK + 16], AF.Exp, ["rowc"], ["esink"])

    def load_x(t0, nst):
        for st in range(nst):
            P.dma("sp", h[:, st, :], x_ext[t0 + st * 128: t0 + (st + 1) * 128, :], writes=[("h", st)])

    def norm_T(layer, sts):
        base = state["ssi"]
        state["ssi"] = (state["ssi"] + 8) % 24
        for st in sts:
            act(ybuf[:, st % 2, :], h[:, st, :], AF.Square, [("h", st)], [("ybuf", st % 2), ("ss", base + st)],
                accum=ss[:, base + st: base + st + 1])
        lo, hi = base + sts[0], base + sts[-1] + 1
        keys_ss = [("ss", base + st) for st in sts]
        keys_r = [("rstd", base + st) for st in sts]
        ts("pool", rstd[:, lo:hi], ss[:, lo:hi], 1.0 / DM, ALU.mult, keys_ss, keys_r, s2=EPS, op1=ALU.add)
        tt("pool", rstd[:, lo:hi], rstd[:, lo:hi], cst[:, C_MHALF:C_MHALF + 1].to_broadcast([128, hi - lo]),
           ALU.pow, keys_r + ["cst"], keys_r)
        for st in sts:
            sl = st % 2
            act(ybuf[:, sl, :], h[:, st, :], AF.Copy, [("h", st), ("rstd", base + st)], [("ybuf", sl)],
                scale=rstd[:, base + st: base + st + 1])
            b = nbank()
            pb = ps[b][:].bitcast(BF16)
            for kc in range(8):
                tr(pb[:, kc * 128:(kc + 1) * 128], ybuf[:, sl, kc * 128:(kc + 1) * 128], [("ybuf", sl)], [("ps", b)])
            tt("vector", yT[:, :, st * 128:(st + 1) * 128], pb.rearrange("p (k t) -> p k t", k=8),
               cst[:, C_G + layer * 8: C_G + layer * 8 + 8].unsqueeze(2).to_broadcast([128, 8, 128]), ALU.mult,
               [("ps", b), "cst"], [("yT", st)])

    def norm_st(layer, st):
        i = state["ssi"]
        state["ssi"] = (state["ssi"] + 1) % 24
        sl = state["ysl"] % 2
        state["ysl"] += 1
        act(ybuf[:, sl, :], h[:, st, :], AF.Square, [("h", st)], [("ybuf", sl), ("ss", i)], accum=ss[:, i:i + 1])
        ts("pool", rstd[:, i:i + 1], ss[:, i:i + 1], 1.0 / DM, ALU.mult, [("ss", i)], [("rstd", i)], s2=EPS, op1=ALU.add)
        tt("pool", rstd[:, i:i + 1], rstd[:, i:i + 1], cst[:, C_MHALF:C_MHALF + 1], ALU.pow, [("rstd", i), "cst"], [("rstd", i)])
        act(ybuf[:, sl, :], h[:, st, :], AF.Copy, [("h", st), ("rstd", i)], [("ybuf", sl)], scale=rstd[:, i:i + 1])
        b = nbank()
        pb = ps[b][:].bitcast(BF16)
        for kc in range(8):
            tr(pb[:, kc * 128:(kc + 1) * 128], ybuf[:, sl, kc * 128:(kc + 1) * 128], [("ybuf", sl)], [("ps", b)])
        tt("vector", yT[:, :, st * 128:(st + 1) * 128], pb.rearrange("p (k t) -> p k t", k=8),
           cst[:, C_G + layer * 8: C_G + layer * 8 + 8].unsqueeze(2).to_broadcast([128, 8, 128]), ALU.mult,
           [("ps", b), "cst"], [("yT", st)])

    def load_group(src, chunk_ids, gs, base):
        for ci, ch in enumerate(chunk_ids):
            stage_cast(src[ch], wg[:, gs, ci, :], [("wg", gs, ci)], pid=base + ch)

    def proj(gs, ci, t0, n, sts):
        b = nbank()
        w = wg[:, gs, ci, :].rearrange("p (k n) -> p k n", k=8)
        for kc in range(8):
            mm(ps[b][:, 0:n], w[:, kc, :], yT[:, kc, t0:t0 + n], kc == 0, kc == 7,
               [("wg", gs, ci)] + [("yT", st) for st in sts], [("ps", b)])
        return b

    def l0_group_a(g, gs, slices, first_real):
        w = 2 << g
        for sl, (t0, n, sts) in enumerate(slices):
            bu = [proj(gs, 0, t0, n, sts), proj(gs, 1, t0, n, sts)]
            bz = [proj(gs, 2, t0, n, sts), proj(gs, 3, t0, n, sts)]
            zs = []
            for i in range(2):
                c = 2 * g + i
                ub, ta, tb = nS(), nS(), nS()
                kc_ = "ucar%d" % c
                cp("pool", S[:, ub, 0:16], ucar[:, c, :], [kc_], [("S", ub)])
                cp("scalar", S[:, ub, 16:16 + n], ps[bu[i]][:, 0:n], [("ps", bu[i])], [("S", ub)])
                cp("pool", ucar[:, c, :], S[:, ub, n:n + 16], [("S", ub)], [kc_])
                src = ub
                lvl = 1
                dst = ta
                while (1 << lvl) <= w:
                    sh = 1 << (lvl - 1)
                    lo = 16 - (16 - (1 << lvl)) if (1 << lvl) < 16 else 16
                    lo = (1 << lvl)
                    tt("vector", S[:, dst, lo:16 + n], S[:, src, lo:16 + n], S[:, src, lo - sh:16 + n - sh], ALU.add,
                       [("S", src)], [("S", dst)])
                    src = dst
                    dst = tb if dst == ta else ta
                    lvl += 1
                if first_real and sl == 0:
                    tt("vector", S[:, src, 16:32], S[:, src, 16:32], pcorr[:, g * 16:(g + 1) * 16], ALU.mult,
                       [("S", src), "pcorr"], [("S", src)])
                stt(pooled[:, i, 0:n], S[:, src, 16:16 + n], 1.0 / w, S[:, ub, 16:16 + n], ALU.mult, ALU.subtract,
                    [("S", src), ("S", ub)], [("pooled", i)])
                z = nS()
                act(S[:, z, 0:n], ps[bz[i]][:, 0:n], AF.Silu, [("ps", bz[i])], [("S", z)])
                zs.append(z)
            for oc in range(2):
                c = 2 * g + oc
                b = nbank()
                for kc in range(2):
                    o = (g % 2) * 512 + kc * 256 + oc * 128
                    mm(ps[b][:, 0:n], wpool[:, g // 2, o:o + 128], pooled[:, kc, 0:n], kc == 0, kc == 1,
                       [("wpool", g // 2), ("pooled", 0), ("pooled", 1)], [("ps", b)])
                stt(big[:, c, t0:t0 + n], ps[b][:, 0:n], col(C_PS + c), S[:, zs[oc], 0:n], ALU.mult, ALU.mult,
                    [("ps", b), "cst", ("S", zs[oc])], [("big", c, t0 // 512)])

    def l0_group_b(j, gs, slices):
        for sl, (t0, n, sts) in enumerate(slices):
            bgc = proj(gs, 0, t0, n, sts)
            bhc = proj(gs, 1, t0, n, sts)
            bgb = proj(gs, 2, t0, n, sts)
            bzb = proj(gs, 3, t0, n, sts)
            hc, cu, v0, v1, zz, gz = nS(), nS(), nS(), nS(), nS(), nS()
            kc_ = "ccar%d" % j
            cp("scalar", S[:, hc, 0:n], ps[bhc][:, 0:n], [("ps", bhc)], [("S", hc)])
            cp("pool", S[:, cu, 0:2], ccar[:, j, :], [kc_], [("S", cu)])
            tt("vector", S[:, cu, 2:2 + n], ps[bgc][:, 0:n], S[:, hc, 0:n], ALU.mult, [("ps", bgc), ("S", hc)], [("S", cu)])
            cp("pool", ccar[:, j, :], S[:, cu, n:n + 2], [("S", cu)], [kc_])
            act(S[:, v0, 0:n], S[:, cu, 2:2 + n], AF.Copy, [("S", cu), "cst"], [("S", v0)], scale=col(C_CW + 16 + j))
            stt(S[:, v1, 0:n], S[:, cu, 1:1 + n], col(C_CW + 8 + j), S[:, v0, 0:n], ALU.mult, ALU.add,
                [("S", cu), ("S", v0), "cst"], [("S", v1)])
            stt(S[:, v0, 0:n], S[:, cu, 0:n], col(C_CW + j), S[:, v1, 0:n], ALU.mult, ALU.add,
                [("S", cu), ("S", v1), "cst"], [("S", v0)])
            act(S[:, zz, 0:n], ps[bzb][:, 0:n], AF.Silu, [("ps", bzb)], [("S", zz)])
            tt("vector", S[:, gz, 0:n], ps[bgb][:, 0:n], S[:, zz, 0:n], ALU.mult, [("ps", bgb), ("S", zz)], [("S", gz)])
            tt("pool", big[:, 8 + j, t0:t0 + n], S[:, gz, 0:n], S[:, v0, 0:n], ALU.mult, [("S", gz), ("S", v0)],
               [("big", 8 + j, t0 // 512)])

    def out_proj(st, nkc, coff):
        for nh in range(2):
            b = nbank()
            for kc in range(nkc):
                mm(ps[b][:, 0:512], big[:, coff + kc, st * 128:(st + 1) * 128], wout[:, kc, nh * 512:(nh + 1) * 512],
                   kc == 0, kc == nkc - 1, [("big", coff + kc, st // 4), ("wout", kc)], [("ps", b)])
            tt("vector", h[:, st, nh * 512:(nh + 1) * 512], ps[b][:, 0:512], h[:, st, nh * 512:(nh + 1) * 512], ALU.add,
               [("ps", b), ("h", st)], [("h", st)])

    def rope_tables(t0, n):
        a, k_, r, m = nS(), nS(), nS(), nS()
        posi = S[:, a, :].bitcast(I32)
        for hh in range(0, n, 512):
            nn = min(512, n - hh)
            ki = S[:, k_, 0:nn].bitcast(I32)
            P.dma("sp", posi[:, 0:nn], pos_bc[:, t0 + hh:t0 + hh + nn], writes=[("S", a)])
            ts("vector", S[:, r, 0:nn], posi[:, 0:nn], col(C_INVF), ALU.mult, [("S", a), "cst"], [("S", r)])
            ts("vector", ki, S[:, r, 0:nn], 1.0 / TWO_PI, ALU.mult, [("S", r)], [("S", k_)])
            stt(S[:, m, 0:nn], ki, -C1, S[:, r, 0:nn], ALU.mult, ALU.add, [("S", k_), ("S", r)], [("S", m)])
            stt(S[:, r, 0:nn], ki, -C2, S[:, m, 0:nn], ALU.mult, ALU.add, [("S", k_), ("S", m)], [("S", r)])

            def wrap(buf, tmp):
                ts("vector", S[:, tmp, 0:nn], S[:, buf, 0:nn], PI_SAFE, ALU.is_gt, [("S", buf)], [("S", tmp)], s2=-TWO_PI, op1=ALU.mult)
                tt("vector", S[:, buf, 0:nn], S[:, buf, 0:nn], S[:, tmp, 0:nn], ALU.add, [("S", buf), ("S", tmp)], [("S", buf)])
                ts("vector", S[:, tmp, 0:nn], S[:, buf, 0:nn], -PI_SAFE, ALU.is_lt, [("S", buf)], [("S", tmp)], s2=TWO_PI, op1=ALU.mult)
                tt("vector", S[:, buf, 0:nn], S[:, buf, 0:nn], S[:, tmp, 0:nn], ALU.add, [("S", buf), ("S", tmp)], [("S", buf)])
                ts("vector", S[:, buf, 0:nn], S[:, buf, 0:nn], PI_SAFE, ALU.min, [("S", buf)], [("S", buf)], s2=-PI_SAFE, op1=ALU.max)

            wrap(r, m)
            act(sinT[:, hh:hh + nn], S[:, r, 0:nn], AF.Sin, [("S", r), "cst"], [("sinT", hh // 512)], scale=col(C_SGN))
            ts("vector", S[:, r, 0:nn], S[:, r, 0:nn], 1.5707963267948966, ALU.add, [("S", r)], [("S", r)])
            wrap(r, m)
            act(cosT[:, hh:hh + nn], S[:, r, 0:nn], AF.Sin, [("S", r)], [("cosT", hh // 512)])

    def l1_rot_chunk(gs, ci, bias_col, dst_fn, slices):
        for sl, (t0, n, sts, tb) in enumerate(slices):
            b = proj(gs, ci, t0, n, sts)
            r, xp = nS(), nS()
            t1 = r
            act(S[:, r, 0:n], ps[b][:, 0:n], AF.Identity, [("ps", b), "cst"], [("S", r)], bias=col(bias_col))
            def rows2(buf, p0):
                a_ = S[p0:p0 + 1, buf, 0:n]
                pst = a_.ap[0][0]
                return type(a_)(tensor=a_.tensor, offset=a_.offset, ap=[[2 * pst, 16], [1, n]])
            P.dma("scalar", rows2(xp, 48), rows2(r, 49), reads=[("S", r)], writes=[("S", xp)])
            P.dma("sp", rows2(xp, 49), rows2(r, 48), reads=[("S", r)], writes=[("S", xp)])
            stt(S[:, t1, 0:n], ps[b][:, 0:n], col(bias_col), cosT[:, tb:tb + n], ALU.add, ALU.mult,
                [("ps", b), "cst", ("cosT", tb // 512)], [("S", t1)])
            tt("pool", S[:, xp, 0:n], S[:, xp, 0:n], sinT[:, tb:tb + n], ALU.mult, [("S", xp), ("sinT", tb // 512)], [("S", xp)])
            dst, dkeys = dst_fn(t0, n)
            tt("vector", dst, S[:, t1, 0:n], S[:, xp, 0:n], ALU.add, [("S", t1), ("S", xp)], dkeys)

    def l1_z_chunk(gs, ci, c, slices):
        for sl, (t0, n, sts, tb) in enumerate(slices):
            b = proj(gs, ci, t0, n, sts)
            act(big[:, 8 + c, t0:t0 + n], ps[b][:, 0:n], AF.Silu, [("ps", b), "cst"], [("big", 8 + c, t0 // 512)],
                bias=col(C_B + 9 + c))

    def l1_v(gs, ci, st, blk):
        b = nbank()
        w = wg[:, gs, ci, :].rearrange("p (k n) -> p k n", k=8)
        for kc in range(8):
            mm(ps[b][:, 0:128], yT[:, kc, st * 128:(st + 1) * 128], w[:, kc, :], kc == 0, kc == 7,
               [("wg", gs, ci), ("yT", st)], [("ps", b)])
        tt("vector", vaug[:, blk, :, 0:64], ps[b][:, 0:128].rearrange("p (a d) -> p a d", a=2),
           rowc[:, R_BV:R_BV + 128].rearrange("p (a d) -> p a d", a=2), ALU.add, [("ps", b), "rowc"], [("vaug", blk)])

    def attn_scores(st, kv, first_block):
        slots = {}
        for kb in range(2):
            kcol = (st + kb) * 128
            for hg in range(2):
                b = nbank()
                mm(ps[b][:, 0:512], kT[kv * 64:(kv + 1) * 64, kcol:kcol + 128],
                   big[kv * 64:(kv + 1) * 64, hg * 4:(hg + 1) * 4, st * 128:(st + 1) * 128], True, False,
                   [("kT", st + kb)] + [("big", c, st // 4) for c in range(hg * 4, hg * 4 + 4)], [("ps", b)])
                mi = 1 if kb == 1 else (2 if first_block else 0)
                mm(ps[b][:, 0:512], identb[:], maskb[:, mi, :], False, True, ["ident", "mask", "mask2"], [("ps", b)])
                s = state["pt"] % NPT
                state["pt"] += 1
                act(PT[:, s, :], ps[b][:, 0:512], AF.Exp, [("ps", b)], [("PT", s)], scale=0.125)
                slots[(kb, hg)] = s
        return slots

    def attn_pv(st, kv, slots, asl):
        for hg in range(2):
            b = nbank()
            for hl in range(4):
                for kb in range(2):
                    s = slots[(kb, hg)]
                    mm(ps[b][:, hl * 128: hl * 128 + 65], PT[:, s, hl * 128:(hl + 1) * 128], vaug[:, st + kb, kv, :],
                       kb == 0, kb == 1, [("PT", s), ("vaug", st + kb)], [("ps", b)])
            pv = ps[b][:].rearrange("p (a d) -> p a d", a=4)
            es = esink[:].rearrange("p (c k) -> p c k", k=2)[:, hg * 4:(hg + 1) * 4, kv:kv + 1]
            sm = small[:, (kv * 2 + hg) * 4:(kv * 2 + hg) * 4 + 4]
            kk = ("small", kv * 2 + hg)
            tt("vector", sm.unsqueeze(2), pv[:, :, 64:65], es, ALU.add, [("ps", b), "esink"], [kk])
            P.op("vector", lambda e, sm=sm: e.reciprocal(out=sm, in_=sm), [kk], [kk])
            dst = attn[:, asl, :].rearrange("p (c k d) -> p c k d", c=8, k=2)[:, hg * 4:(hg + 1) * 4, kv, :]
            tt("vector", dst, pv[:, :, 0:64], sm.unsqueeze(2).to_broadcast([128, 4, 64]), ALU.mult,
               [("ps", b), kk], [("attn", asl, kv, hg)])

    def attn_finish(st, asl, out_row0, final, part="ab"):
        b = nbank()
        pb = ps[b][:].bitcast(BF16)
        akeys = [("attn", asl, kv, hg) for kv in range(2) for hg in range(2)]
        for c in range(8):
            tr(pb[:, c * 128:(c + 1) * 128], attn[:, asl, c * 128:(c + 1) * 128], akeys, [("ps", b)])
        gk = [("big", 8 + c, st // 4) for c in range(8)]
        tt("vector", big[:, 8:16, st * 128:(st + 1) * 128], pb.rearrange("p (k t) -> p k t", k=8),
           big[:, 8:16, st * 128:(st + 1) * 128], ALU.mult, [("ps", b)] + gk, gk)
        if part == "a":
            return
        attn_finish_b(st, out_row0)

    def attn_finish_b(st, out_row0):
        out_proj(st, 8, 8)
        i = state["ssi"]
        state["ssi"] = (state["ssi"] + 1) % 24
        act(junk, h[:, st, :], AF.Square, [("h", st)], JK + [("ss", i)], accum=ss[:, i:i + 1])
        ts("pool", rstd[:, i:i + 1], ss[:, i:i + 1], 1.0 / DM, ALU.mult, [("ss", i)], [("rstd", i)], s2=EPS, op1=ALU.add)
        tt("pool", rstd[:, i:i + 1], rstd[:, i:i + 1], cst[:, C_MHALF:C_MHALF + 1], ALU.pow, [("rstd", i), "cst"], [("rstd", i)])
        stt(h[:, st, :], h[:, st, :], rstd[:, i:i + 1], rowc[:, R_FG:R_FG + 1024], ALU.mult, ALU.mult,
            [("h", st), ("rstd", i), "rowc"], [("h", st)])
        P.dma("pool", out_d[out_row0: out_row0 + 128, :], h[:, st, :], reads=[("h", st)], writes=[("out", out_row0)])

    A_GROUPS = [[4 * g + i for i in range(4)] for g in range(4)]
    B_GROUPS = [[16 + 4 * j + i for i in range(4)] for j in range(8)]
    L0_GROUPS = A_GROUPS + B_GROUPS
    L0_ORDER = [3, 4, 2, 5, 1, 6, 0, 7, 8, 9, 10, 11]
    L1_GROUPS = [[0, 1, 2, 3], [4, 5, 6, 7], [8, 17], [9, 10, 11, 12], [13, 14, 15, 16]]
    gslot = {"i": 0}

    def next_gs():
        g = gslot["i"] % 2
        gslot["i"] += 1
        return g

    first_real_done = False
    for (T0, NT, is_halo) in TILES[:ntiles]:
        nst = NT // 128
        load_x(T0, nst)
        if is_halo:
            slices0 = [(0, 256, [0, 1])]
        else:
            slices0 = [(0, 512, [0, 1, 2, 3]), (512, 512, [4, 5, 6, 7])]
        norm_T(0, list(range(nst)))
        first_real = (not is_halo) and (not first_real_done)
        pending = None
        gs_list = []
        for gi, gid in enumerate(L0_ORDER):
            chunks = L0_GROUPS[gid]
            gs = next_gs()
            load_group(wie, chunks, gs, PID["wie"])
            if gi == 1:
                for kc in range(16):
                    stage_cast(woe[kc], wout[:, kc, :], [("wout", kc)], pid=PID["woe"] + kc)
            if pending is not None:
                pg, pgs = pending
                if pg < 4:
                    l0_group_a(pg, pgs, slices0, first_real)
                else:
                    l0_group_b(pg - 4, pgs, slices0)
            pending = (gid, gs)
        l1_gs = []
        gs = next_gs()
        load_group(wio, L1_GROUPS[2] if is_halo else L1_GROUPS[0], gs, PID["wio"])
        l1_gs.append(gs)
        pg, pgs = pending
        l0_group_b(pg - 4, pgs, slices0)
        if not is_halo:
            first_real_done = True
        l0_sts = [1] if is_halo else list(range(nst))
        pipe_norm = (mode != "L0") and dbg >= 1
        def norm_bias(st_):
            norm_st(1, st_)
            if not is_halo:
                tt("pool", h[:, st_, :], h[:, st_, :], rowc[:, R_BOUT:R_BOUT + 1024], ALU.add, [("h", st_), "rowc"], [("h", st_)])

        for i, st in enumerate(l0_sts):
            out_proj(st, 16, 0)
            if pipe_norm and i >= 1:
                norm_bias(l0_sts[i - 1])
        if pipe_norm:
            norm_bias(l0_sts[-1])
        if mode == "L0":
            if not is_halo:
                for st in range(nst):
                    r0 = T0 - HALO + st * 128
                    P.dma("pool", out_d[r0:r0 + 128, :], h[:, st, :], reads=[("h", st)], writes=[("out", r0)])
            continue
        if dbg < 1:
            continue
        if is_halo:
            if dbg < 2:
                continue
            rope_tables(T0 + 128, 128)
            gs = l1_gs[0]
            if dbg < 3:
                continue
            l1_rot_chunk(gs, 0, C_B + 8, lambda t0, n: (kT[:, 0:128], [("kT", 0)]), [(128, 128, [1], 0)])
            if dbg < 4:
                continue
            l1_v(gs, 1, 1, 0)
            continue
        if dbg < 4.5:
            continue
        rope_tables(T0, NT)
        if dbg < 4.6:
            continue
        slices1 = [(0, 512, [0, 1, 2, 3], 0), (512, 512, [4, 5, 6, 7], 512)]
        import os
        if os.environ.get("SL1") == "a":
            slices1 = [(0, 128, [0], 0)]
        if os.environ.get("SL1") == "b":
            slices1 = [(0, 512, [0, 1, 2, 3], 0)]
        if os.environ.get("SL1") == "c":
            slices1 = [(0, 256, [0, 1], 0)]
        import os
        for gi in range(5):
            if dbg < 5 and dbg < [4.7, 4.8, 4.9, 4.95, 4.97][gi]:
                break
            if gi + 1 < 5:
                gs = next_gs()
                if not os.environ.get("NOLOAD1"):
                    load_group(wio, L1_GROUPS[gi + 1], gs, PID["wio"])
                l1_gs.append(gs)
            if gi == 1:
                for kc in range(8):
                    stage_cast(woo[kc], wout[:, kc, :], [("wout", kc)], pid=PID["woo"] + kc)
            gs = l1_gs[gi]
            if gi in (0, 1):
                for ci in range(int(os.environ.get("ONECI", "4"))):
                    c = gi * 4 + ci
                    l1_rot_chunk(gs, ci, C_B + (0 if os.environ.get("BIAS0") else c),
                                 lambda t0, n, c=c: (big[:, c, t0:t0 + n], [("big", c, t0 // 512)]), slices1)
            elif gi == 2:
                l1_rot_chunk(gs, 0, C_B + 8,
                             lambda t0, n: (kT[:, 128 + t0:128 + t0 + n], [("kT", 1 + t0 // 128 + i) for i in range(n // 128)]),
                             slices1)
                for st in range(nst):
                    l1_v(gs, 1, st, 1 + st)
            elif not os.environ.get("SKIP_Z"):
                for ci in range(4):
                    l1_z_chunk(gs, ci, (gi - 3) * 4 + ci, slices1)
        if dbg < 6:
            continue
        units = [(st, kv) for st in range(nst) for kv in range(2)]
        prev = None
        finb = None
        fin = None
        for ui, (st, kv) in enumerate(units):
            fb = (T0 == HALO and st == 0)
            slots = attn_scores(st, kv, fb)
            if prev is not None:
                pst, pkv, pslots = prev
                if dbg >= 7:
                    attn_pv(pst, pkv, pslots, pst % 2)
                if finb is not None and dbg >= 8:
                    attn_finish_b(finb, T0 - HALO + finb * 128)
                    finb = None
                if fin is not None and dbg >= 8:
                    attn_finish(fin, fin % 2, T0 - HALO + fin * 128, True, part="a")
                    finb = fin
                    fin = None
                if pkv == 1:
                    fin = pst
            prev = (st, kv, slots)
        pst, pkv, pslots = prev
        if dbg >= 7:
            attn_pv(pst, pkv, pslots, pst % 2)
        if dbg >= 8:
            if finb is not None:
                attn_finish_b(finb, T0 - HALO + finb * 128)
            if fin is not None:
                attn_finish(fin, fin % 2, T0 - HALO + fin * 128, True)
            attn_finish(pst, pst % 2, T0 - HALO + pst * 128, True)
        cp("pool", kT[:, 0:128], kT[:, 1024:1152], [("kT", 8)], [("kT", 0)])
        cp("pool", vaug[:, 0, :, :], vaug[:, 8, :, :], [("vaug", 8)], [("vaug", 0)])

    P.emit()
    return nc, P


def _blk(w, cols):
    sub = w[:, cols]
    return np.ascontiguousarray(sub.reshape(8, 128, len(cols)).transpose(1, 0, 2).reshape(128, 8 * len(cols)))


def prepare(inputs):
    f = np.float32
    x = np.asarray(inputs["x"], f)
    pos = np.asarray(inputs["positions"]).astype(np.int32)
    norm_g = np.asarray(inputs["norm_g"], f)
    w_in_even = np.asarray(inputs["w_in_even"], f)[0]
    w_pool = np.asarray(inputs["w_pool"], f)[0]
    pool_scale = np.asarray(inputs["pool_scale"], f)[0]
    conv_w = np.asarray(inputs["conv_w"], f)[0]
    w_out_even = np.asarray(inputs["w_out_even"], f)[0]
    w_in_odd = np.asarray(inputs["w_in_odd"], f)[0]
    b_in_odd = np.asarray(inputs["b_in_odd"], f)[0]
    sinks = np.asarray(inputs["attn_sinks"], f)[0]
    w_out_odd = np.asarray(inputs["w_out_odd"], f)[0]
    b_out_odd = np.asarray(inputs["b_out_odd"], f)[0]
    fg = np.asarray(inputs["final_norm_g"], f)

    ar = np.arange(128)
    cols = []
    for g in range(4):
        cols += [(2 * g) * 128 + ar, (2 * g + 1) * 128 + ar, 4096 + (2 * g) * 128 + ar, 4096 + (2 * g + 1) * 128 + ar]
    for j in range(8):
        cols += [2048 + j * 128 + ar, 3072 + j * 128 + ar, 1024 + j * 128 + ar, 4096 + (8 + j) * 128 + ar]
    wie = np.stack([_blk(w_in_even, c) for c in cols])
    wpl = np.ascontiguousarray(w_pool.reshape(4, 2, 128, 256).transpose(2, 0, 1, 3).reshape(128, 2048))
    wpl = np.ascontiguousarray(wpl.reshape(128, 2, 1024).transpose(1, 0, 2))
    woe = np.ascontiguousarray(w_out_even.reshape(16, 128, 1024))
    perm = np.concatenate([np.concatenate([c * 64 + np.arange(64), (8 + c) * 64 + np.arange(64)]) for c in range(8)])
    rot_il = np.stack([np.arange(8), 8 + np.arange(8)], 1).reshape(-1)
    dimA = np.concatenate([16 + np.arange(48), rot_il])
    dimB = np.concatenate([rot_il, 16 + np.arange(48)])
    ocols = [np.concatenate([c * 64 + dimA, (8 + c) * 64 + dimB]) for c in range(8)]
    ocols += [1024 + np.concatenate([dimA, 64 + dimB])]
    ocols += [1280 + perm[c * 128:(c + 1) * 128] for c in range(8)]
    ocols += [1152 + ar]
    wio = np.stack([_blk(w_in_odd, c) for c in ocols])
    woo = np.ascontiguousarray(w_out_odd[perm].reshape(8, 128, 1024))
    cst = np.zeros((128, NCST), f)
    cst[:, C_G:C_G + 16] = norm_g.reshape(2, 8, 128).transpose(2, 0, 1).reshape(128, 16)
    cst[:, C_PS:C_PS + 8] = pool_scale.reshape(8, 128).T
    cst[:, C_CW:C_CW + 24] = conv_w.reshape(3, 8, 128).transpose(2, 0, 1).reshape(128, 24)
    for i in range(17):
        cst[:, C_B + i] = b_in_odd[ocols[i]]
    inv_freq = (np.float32(500000.0) ** (-np.arange(0, 16, 2, dtype=np.float32) / np.float32(16))).astype(f)
    rotp = (ar >= 48) & (ar < 80)
    jj = ((ar - 48) % 16) // 2
    cst[:, C_INVF] = np.where(rotp, inv_freq[np.clip(jj, 0, 7)], 0.0)
    cst[:, C_SGN] = np.where(rotp, np.where((ar - 48) % 2 == 0, -1.0, 1.0), 0.0)
    cst[:, C_MHALF] = -0.5
    rowc = np.zeros((128, NROW), f)
    rowc[:, R_BOUT:R_BOUT + 1024] = b_out_odd[None]
    rowc[:, R_FG:R_FG + 1024] = fg[None]
    rowc[:, R_BV:R_BV + 128] = b_in_odd[1152:1280][None]
    sl = np.zeros(16, f)
    for c in range(8):
        for k in range(2):
            sl[2 * c + k] = sinks[c + 8 * k]
    rowc[:, R_SINK:R_SINK + 16] = sl[None]
    pm = np.zeros((128, 1024), f)
    for m in range(128):
        dd = m % 64
        if dd < 8:
            pm[m + 8, m] = 1.0
        elif dd < 16:
            pm[m - 8, m] = 1.0
    s_ = np.arange(128)[:, None]
    q_ = np.arange(128)[None, :]
    m_prev = np.where(q_ < s_, 0.0, NEG).astype(f)
    m_diag = np.where(q_ >= s_, 0.0, NEG).astype(f)
    m_all = np.full((128, 128), NEG, f)
    in_maps = []
    for c in range(NCORES):
        b, half = c // 2, c % 2
        t0 = half * TOK
        xe = np.zeros((TL, DM), f)
        pe = np.zeros((TL,), np.int32)
        if half == 0:
            xe[HALO:] = x[b, 0:TOK]
            pe[HALO:] = pos[b, 0:TOK]
        else:
            xe[:] = x[b, t0 - HALO:t0 + TOK]
            pe[:] = pos[b, t0 - HALO:t0 + TOK]
        pc = np.ones((4, 16), f)
        if half == 0:
            for g in range(4):
                w = 2 << g
                pc[g] = w / np.minimum(np.arange(16) + 1, w)
        masks = np.zeros((2, 128, 1024), f)
        masks[0, :, 0:512] = np.tile(m_prev, (1, 4))
        masks[0, :, 512:1024] = np.tile(m_diag, (1, 4))
        masks[1, :, 0:512] = np.tile(m_all if half == 0 else m_prev, (1, 4))
        in_maps.append({
            "x_ext": xe, "pos_bc": np.ascontiguousarray(np.broadcast_to(pe[None], (128, TL))),
            "wie": wie, "wpl": wpl, "woe": woe, "wio": wio, "woo": woo, "cst": cst, "rowc": rowc,
            "pcorr": np.ascontiguousarray(np.broadcast_to(pc.reshape(1, 64), (128, 64))),
            "masks": masks, "pmat": pm,
        })
    return in_maps


_CACHE = {}


def kernel(**inputs):
    mode = "fused"
    if mode not in _CACHE:
        _CACHE[mode] = build(mode)[0]
    nc = _CACHE[mode]
    in_maps = prepare(inputs)
    res = run_bass_kernel_spmd(nc, in_maps, core_ids=list(range(NCORES)))
    out = np.empty((4, 8192, DM), np.float32)
    for c in range(NCORES):
        b, half = c // 2, c % 2
        out[b, half * TOK:(half + 1) * TOK] = res.results[c]["out"]
    return out
```

```python
from contextlib import ExitStack
import numpy as np
import concourse.bass as bass
import concourse.mybir as mybir
from concourse.bass_utils import run_bass_kernel_spmd

F32 = mybir.dt.float32
BF16 = mybir.dt.bfloat16
I32 = mybir.dt.int32
AF = mybir.ActivationFunctionType
ALU = mybir.AluOpType
AX = mybir.AxisListType


class _I:
    __slots__ = ("eng", "fn", "kind", "deps", "signal", "sig", "idx", "rawdeps", "line", "rw")


class Prog:
    NDMA = 30
    SAME_ENGINE_RAW = True

    def __init__(self, nc):
        self.nc = nc
        self.stack = ExitStack()
        self.instrs = []
        self.lw = {}
        self.rd = {}
        self.trace = None

    def sbuf(self, name, shape, dtype):
        return self.stack.enter_context(self.nc.sbuf_tensor(name, shape, dtype))

    def psum(self, name, shape, dtype):
        return self.stack.enter_context(self.nc.psum_tensor(name, shape, dtype))

    def _add(self, eng, fn, reads, writes, kind):
        ins = _I()
        ins.eng, ins.fn, ins.kind = eng, fn, kind
        ins.idx = len(self.instrs)
        import sys as _sys
        f = _sys._getframe(2)
        ln = []
        while f is not None and len(ln) < 4:
            ln.append(str(f.f_lineno))
            f = f.f_back
        ins.line = "<".join(ln)
        ins.rw = (reads, writes)
        ins.signal = False
        ins.sig = None
        deps = {}
        for k in reads:
            w = self.lw.get(k)
            if w is not None:
                deps[w.idx] = (w, True)
        for k in writes:
            w = self.lw.get(k)
            if w is not None and w.idx not in deps:
                deps[w.idx] = (w, False)
            for r in self.rd.get(k, ()):
                if r.idx not in deps:
                    deps[r.idx] = (r, False)
        out = []
        for d, raw in deps.values():
            if d.kind == "op" and d.eng == eng and kind == "op":
                if eng == "tensor" or not self.SAME_ENGINE_RAW:
                    continue
            out.append(d)
        ins.deps = out
        for k in reads:
            lst = self.rd.setdefault(k, [])
            if kind == "op":
                lst[:] = [r for r in lst if not (r.kind == "op" and r.eng == eng)]
            lst.append(ins)
        for k in writes:
            self.lw[k] = ins
            self.rd[k] = []
        self.instrs.append(ins)
        return ins

    def op(self, eng, fn, reads=(), writes=()):
        return self._add(eng, fn, tuple(reads), tuple(writes), "op")

    def dma(self, eng, out, in_, reads=(), writes=(), is_output=False):
        return self._add(eng, lambda e: e.dma_start(out=out, in_=in_), tuple(reads), tuple(writes), "dma")

    def emit(self):
        nc = self.nc
        st = self.stack
        engs = ["tensor", "vector", "scalar", "pool", "sp"]
        esem = {e: st.enter_context(nc.semaphore("sem_" + e)) for e in engs}
        dsem = [st.enter_context(nc.semaphore("dsem%d" % i)) for i in range(self.NDMA)]
        for ins in self.instrs:
            for d in ins.deps:
                d.signal = True
        cnt = {e: 0 for e in engs}
        dcnt = [0] * self.NDMA
        dlast = [None] * self.NDMA
        pools = {"sp": list(range(0, self.NDMA - 14)), "pool": list(range(self.NDMA - 14, self.NDMA - 8)),
                 "scalar": list(range(self.NDMA - 8, self.NDMA))}
        nd = {"sp": 0, "pool": 0, "scalar": 0}
        for ins in self.instrs:
            if ins.kind == "dma":
                pl = pools[ins.eng]
                s = pl[nd[ins.eng] % len(pl)]
                nd[ins.eng] += 1
                if dlast[s] is not None:
                    ins.deps.append(dlast[s])
                dcnt[s] += 16
                ins.sig = (dsem[s], dcnt[s])
                dlast[s] = ins
                ins.signal = True
            elif ins.signal:
                cnt[ins.eng] += 1
                ins.sig = (esem[ins.eng], cnt[ins.eng])
        per = {e: [i for i in self.instrs if i.eng == e] for e in engs}
        self.stats = {e: len(per[e]) for e in engs}
        self.stats["signals"] = dict(cnt)
        block = st.enter_context(nc.Block())

        def run(e, lst, final):
            waited = {}
            for ins in lst:
                if self.trace is not None:
                    self.trace.append((ins.eng, ins.idx, ins.kind, ins.line, [(d.eng, d.idx, d.sig[1]) for d in ins.deps], ins.sig[1] if ins.sig else None, ins.rw))
                need = {}
                for d in ins.deps:
                    sem, val = d.sig
                    key = id(sem)
                    if waited.get(key, 0) >= val:
                        continue
                    if key not in need or need[key][1] < val:
                        need[key] = (sem, val)
                for key, (sem, val) in need.items():
                    e.wait_ge(sem, val)
                    waited[key] = val
                r = ins.fn(e)
                if ins.signal:
                    sem, val = ins.sig
                    r.then_inc(sem, 16 if ins.kind == "dma" else 1)
            if final:
                for s in range(self.NDMA):
                    if dcnt[s] and waited.get(id(dsem[s]), 0) < dcnt[s]:
                        e.wait_ge(dsem[s], dcnt[s])

        @block.tensor
        def _(e):
            run(e, per["tensor"], False)

        @block.vector
        def _(e):
            run(e, per["vector"], False)

        @block.scalar
        def _(e):
            run(e, per["scalar"], True)

        @block.gpsimd
        def _(e):
            run(e, per["pool"], True)

        @block.sync
        def _(e):
            run(e, per["sp"], True)

        st.close()


NCORES = 8
DM = 1024
TOK = 4096
HALO = 256
TL = TOK + HALO
EPS = 1e-5
NEG = -30000.0
TWO_PI = 6.283185307179586
C1 = 6.28125
C2 = TWO_PI - C1
PI_SAFE = 3.1415925
TILES = [(0, 256, True)] + [(HALO + 1024 * i, 1024, False) for i in range(4)]

C_G = 0
C_PS = 16
C_CW = 24
C_B = 48
C_INVF = 65
C_SGN = 66
C_MHALF = 67
NCST = 68
R_BOUT = 0
R_FG = 1024
R_BV = 2048
R_SINK = 2176
NROW = 2192


def build(mode="fused", ntiles=5, dbg=99):
    nc = bass.Bass("TRN2", target_bir_lowering=False)
    P = Prog(nc)

    def din(name, shape, dt=F32):
        return nc.dram_tensor(name, shape, dt, kind="ExternalInput").ap()

    x_ext = din("x_ext", [TL, DM])
    pos_bc = din("pos_bc", [128, TL], I32)
    wie = din("wie", [48, 128, 1024])
    wpl = din("wpl", [2, 128, 1024])
    woe = din("woe", [16, 128, 1024])
    wio = din("wio", [18, 128, 1024])
    woo = din("woo", [8, 128, 1024])
    cst_d = din("cst", [128, NCST])
    rowc_d = din("rowc", [128, NROW])
    pcorr_d = din("pcorr", [128, 64])
    masks_d = din("masks", [2, 128, 1024])
    pmat_d = din("pmat", [128, 1024])
    out_d = nc.dram_tensor("out", [TOK, DM], F32, kind="ExternalOutput").ap()
    wscr = nc.dram_tensor("wscr", [90, 128, 1024], BF16, kind="Internal").ap()
    PID = {"wie": 0, "woe": 48, "wio": 64, "woo": 82}

    NSTG = 3
    h = P.sbuf("h", [128, 8, 1024], F32)
    ybuf = P.sbuf("ybuf", [128, 2, 1024], BF16)
    yT = P.sbuf("yT", [128, 8, 1024], BF16)
    big = P.sbuf("big", [128, 16, 1024], BF16)
    stg = P.sbuf("stg", [128, NSTG, 1024], F32)
    wg = P.sbuf("wg", [128, 2, 4, 1024], BF16)
    wout = P.sbuf("wout", [128, 16, 1024], BF16)
    wpool = P.sbuf("wpool", [128, 2, 1024], BF16)
    cst = P.sbuf("cstt", [128, NCST], F32)
    rowc = P.sbuf("rowct", [128, NROW], F32)
    pcorr = P.sbuf("pcorrt", [128, 64], F32)
    maskb = P.sbuf("maskb", [128, 3, 512], BF16)
    identf = P.sbuf("identf", [128, 128], F32)
    identb = P.sbuf("identb", [128, 128], BF16)
    esink = P.sbuf("esink", [128, 16], F32)
    ucar = P.sbuf("ucar", [128, 8, 16], F32)
    ccar = P.sbuf("ccar", [128, 8, 2], F32)
    NS = 8
    S = P.sbuf("S", [128, NS, 528], F32)
    pooled = P.sbuf("pooled", [128, 2, 512], BF16)
    xbq = P.sbuf("xbq", [128, 2, 512], BF16)
    kT = P.sbuf("kT", [128, 128 + 1024], BF16)
    vaug = P.sbuf("vaug", [128, 9, 2, 65], BF16)
    cosT = P.sbuf("cosT", [128, 1024], F32)
    sinT = P.sbuf("sinT", [128, 1024], F32)
    NPT = 8
    PT = P.sbuf("PT", [128, NPT, 512], BF16)
    attn = P.sbuf("attn", [128, 2, 1024], BF16)
    ss = P.sbuf("ss", [128, 32], F32)
    rstd = P.sbuf("rstd", [128, 32], F32)
    small = P.sbuf("small", [128, 16], F32)
    ps = [P.psum("ps%d" % i, [128, 512], F32) for i in range(8)]
    junk = xbq[:].rearrange("p a b -> p (a b)")
    JK = [("xbq", 0), ("xbq", 1)]

    state = {"bank": 0, "stg": 0, "S": 0, "pt": 0, "ssi": 0, "xs": 0, "ysl": 0, "pbi": 0}

    def nbank():
        b = state["bank"] % 8
        state["bank"] += 1
        return b

    def nS():
        s = state["S"] % NS
        state["S"] += 1
        return s

    def col(i):
        return cst[:, i:i + 1]

    def mm(out, lhsT, rhs, start, stop, reads, writes):
        P.op("tensor", lambda e: e.matmul(out=out, lhsT=lhsT, rhs=rhs, start=start, stop=stop),
             reads, writes)

    def tr(out, in_, reads, writes):
        P.op("tensor", lambda e: e.transpose(out=out, in_=in_, identity=identb[:]), list(reads) + ["ident"], writes)

    def act(out, in_, func, reads, writes, scale=None, bias=None, accum=None):
        kw = {}
        if scale is not None:
            kw["scale"] = scale
        if bias is not None:
            kw["bias"] = bias
        if accum is not None:
            kw["accum_out"] = accum
        P.op("scalar", lambda e: e.activation(out=out, in_=in_, func=func, **kw), reads, writes)

    def tt(eng, out, in0, in1, op, reads, writes):
        P.op(eng, lambda e: e.tensor_tensor(out=out, in0=in0, in1=in1, op=op), reads, writes)

    def stt(out, in0, scalar, in1, op0, op1, reads, writes):
        P.op("vector", lambda e: e.scalar_tensor_tensor(out=out, in0=in0, scalar=scalar, in1=in1, op0=op0, op1=op1),
             reads, writes)

    def ts(eng, out, in0, s1, op0, reads, writes, s2=None, op1=None):
        if op1 is None:
            P.op(eng, lambda e: e.tensor_scalar(out=out, in0=in0, scalar1=s1, scalar2=None, op0=op0), reads, writes)
        else:
            P.op(eng, lambda e: e.tensor_scalar(out=out, in0=in0, scalar1=s1, scalar2=s2, op0=op0, op1=op1), reads, writes)

    def cp(eng, out, in_, reads, writes):
        if eng == "scalar":
            P.op(eng, lambda e: e.copy(out=out, in_=in_), reads, writes)
        else:
            P.op(eng, lambda e: e.tensor_copy(out=out, in_=in_), reads, writes)

    cast_rr = {"i": 0}
    CAST_ENGS = ["scalar", "scalar", "pool"]

    cached = set()

    def stage_cast(dram_ap, dst_ap, dst_keys, ncols=1024, eng=None, pid=None):
        if pid is not None and pid in cached:
            P.dma("sp", dst_ap, wscr[pid], reads=[("scr", pid)], writes=dst_keys)
            return
        stage_cast_(dram_ap, dst_ap, dst_keys, ncols, eng)
        if pid is not None:
            P.dma("pool", wscr[pid], dst_ap, reads=dst_keys, writes=[("scr", pid)])
            cached.add(pid)

    def stage_cast_(dram_ap, dst_ap, dst_keys, ncols=1024, eng=None):
        s = state["stg"] % NSTG
        state["stg"] += 1
        P.dma("sp", stg[:, s, 0:ncols], dram_ap, reads=[], writes=[("stg", s)])
        if eng is None:
            eng = CAST_ENGS[cast_rr["i"] % len(CAST_ENGS)]
            cast_rr["i"] += 1
        cp(eng, dst_ap, stg[:, s, 0:ncols], [("stg", s)], dst_keys)

    P.dma("sp", cst[:], cst_d, writes=["cst"])
    P.dma("sp", rowc[:], rowc_d, writes=["rowc"])
    P.dma("sp", pcorr[:], pcorr_d, writes=["pcorr"])
    P.op("pool", lambda e: e.memset(identf[:], 0.0), writes=["identf"])
    P.op("pool", lambda e: e.affine_select(out=identf[:], in_=identf[:], pattern=[[-1, 128]],
                                           compare_op=ALU.not_equal, fill=1.0, base=0, channel_multiplier=1),
         reads=["identf"], writes=["identf"])
    cp("vector", identb[:], identf[:], ["identf"], ["ident"])
    P.op("pool", lambda e: e.memset(ucar[:], 0.0), writes=["ucar%d" % i for i in range(8)])
    P.op("pool", lambda e: e.memset(ccar[:], 0.0), writes=["ccar%d" % i for i in range(8)])
    P.op("vector", lambda e: e.memset(vaug[:, :, :, 64:65], 1.0), writes=[("vaug", i) for i in range(9)])
    for i in range(2):
        stage_cast(wpl[i], wpool[:, i, :], [("wpool", i)], eng="vector")
    stage_cast(masks_d[0], maskb[:, 0:2, :].rearrange("p a b -> p (a b)"), ["mask"], eng="vector")
    stage_cast(masks_d[1][:, 0:512], maskb[:, 2, :], ["mask2"], ncols=512, eng="vector")
    P.op("vector", lambda e: e.memset(S[:], 0.0), writes=[("S", i) for i in range(NS)])
    act(esink[:], rowc[:, R_SINK:R_SINK + 16], AF.Exp, ["rowc"], ["esink"])

    def load_x(t0, nst):
        for st in range(nst):
            P.dma("sp", h[:, st, :], x_ext[t0 + st * 128: t0 + (st + 1) * 128, :], writes=[("h", st)])

    def norm_T(layer, sts):
        base = state["ssi"]
        state["ssi"] = (state["ssi"] + 8) % 24
        for st in sts:
            act(ybuf[:, st % 2, :], h[:, st, :], AF.Square, [("h", st)], [("ybuf", st % 2), ("ss", base + st)],
                accum=ss[:, base + st: base + st + 1])
        lo, hi = base + sts[0], base + sts[-1] + 1
        keys_ss = [("ss", base + st) for st in sts]
        keys_r = [("rstd", base + st) for st in sts]
        ts("pool", rstd[:, lo:hi], ss[:, lo:hi], 1.0 / DM, ALU.mult, keys_ss, keys_r, s2=EPS, op1=ALU.add)
        tt("pool", rstd[:, lo:hi], rstd[:, lo:hi], cst[:, C_MHALF:C_MHALF + 1].to_broadcast([128, hi - lo]),
           ALU.pow, keys_r + ["cst"], keys_r)
        for st in sts:
            sl = st % 2
            act(ybuf[:, sl, :], h[:, st, :], AF.Copy, [("h", st), ("rstd", base + st)], [("ybuf", sl)],
                scale=rstd[:, base + st: base + st + 1])
            b = nbank()
            pb = ps[b][:].bitcast(BF16)
            for kc in range(8):
                tr(pb[:, kc * 128:(kc + 1) * 128], ybuf[:, sl, kc * 128:(kc + 1) * 128], [("ybuf", sl)], [("ps", b)])
            tt("vector", yT[:, :, st * 128:(st + 1) * 128], pb.rearrange("p (k t) -> p k t", k=8),
               cst[:, C_G + layer * 8: C_G + layer * 8 + 8].unsqueeze(2).to_broadcast([128, 8, 128]), ALU.mult,
               [("ps", b), "cst"], [("yT", st)])

    def norm_st(layer, st):
        i = state["ssi"]
        state["ssi"] = (state["ssi"] + 1) % 24
        sl = state["ysl"] % 2
        state["ysl"] += 1
        act(ybuf[:, sl, :], h[:, st, :], AF.Square, [("h", st)], [("ybuf", sl), ("ss", i)], accum=ss[:, i:i + 1])
        ts("pool", rstd[:, i:i + 1], ss[:, i:i + 1], 1.0 / DM, ALU.mult, [("ss", i)], [("rstd", i)], s2=EPS, op1=ALU.add)
        tt("pool", rstd[:, i:i + 1], rstd[:, i:i + 1], cst[:, C_MHALF:C_MHALF + 1], ALU.pow, [("rstd", i), "cst"], [("rstd", i)])
        act(ybuf[:, sl, :], h[:, st, :], AF.Copy, [("h", st), ("rstd", i)], [("ybuf", sl)], scale=rstd[:, i:i + 1])
        b = nbank()
        pb = ps[b][:].bitcast(BF16)
        for kc in range(8):
            tr(pb[:, kc * 128:(kc + 1) * 128], ybuf[:, sl, kc * 128:(kc + 1) * 128], [("ybuf", sl)], [("ps", b)])
        tt("vector", yT[:, :, st * 128:(st + 1) * 128], pb.rearrange("p (k t) -> p k t", k=8),
           cst[:, C_G + layer * 8: C_G + layer * 8 + 8].unsqueeze(2).to_broadcast([128, 8, 128]), ALU.mult,
           [("ps", b), "cst"], [("yT", st)])

    def load_group(src, chunk_ids, gs, base):
        for ci, ch in enumerate(chunk_ids):
            stage_cast(src[ch], wg[:, gs, ci, :], [("wg", gs, ci)], pid=base + ch)

    def proj(gs, ci, t0, n, sts):
        b = nbank()
        w = wg[:, gs, ci, :].rearrange("p (k n) -> p k n", k=8)
        for kc in range(8):
            mm(ps[b][:, 0:n], w[:, kc, :], yT[:, kc, t0:t0 + n], kc == 0, kc == 7,
               [("wg", gs, ci)] + [("yT", st) for st in sts], [("ps", b)])
        return b

    def l0_group_a(g, gs, slices, first_real):
        w = 2 << g
        pend = None
        pbufs = [(pooled, "pooled"), (xbq, "xbq")]

        def make_tail(t0, n, pb, pkey):
            banks = []

            def tail_mm():
                for oc in range(2):
                    b = nbank()
                    banks.append(b)
                    for kc in range(2):
                        o = (g % 2) * 512 + kc * 256 + oc * 128
                        mm(ps[b][:, 0:n], wpool[:, g // 2, o:o + 128], pb[:, kc, 0:n], kc == 0, kc == 1,
                           [("wpool", g // 2), (pkey, 0), (pkey, 1)], [("ps", b)])

            def tail_stt():
                for oc in range(2):
                    c = 2 * g + oc
                    b = banks[oc]
                    stt(big[:, c, t0:t0 + n], ps[b][:, 0:n], col(C_PS + c), big[:, c, t0:t0 + n], ALU.mult, ALU.mult,
                        [("ps", b), "cst", ("big", c, t0 // 512)], [("big", c, t0 // 512)])
            return tail_mm, tail_stt

        for sl, (t0, n, sts) in enumerate(slices):
            pb, pkey = pbufs[state["pbi"] % 2]
            state["pbi"] += 1
            bu = [proj(gs, 0, t0, n, sts), proj(gs, 1, t0, n, sts)]
            bz = [proj(gs, 2, t0, n, sts), proj(gs, 3, t0, n, sts)]
            if pend is not None:
                pend[0]()
            for i in range(2):
                c = 2 * g + i
                ub, ta, tb = nS(), nS(), nS()
                kc_ = "ucar%d" % c
                cp("pool", S[:, ub, 0:16], ucar[:, c, :], [kc_], [("S", ub)])
                cp("scalar", S[:, ub, 16:16 + n], ps[bu[i]][:, 0:n], [("ps", bu[i])], [("S", ub)])
                cp("pool", ucar[:, c, :], S[:, ub, n:n + 16], [("S", ub)], [kc_])
                src = ub
                lvl = 1
                dst = ta
                while (1 << lvl) <= w:
                    sh = 1 << (lvl - 1)
                    lo = (1 << lvl)
                    tt("vector", S[:, dst, lo:16 + n], S[:, src, lo:16 + n], S[:, src, lo - sh:16 + n - sh], ALU.add,
                       [("S", src)], [("S", dst)])
                    src = dst
                    dst = tb if dst == ta else ta
                    lvl += 1
                if first_real and sl == 0:
                    tt("vector", S[:, src, 16:32], S[:, src, 16:32], pcorr[:, g * 16:(g + 1) * 16], ALU.mult,
                       [("S", src), "pcorr"], [("S", src)])
                stt(pb[:, i, 0:n], S[:, src, 16:16 + n], 1.0 / w, S[:, ub, 16:16 + n], ALU.mult, ALU.subtract,
                    [("S", src), ("S", ub)], [(pkey, i)])
                act(big[:, c, t0:t0 + n], ps[bz[i]][:, 0:n], AF.Silu, [("ps", bz[i])], [("big", c, t0 // 512)])
            if pend is not None:
                pend[1]()
            pend = make_tail(t0, n, pb, pkey)
        pend[0]()
        pend[1]()

    def l0_group_b(j, gs, slices):
        for sl, (t0, n, sts) in enumerate(slices):
            bgc = proj(gs, 0, t0, n, sts)
            bhc = proj(gs, 1, t0, n, sts)
            bgb = proj(gs, 2, t0, n, sts)
            bzb = proj(gs, 3, t0, n, sts)
            hc, cu, v0, v1, zz, gz = nS(), nS(), nS(), nS(), nS(), nS()
            kc_ = "ccar%d" % j
            cp("scalar", S[:, hc, 0:n], ps[bhc][:, 0:n], [("ps", bhc)], [("S", hc)])
            cp("pool", S[:, cu, 0:2], ccar[:, j, :], [kc_], [("S", cu)])
            tt("vector", S[:, cu, 2:2 + n], ps[bgc][:, 0:n], S[:, hc, 0:n], ALU.mult, [("ps", bgc), ("S", hc)], [("S", cu)])
            cp("pool", ccar[:, j, :], S[:, cu, n:n + 2], [("S", cu)], [kc_])
            act(S[:, v0, 0:n], S[:, cu, 2:2 + n], AF.Copy, [("S", cu), "cst"], [("S", v0)], scale=col(C_CW + 16 + j))
            stt(S[:, v1, 0:n], S[:, cu, 1:1 + n], col(C_CW + 8 + j), S[:, v0, 0:n], ALU.mult, ALU.add,
                [("S", cu), ("S", v0), "cst"], [("S", v1)])
            stt(S[:, v0, 0:n], S[:, cu, 0:n], col(C_CW + j), S[:, v1, 0:n], ALU.mult, ALU.add,
                [("S", cu), ("S", v1), "cst"], [("S", v0)])
            act(S[:, zz, 0:n], ps[bzb][:, 0:n], AF.Silu, [("ps", bzb)], [("S", zz)])
            tt("vector", S[:, gz, 0:n], ps[bgb][:, 0:n], S[:, zz, 0:n], ALU.mult, [("ps", bgb), ("S", zz)], [("S", gz)])
            tt("pool", big[:, 8 + j, t0:t0 + n], S[:, gz, 0:n], S[:, v0, 0:n], ALU.mult, [("S", gz), ("S", v0)],
               [("big", 8 + j, t0 // 512)])

    def out_proj(st, nkc, coff):
        for nh in range(2):
            b = nbank()
            for kc in range(nkc):
                mm(ps[b][:, 0:512], big[:, coff + kc, st * 128:(st + 1) * 128], wout[:, kc, nh * 512:(nh + 1) * 512],
                   kc == 0, kc == nkc - 1, [("big", coff + kc, st // 4), ("wout", kc)], [("ps", b)])
            tt("vector", h[:, st, nh * 512:(nh + 1) * 512], ps[b][:, 0:512], h[:, st, nh * 512:(nh + 1) * 512], ALU.add,
               [("ps", b), ("h", st)], [("h", st)])

    def rope_tables(t0, n):
        a, k_, r, m = nS(), nS(), nS(), nS()
        posi = S[:, a, :].bitcast(I32)
        for hh in range(0, n, 512):
            nn = min(512, n - hh)
            ki = S[:, k_, 0:nn].bitcast(I32)
            P.dma("sp", posi[:, 0:nn], pos_bc[:, t0 + hh:t0 + hh + nn], writes=[("S", a)])
            ts("vector", S[:, r, 0:nn], posi[:, 0:nn], col(C_INVF), ALU.mult, [("S", a), "cst"], [("S", r)])
            ts("vector", ki, S[:, r, 0:nn], 1.0 / TWO_PI, ALU.mult, [("S", r)], [("S", k_)])
            stt(S[:, m, 0:nn], ki, -C1, S[:, r, 0:nn], ALU.mult, ALU.add, [("S", k_), ("S", r)], [("S", m)])
            stt(S[:, r, 0:nn], ki, -C2, S[:, m, 0:nn], ALU.mult, ALU.add, [("S", k_), ("S", m)], [("S", r)])

            def wrap(buf, tmp):
                ts("vector", S[:, tmp, 0:nn], S[:, buf, 0:nn], PI_SAFE, ALU.is_gt, [("S", buf)], [("S", tmp)], s2=-TWO_PI, op1=ALU.mult)
                tt("vector", S[:, buf, 0:nn], S[:, buf, 0:nn], S[:, tmp, 0:nn], ALU.add, [("S", buf), ("S", tmp)], [("S", buf)])
                ts("vector", S[:, tmp, 0:nn], S[:, buf, 0:nn], -PI_SAFE, ALU.is_lt, [("S", buf)], [("S", tmp)], s2=TWO_PI, op1=ALU.mult)
                tt("vector", S[:, buf, 0:nn], S[:, buf, 0:nn], S[:, tmp, 0:nn], ALU.add, [("S", buf), ("S", tmp)], [("S", buf)])
                ts("vector", S[:, buf, 0:nn], S[:, buf, 0:nn], PI_SAFE, ALU.min, [("S", buf)], [("S", buf)], s2=-PI_SAFE, op1=ALU.max)

            wrap(r, m)
            act(sinT[:, hh:hh + nn], S[:, r, 0:nn], AF.Sin, [("S", r), "cst"], [("sinT", hh // 512)], scale=col(C_SGN))
            ts("vector", S[:, r, 0:nn], S[:, r, 0:nn], 1.5707963267948966, ALU.add, [("S", r)], [("S", r)])
            wrap(r, m)
            act(cosT[:, hh:hh + nn], S[:, r, 0:nn], AF.Sin, [("S", r)], [("cosT", hh // 512)])

    def l1_rot_chunk(gs, ci, bias_col, dst_fn, slices):
        for sl, (t0, n, sts, tb) in enumerate(slices):
            b = proj(gs, ci, t0, n, sts)
            r, xp = nS(), nS()
            t1 = r
            act(S[:, r, 0:n], ps[b][:, 0:n], AF.Identity, [("ps", b), "cst"], [("S", r)], bias=col(bias_col))
            def rows2(buf, p0):
                a_ = S[p0:p0 + 1, buf, 0:n]
                pst = a_.ap[0][0]
                return type(a_)(tensor=a_.tensor, offset=a_.offset, ap=[[2 * pst, 16], [1, n]])
            P.dma("scalar", rows2(xp, 48), rows2(r, 49), reads=[("S", r)], writes=[("S", xp)])
            P.dma("sp", rows2(xp, 49), rows2(r, 48), reads=[("S", r)], writes=[("S", xp)])
            stt(S[:, t1, 0:n], ps[b][:, 0:n], col(bias_col), cosT[:, tb:tb + n], ALU.add, ALU.mult,
                [("ps", b), "cst", ("cosT", tb // 512)], [("S", t1)])
            tt("pool", S[:, xp, 0:n], S[:, xp, 0:n], sinT[:, tb:tb + n], ALU.mult, [("S", xp), ("sinT", tb // 512)], [("S", xp)])
            dst, dkeys = dst_fn(t0, n)
            tt("vector", dst, S[:, t1, 0:n], S[:, xp, 0:n], ALU.add, [("S", t1), ("S", xp)], dkeys)

    def l1_z_chunk(gs, ci, c, slices):
        for sl, (t0, n, sts, tb) in enumerate(slices):
            b = proj(gs, ci, t0, n, sts)
            act(big[:, 8 + c, t0:t0 + n], ps[b][:, 0:n], AF.Silu, [("ps", b), "cst"], [("big", 8 + c, t0 // 512)],
                bias=col(C_B + 9 + c))

    def l1_v(gs, ci, st, blk):
        b = nbank()
        w = wg[:, gs, ci, :].rearrange("p (k n) -> p k n", k=8)
        for kc in range(8):
            mm(ps[b][:, 0:128], yT[:, kc, st * 128:(st + 1) * 128], w[:, kc, :], kc == 0, kc == 7,
               [("wg", gs, ci), ("yT", st)], [("ps", b)])
        tt("vector", vaug[:, blk, :, 0:64], ps[b][:, 0:128].rearrange("p (a d) -> p a d", a=2),
           rowc[:, R_BV:R_BV + 128].rearrange("p (a d) -> p a d", a=2), ALU.add, [("ps", b), "rowc"], [("vaug", blk)])

    def attn_scores(st, kv, first_block):
        slots = {}
        for kb in range(2):
            kcol = (st + kb) * 128
            for hg in range(2):
                b = nbank()
                mm(ps[b][:, 0:512], kT[kv * 64:(kv + 1) * 64, kcol:kcol + 128],
                   big[kv * 64:(kv + 1) * 64, hg * 4:(hg + 1) * 4, st * 128:(st + 1) * 128], True, False,
                   [("kT", st + kb)] + [("big", c, st // 4) for c in range(hg * 4, hg * 4 + 4)], [("ps", b)])
                mi = 1 if kb == 1 else (2 if first_block else 0)
                mm(ps[b][:, 0:512], identb[:], maskb[:, mi, :], False, True, ["ident", "mask", "mask2"], [("ps", b)])
                s = state["pt"] % NPT
                state["pt"] += 1
                act(PT[:, s, :], ps[b][:, 0:512], AF.Exp, [("ps", b)], [("PT", s)], scale=0.125)
                slots[(kb, hg)] = s
        return slots

    def attn_pv(st, kv, slots, asl):
        for hg in range(2):
            b = nbank()
            for hl in range(4):
                for kb in range(2):
                    s = slots[(kb, hg)]
                    mm(ps[b][:, hl * 128: hl * 128 + 65], PT[:, s, hl * 128:(hl + 1) * 128], vaug[:, st + kb, kv, :],
                       kb == 0, kb == 1, [("PT", s), ("vaug", st + kb)], [("ps", b)])
            pv = ps[b][:].rearrange("p (a d) -> p a d", a=4)
            es = esink[:].rearrange("p (c k) -> p c k", k=2)[:, hg * 4:(hg + 1) * 4, kv:kv + 1]
            sm = small[:, (kv * 2 + hg) * 4:(kv * 2 + hg) * 4 + 4]
            kk = ("small", kv * 2 + hg)
            tt("vector", sm.unsqueeze(2), pv[:, :, 64:65], es, ALU.add, [("ps", b), "esink"], [kk])
            P.op("vector", lambda e, sm=sm: e.reciprocal(out=sm, in_=sm), [kk], [kk])
            dst = attn[:, asl, :].rearrange("p (c k d) -> p c k d", c=8, k=2)[:, hg * 4:(hg + 1) * 4, kv, :]
            tt("vector", dst, pv[:, :, 0:64], sm.unsqueeze(2).to_broadcast([128, 4, 64]), ALU.mult,
               [("ps", b), kk], [("attn", asl, kv, hg)])

    def attn_finish(st, asl, out_row0, final, part="ab"):
        b = nbank()
        pb = ps[b][:].bitcast(BF16)
        akeys = [("attn", asl, kv, hg) for kv in range(2) for hg in range(2)]
        for c in range(8):
            tr(pb[:, c * 128:(c + 1) * 128], attn[:, asl, c * 128:(c + 1) * 128], akeys, [("ps", b)])
        gk = [("big", 8 + c, st // 4) for c in range(8)]
        tt("vector", big[:, 8:16, st * 128:(st + 1) * 128], pb.rearrange("p (k t) -> p k t", k=8),
           big[:, 8:16, st * 128:(st + 1) * 128], ALU.mult, [("ps", b)] + gk, gk)
        if part == "a":
            return
        attn_finish_b(st, out_row0)

    def attn_finish_b(st, out_row0):
        out_proj(st, 8, 8)
        i = state["ssi"]
        state["ssi"] = (state["ssi"] + 1) % 24
        act(junk, h[:, st, :], AF.Square, [("h", st)], JK + [("ss", i)], accum=ss[:, i:i + 1])
        ts("pool", rstd[:, i:i + 1], ss[:, i:i + 1], 1.0 / DM, ALU.mult, [("ss", i)], [("rstd", i)], s2=EPS, op1=ALU.add)
        tt("pool", rstd[:, i:i + 1], rstd[:, i:i + 1], cst[:, C_MHALF:C_MHALF + 1], ALU.pow, [("rstd", i), "cst"], [("rstd", i)])
        stt(h[:, st, :], h[:, st, :], rstd[:, i:i + 1], rowc[:, R_FG:R_FG + 1024], ALU.mult, ALU.mult,
            [("h", st), ("rstd", i), "rowc"], [("h", st)])
        P.dma("pool", out_d[out_row0: out_row0 + 128, :], h[:, st, :], reads=[("h", st)], writes=[("out", out_row0)])

    A_GROUPS = [[4 * g + i for i in range(4)] for g in range(4)]
    B_GROUPS = [[16 + 4 * j + i for i in range(4)] for j in range(8)]
    L0_GROUPS = A_GROUPS + B_GROUPS
    L0_ORDER = [3, 4, 2, 5, 1, 6, 0, 7, 8, 9, 10, 11]
    L1_GROUPS = [[0, 1, 2, 3], [4, 5, 6, 7], [8, 17], [9, 10, 11, 12], [13, 14, 15, 16]]
    gslot = {"i": 0}

    def next_gs():
        g = gslot["i"] % 2
        gslot["i"] += 1
        return g

    first_real_done = False
    for (T0, NT, is_halo) in TILES[:ntiles]:
        nst = NT // 128
        load_x(T0, nst)
        if is_halo:
            slices0 = [(0, 256, [0, 1])]
        else:
            slices0 = [(0, 512, [0, 1, 2, 3]), (512, 512, [4, 5, 6, 7])]
        norm_T(0, list(range(nst)))
        first_real = (not is_halo) and (not first_real_done)
        pending = None
        gs_list = []
        for gi, gid in enumerate(L0_ORDER):
            chunks = L0_GROUPS[gid]
            gs = next_gs()
            load_group(wie, chunks, gs, PID["wie"])
            if gi == 1:
                for kc in range(16):
                    stage_cast(woe[kc], wout[:, kc, :], [("wout", kc)], pid=PID["woe"] + kc)
            if pending is not None:
                pg, pgs = pending
                if pg < 4:
                    l0_group_a(pg, pgs, slices0, first_real)
                else:
                    l0_group_b(pg - 4, pgs, slices0)
            pending = (gid, gs)
        l1_gs = []
        gs = next_gs()
        load_group(wio, L1_GROUPS[2] if is_halo else L1_GROUPS[0], gs, PID["wio"])
        l1_gs.append(gs)
        pg, pgs = pending
        l0_group_b(pg - 4, pgs, slices0)
        if not is_halo:
            first_real_done = True
        l0_sts = [1] if is_halo else list(range(nst))
        pipe_norm = (mode != "L0") and dbg >= 1
        def norm_bias(st_):
            norm_st(1, st_)
            if not is_halo:
                tt("pool", h[:, st_, :], h[:, st_, :], rowc[:, R_BOUT:R_BOUT + 1024], ALU.add, [("h", st_), "rowc"], [("h", st_)])

        for i, st in enumerate(l0_sts):
            out_proj(st, 16, 0)
            if pipe_norm and i >= 1:
                norm_bias(l0_sts[i - 1])
        if pipe_norm:
            norm_bias(l0_sts[-1])
        if mode == "L0":
            if not is_halo:
                for st in range(nst):
                    r0 = T0 - HALO + st * 128
                    P.dma("pool", out_d[r0:r0 + 128, :], h[:, st, :], reads=[("h", st)], writes=[("out", r0)])
            continue
        if dbg < 1:
            continue
        if is_halo:
            if dbg < 2:
                continue
            rope_tables(T0 + 128, 128)
            gs = l1_gs[0]
            if dbg < 3:
                continue
            l1_rot_chunk(gs, 0, C_B + 8, lambda t0, n: (kT[:, 0:128], [("kT", 0)]), [(128, 128, [1], 0)])
            if dbg < 4:
                continue
            l1_v(gs, 1, 1, 0)
            continue
        if dbg < 4.5:
            continue
        rope_tables(T0, NT)
        if dbg < 4.6:
            continue
        slices1 = [(0, 512, [0, 1, 2, 3], 0), (512, 512, [4, 5, 6, 7], 512)]
        import os
        if os.environ.get("SL1") == "a":
            slices1 = [(0, 128, [0], 0)]
        if os.environ.get("SL1") == "b":
            slices1 = [(0, 512, [0, 1, 2, 3], 0)]
        if os.environ.get("SL1") == "c":
            slices1 = [(0, 256, [0, 1], 0)]
        import os
        for gi in range(5):
            if dbg < 5 and dbg < [4.7, 4.8, 4.9, 4.95, 4.97][gi]:
                break
            if gi + 1 < 5:
                gs = next_gs()
                if not os.environ.get("NOLOAD1"):
                    load_group(wio, L1_GROUPS[gi + 1], gs, PID["wio"])
                l1_gs.append(gs)
            if gi == 1:
                for kc in range(8):
                    stage_cast(woo[kc], wout[:, kc, :], [("wout", kc)], pid=PID["woo"] + kc)
            gs = l1_gs[gi]
            if gi in (0, 1):
                for ci in range(int(os.environ.get("ONECI", "4"))):
                    c = gi * 4 + ci
                    l1_rot_chunk(gs, ci, C_B + (0 if os.environ.get("BIAS0") else c),
                                 lambda t0, n, c=c: (big[:, c, t0:t0 + n], [("big", c, t0 // 512)]), slices1)
            elif gi == 2:
                l1_rot_chunk(gs, 0, C_B + 8,
                             lambda t0, n: (kT[:, 128 + t0:128 + t0 + n], [("kT", 1 + t0 // 128 + i) for i in range(n // 128)]),
                             slices1)
                for st in range(nst):
                    l1_v(gs, 1, st, 1 + st)
            elif not os.environ.get("SKIP_Z"):
                for ci in range(4):
                    l1_z_chunk(gs, ci, (gi - 3) * 4 + ci, slices1)
        if dbg < 6:
            continue
        units = [(st, kv) for st in range(nst) for kv in range(2)]
        prev = None
        finb = None
        fin = None
        for ui, (st, kv) in enumerate(units):
            fb = (T0 == HALO and st == 0)
            slots = attn_scores(st, kv, fb)
            if prev is not None:
                pst, pkv, pslots = prev
                if dbg >= 7:
                    attn_pv(pst, pkv, pslots, pst % 2)
                if finb is not None and dbg >= 8:
                    attn_finish_b(finb, T0 - HALO + finb * 128)
                    finb = None
                if fin is not None and dbg >= 8:
                    attn_finish(fin, fin % 2, T0 - HALO + fin * 128, True, part="a")
                    finb = fin
                    fin = None
                if pkv == 1:
                    fin = pst
            prev = (st, kv, slots)
        pst, pkv, pslots = prev
        if dbg >= 7:
            attn_pv(pst, pkv, pslots, pst % 2)
        if dbg >= 8:
            if finb is not None:
                attn_finish_b(finb, T0 - HALO + finb * 128)
            if fin is not None:
                attn_finish(fin, fin % 2, T0 - HALO + fin * 128, True)
            attn_finish(pst, pst % 2, T0 - HALO + pst * 128, True)
        cp("pool", kT[:, 0:128], kT[:, 1024:1152], [("kT", 8)], [("kT", 0)])
        cp("pool", vaug[:, 0, :, :], vaug[:, 8, :, :], [("vaug", 8)], [("vaug", 0)])

    P.emit()
    return nc, P


def _blk(w, cols):
    sub = w[:, cols]
    return np.ascontiguousarray(sub.reshape(8, 128, len(cols)).transpose(1, 0, 2).reshape(128, 8 * len(cols)))


def prepare(inputs):
    f = np.float32
    x = np.asarray(inputs["x"], f)
    pos = np.asarray(inputs["positions"]).astype(np.int32)
    norm_g = np.asarray(inputs["norm_g"], f)
    w_in_even = np.asarray(inputs["w_in_even"], f)[0]
    w_pool = np.asarray(inputs["w_pool"], f)[0]
    pool_scale = np.asarray(inputs["pool_scale"], f)[0]
    conv_w = np.asarray(inputs["conv_w"], f)[0]
    w_out_even = np.asarray(inputs["w_out_even"], f)[0]
    w_in_odd = np.asarray(inputs["w_in_odd"], f)[0]
    b_in_odd = np.asarray(inputs["b_in_odd"], f)[0]
    sinks = np.asarray(inputs["attn_sinks"], f)[0]
    w_out_odd = np.asarray(inputs["w_out_odd"], f)[0]
    b_out_odd = np.asarray(inputs["b_out_odd"], f)[0]
    fg = np.asarray(inputs["final_norm_g"], f)

    ar = np.arange(128)
    cols = []
    for g in range(4):
        cols += [(2 * g) * 128 + ar, (2 * g + 1) * 128 + ar, 4096 + (2 * g) * 128 + ar, 4096 + (2 * g + 1) * 128 + ar]
    for j in range(8):
        cols += [2048 + j * 128 + ar, 3072 + j * 128 + ar, 1024 + j * 128 + ar, 4096 + (8 + j) * 128 + ar]
    wie = np.stack([_blk(w_in_even, c) for c in cols])
    wpl = np.ascontiguousarray(w_pool.reshape(4, 2, 128, 256).transpose(2, 0, 1, 3).reshape(128, 2048))
    wpl = np.ascontiguousarray(wpl.reshape(128, 2, 1024).transpose(1, 0, 2))
    woe = np.ascontiguousarray(w_out_even.reshape(16, 128, 1024))
    perm = np.concatenate([np.concatenate([c * 64 + np.arange(64), (8 + c) * 64 + np.arange(64)]) for c in range(8)])
    rot_il = np.stack([np.arange(8), 8 + np.arange(8)], 1).reshape(-1)
    dimA = np.concatenate([16 + np.arange(48), rot_il])
    dimB = np.concatenate([rot_il, 16 + np.arange(48)])
    ocols = [np.concatenate([c * 64 + dimA, (8 + c) * 64 + dimB]) for c in range(8)]
    ocols += [1024 + np.concatenate([dimA, 64 + dimB])]
    ocols += [1280 + perm[c * 128:(c + 1) * 128] for c in range(8)]
    ocols += [1152 + ar]
    wio = np.stack([_blk(w_in_odd, c) for c in ocols])
    woo = np.ascontiguousarray(w_out_odd[perm].reshape(8, 128, 1024))
    cst = np.zeros((128, NCST), f)
    cst[:, C_G:C_G + 16] = norm_g.reshape(2, 8, 128).transpose(2, 0, 1).reshape(128, 16)
    cst[:, C_PS:C_PS + 8] = pool_scale.reshape(8, 128).T
    cst[:, C_CW:C_CW + 24] = conv_w.reshape(3, 8, 128).transpose(2, 0, 1).reshape(128, 24)
    for i in range(17):
        cst[:, C_B + i] = b_in_odd[ocols[i]]
    inv_freq = (np.float32(500000.0) ** (-np.arange(0, 16, 2, dtype=np.float32) / np.float32(16))).astype(f)
    rotp = (ar >= 48) & (ar < 80)
    jj = ((ar - 48) % 16) // 2
    cst[:, C_INVF] = np.where(rotp, inv_freq[np.clip(jj, 0, 7)], 0.0)
    cst[:, C_SGN] = np.where(rotp, np.where((ar - 48) % 2 == 0, -1.0, 1.0), 0.0)
    cst[:, C_MHALF] = -0.5
    rowc = np.zeros((128, NROW), f)
    rowc[:, R_BOUT:R_BOUT + 1024] = b_out_odd[None]
    rowc[:, R_FG:R_FG + 1024] = fg[None]
    rowc[:, R_BV:R_BV + 128] = b_in_odd[1152:1280][None]
    sl = np.zeros(16, f)
    for c in range(8):
        for k in range(2):
            sl[2 * c + k] = sinks[c + 8 * k]
    rowc[:, R_SINK:R_SINK + 16] = sl[None]
    pm = np.zeros((128, 1024), f)
    for m in range(128):
        dd = m % 64
        if dd < 8:
            pm[m + 8, m] = 1.0
        elif dd < 16:
            pm[m - 8, m] = 1.0
    s_ = np.arange(128)[:, None]
    q_ = np.arange(128)[None, :]
    m_prev = np.where(q_ < s_, 0.0, NEG).astype(f)
    m_diag = np.where(q_ >= s_, 0.0, NEG).astype(f)
    m_all = np.full((128, 128), NEG, f)
    in_maps = []
    for c in range(NCORES):
        b, half = c // 2, c % 2
        t0 = half * TOK
        xe = np.zeros((TL, DM), f)
        pe = np.zeros((TL,), np.int32)
        if half == 0:
            xe[HALO:] = x[b, 0:TOK]
            pe[HALO:] = pos[b, 0:TOK]
        else:
            xe[:] = x[b, t0 - HALO:t0 + TOK]
            pe[:] = pos[b, t0 - HALO:t0 + TOK]
        pc = np.ones((4, 16), f)
        if half == 0:
            for g in range(4):
                w = 2 << g
                pc[g] = w / np.minimum(np.arange(16) + 1, w)
        masks = np.zeros((2, 128, 1024), f)
        masks[0, :, 0:512] = np.tile(m_prev, (1, 4))
        masks[0, :, 512:1024] = np.tile(m_diag, (1, 4))
        masks[1, :, 0:512] = np.tile(m_all if half == 0 else m_prev, (1, 4))
        in_maps.append({
            "x_ext": xe, "pos_bc": np.ascontiguousarray(np.broadcast_to(pe[None], (128, TL))),
            "wie": wie, "wpl": wpl, "woe": woe, "wio": wio, "woo": woo, "cst": cst, "rowc": rowc,
            "pcorr": np.ascontiguousarray(np.broadcast_to(pc.reshape(1, 64), (128, 64))),
            "masks": masks, "pmat": pm,
        })
    return in_maps


_CACHE = {}


def kernel(**inputs):
    mode = "fused"
    if mode not in _CACHE:
        _CACHE[mode] = build(mode)[0]
    nc = _CACHE[mode]
    in_maps = prepare(inputs)
    res = run_bass_kernel_spmd(nc, in_maps, core_ids=list(range(NCORES)))
    out = np.empty((4, 8192, DM), np.float32)
    for c in range(NCORES):
        b, half = c // 2, c % 2
        out[b, half * TOK:(half + 1) * TOK] = res.results[c]["out"]
    return out
```
